# Optimizing a Trainium2 kernel written in Bass

```python
import jax, jax.numpy as jnp
from jax import lax
import numpy as np

D_MODEL = 1024
BATCH = 4
SEQ = 8192
DEPTH = 2
DEC_BATCH = 8
DEC_SEQ = 16
PAST_LEN = 2048

CHUNK = 64
HEAD_DIM = 64
N_Q_HEADS = 16
N_KV_HEADS = 4
GROUP = N_Q_HEADS // N_KV_HEADS
ATTN_WIDTH = N_Q_HEADS * HEAD_DIM
KV_WIDTH = N_KV_HEADS * HEAD_DIM
WINDOW = 128
W_CHUNKS = WINDOW // CHUNK
BAND = (W_CHUNKS + 1) * CHUNK
CONV_CH = D_MODEL
CONV_W = 31
D_FF = 4 * D_MODEL
N_BRANCH = 2
N_IN = ATTN_WIDTH + 2 * KV_WIDTH + 2 * CONV_CH + N_BRANCH * D_MODEL
SPLITS = (ATTN_WIDTH, ATTN_WIDTH + KV_WIDTH, ATTN_WIDTH + 2 * KV_WIDTH,
          ATTN_WIDTH + 2 * KV_WIDTH + 2 * CONV_CH)
ROPE_THETA = 10000.0
EPS = 1e-6
NEG_INF = -1e30

kernel_name = "hybrid_swa_sink_conformer_conv_streaming_step"


def rms_norm(x, g):
    xf = x.astype(jnp.float32)
    y = xf * lax.rsqrt(jnp.mean(xf * xf, axis=-1, keepdims=True) + EPS)
    return (y * g.astype(jnp.float32)).astype(x.dtype)


def layer_norm(x, g, b):
    xf = x.astype(jnp.float32)
    mu = jnp.mean(xf, axis=-1, keepdims=True)
    var = jnp.mean(jnp.square(xf - mu), axis=-1, keepdims=True)
    y = (xf - mu) * lax.rsqrt(var + EPS)
    return (y * g.astype(jnp.float32) + b.astype(jnp.float32)).astype(x.dtype)


def rotary(x, pos):
    half = HEAD_DIM // 2
    inv_freq = ROPE_THETA ** (-jnp.arange(half, dtype=jnp.float32) / half)
    ang = pos.astype(jnp.float32)[:, None] * inv_freq[None, :]
    cos = jnp.cos(ang)[:, None, :]
    sin = jnp.sin(ang)[:, None, :]
    xf = x.astype(jnp.float32)
    x1, x2 = xf[..., :half], xf[..., half:]
    out = jnp.concatenate([x1 * cos - x2 * sin, x2 * cos + x1 * sin], axis=-1)
    return out.astype(x.dtype)


def softmax_with_sink(s, sink_b):
    m = jnp.maximum(jnp.max(s, axis=-1, keepdims=True), sink_b)
    p = jnp.exp(s - m)
    return p / (jnp.sum(p, axis=-1, keepdims=True) + jnp.exp(sink_b - m))


def attn_prompt(q, k, v, sink):
    B, T = q.shape[0], q.shape[1]
    nc = T // CHUNK
    qb = q.reshape(B, nc, CHUNK, N_KV_HEADS, GROUP, HEAD_DIM)
    pad = ((0, 0), (WINDOW, 0), (0, 0), (0, 0))
    kp = jnp.pad(k, pad).reshape(B, nc + W_CHUNKS, CHUNK, N_KV_HEADS, HEAD_DIM)
    vp = jnp.pad(v, pad).reshape(B, nc + W_CHUNKS, CHUNK, N_KV_HEADS, HEAD_DIM)
    kb = jnp.concatenate([kp[:, i:i + nc] for i in range(W_CHUNKS + 1)], axis=2)
    vb = jnp.concatenate([vp[:, i:i + nc] for i in range(W_CHUNKS + 1)], axis=2)
    s = jnp.einsum('bnqkgd,bnskd->bnkgqs', qb, kb).astype(jnp.float32) * (HEAD_DIM ** -0.5)
    key_chunk = jnp.arange(nc)[:, None] - W_CHUNKS + (jnp.arange(BAND) // CHUNK)[None, :]
    valid = (key_chunk >= 0)[None, :, None, None, None, :]
    s = jnp.where(valid, s, NEG_INF)
    sink_b = sink.astype(jnp.float32).reshape(1, 1, N_KV_HEADS, GROUP, 1, 1)
    p = softmax_with_sink(s, sink_b).astype(v.dtype)
    o = jnp.einsum('bnkgqs,bnskd->bnqkgd', p, vb)
    return o.reshape(B, T, ATTN_WIDTH)


def attn_sample(q, k, v, k_past, v_past, sink):
    B, T = q.shape[0], q.shape[1]
    qs = q.reshape(B, T, N_KV_HEADS, GROUP, HEAD_DIM)
    kk = jnp.concatenate([k_past.astype(k.dtype), k], axis=1)
    vv = jnp.concatenate([v_past.astype(v.dtype), v], axis=1)
    s = jnp.einsum('bqkgd,bskd->bkgqs', qs, kk).astype(jnp.float32) * (HEAD_DIM ** -0.5)
    sink_b = sink.astype(jnp.float32).reshape(1, N_KV_HEADS, GROUP, 1, 1)
    p = softmax_with_sink(s, sink_b).astype(v.dtype)
    o = jnp.einsum('bkgqs,bskd->bqkgd', p, vv)
    return o.reshape(B, T, ATTN_WIDTH)


def conv_module(u, conv_past, w_dw, b_dw, ln_g, ln_b, w_pw2):
    a = u[..., :CONV_CH] * jax.nn.sigmoid(u[..., CONV_CH:])
    full = jnp.concatenate([conv_past.astype(a.dtype), a], axis=1)
    y = lax.conv_general_dilated(full, w_dw[:, None, :].astype(a.dtype), window_strides=(1,),
                                 padding='VALID', dimension_numbers=('NWC', 'WIO', 'NWC'),
                                 feature_group_count=CONV_CH) + b_dw
    y = jax.nn.silu(layer_norm(y, ln_g, ln_b))
    return y @ w_pw2, full[:, -(CONV_W - 1):]


def trunk_layer(x, pos, k_past, v_past, conv_past, norm_mix, w_in, sink, w_o_attn, w_dw, b_dw,
                ln_g, ln_b, w_pw2, w_out, norm_mlp, w_up, w_down):
    B, T = x.shape[0], x.shape[1]
    h = rms_norm(x, norm_mix)
    q, k, v, u, gl = jnp.split(h @ w_in, SPLITS, axis=-1)
    q = rotary(q.reshape(B, T, N_Q_HEADS, HEAD_DIM), pos)
    k = rotary(k.reshape(B, T, N_KV_HEADS, HEAD_DIM), pos)
    v = v.reshape(B, T, N_KV_HEADS, HEAD_DIM)
    if k_past is None:
        attn = attn_prompt(q, k, v, sink)
        k_keep, v_keep = k[:, -WINDOW:], v[:, -WINDOW:]
        conv_past = jnp.zeros((B, CONV_W - 1, CONV_CH), x.dtype)
    else:
        attn = attn_sample(q, k, v, k_past, v_past, sink)
        k_keep, v_keep = k, v
    conv_out, conv_keep = conv_module(u, conv_past, w_dw, b_dw, ln_g, ln_b, w_pw2)
    g = jax.nn.sigmoid(gl)
    merged = g[..., :D_MODEL] * (attn @ w_o_attn) + g[..., D_MODEL:] * conv_out
    x = x + merged @ w_out
    hm = rms_norm(x, norm_mlp)
    x = x + jnp.square(jax.nn.relu(hm @ w_up)) @ w_down
    return x, k_keep, v_keep, conv_keep


def setup_inputs(seed: int = 0) -> dict:
    key = jax.random.key(seed)
    ks = jax.random.split(key, 20)
    f32 = jnp.float32
    nrm = lambda k, shape, scale: jax.random.normal(k, shape, f32) * scale
    return {
        "x_prompt": nrm(ks[0], (BATCH, SEQ, D_MODEL), 1.0),
        "x_sample": nrm(ks[1], (DEC_BATCH, DEC_SEQ, D_MODEL), 1.0),
        "cache_k": nrm(ks[2], (DEPTH, DEC_BATCH, WINDOW, N_KV_HEADS, HEAD_DIM), 1.0),
        "cache_v": nrm(ks[3], (DEPTH, DEC_BATCH, WINDOW, N_KV_HEADS, HEAD_DIM), 1.0),
        "state_conv": nrm(ks[4], (DEPTH, DEC_BATCH, CONV_W - 1, CONV_CH), 0.5),
        "norm_mix": 1.0 + nrm(ks[5], (DEPTH, D_MODEL), 0.02),
        "w_in": nrm(ks[6], (DEPTH, D_MODEL, N_IN), D_MODEL ** -0.5),
        "sinks": nrm(ks[7], (DEPTH, N_Q_HEADS), 0.5),
        "w_o_attn": nrm(ks[8], (DEPTH, ATTN_WIDTH, D_MODEL), ATTN_WIDTH ** -0.5),
        "w_dw": nrm(ks[9], (DEPTH, CONV_W, CONV_CH), CONV_W ** -0.5),
        "b_dw": nrm(ks[10], (DEPTH, CONV_CH), 0.02),
        "ln_conv_g": 1.0 + nrm(ks[11], (DEPTH, CONV_CH), 0.02),
        "ln_conv_b": nrm(ks[12], (DEPTH, CONV_CH), 0.02),
        "w_pw2": nrm(ks[13], (DEPTH, CONV_CH, D_MODEL), CONV_CH ** -0.5),
        "w_out": nrm(ks[14], (DEPTH, D_MODEL, D_MODEL), D_MODEL ** -0.5),
        "norm_mlp": 1.0 + nrm(ks[15], (DEPTH, D_MODEL), 0.02),
        "w_up": nrm(ks[16], (DEPTH, D_MODEL, D_FF), D_MODEL ** -0.5),
        "w_down": nrm(ks[17], (DEPTH, D_FF, D_MODEL), D_FF ** -0.5),
        "norm_final": 1.0 + nrm(ks[18], (D_MODEL,), 0.02),
    }


def reference(x_prompt, x_sample, cache_k, cache_v, state_conv, norm_mix, w_in, sinks, w_o_attn,
              w_dw, b_dw, ln_conv_g, ln_conv_b, w_pw2, w_out, norm_mlp, w_up, w_down, norm_final):
    pos_p = jnp.arange(x_prompt.shape[1], dtype=jnp.int32)
    pos_s = PAST_LEN + jnp.arange(x_sample.shape[1], dtype=jnp.int32)
    xp, xs = x_prompt, x_sample
    kp_l, vp_l, cp_l, ks_l, vs_l, cs_l = [], [], [], [], [], []
    for l in range(DEPTH):
        w = (norm_mix[l], w_in[l], sinks[l], w_o_attn[l], w_dw[l], b_dw[l], ln_conv_g[l],
             ln_conv_b[l], w_pw2[l], w_out[l], norm_mlp[l], w_up[l], w_down[l])
        xp, kp, vp, cp = trunk_layer(xp, pos_p, None, None, None, *w)
        xs, kn, vn, cn = trunk_layer(xs, pos_s, cache_k[l], cache_v[l], state_conv[l], *w)
        kp_l.append(kp); vp_l.append(vp); cp_l.append(cp)
        ks_l.append(kn); vs_l.append(vn); cs_l.append(cn)
    y_prompt = rms_norm(xp, norm_final)
    y_sample = rms_norm(xs, norm_final)
    k_prompt_new = jnp.stack(kp_l)
    v_prompt_new = jnp.stack(vp_l)
    conv_prompt_new = jnp.stack(cp_l)
    k_sample_new = jnp.stack(ks_l)
    v_sample_new = jnp.stack(vs_l)
    conv_sample_new = jnp.stack(cs_l)
    return (y_prompt, y_sample, k_prompt_new, v_prompt_new, conv_prompt_new, k_sample_new, v_sample_new, conv_sample_new)
```

```python
import contextlib
import numpy as np
import concourse.bass as bass
import concourse.mybir as mybir
from concourse.bass_utils import run_bass_kernel_spmd

F32 = mybir.dt.float32
F32R = mybir.dt.float32r
AF = mybir.ActivationFunctionType
ALU = mybir.AluOpType

D = 1024
NCH = 8
HD = 64
NQH = 16
NKV = 4
CONVW = 31
DFF = 4096
NIN = 5632
EPS = 1e-6
HALO = 256
TILE = 512
NSMP = 16
PAST = 2048
NU = 174
TD = 7
NCG = 3
NUS = 156
U_Q, U_K, U_V, U_U, U_CONV, U_MERGE, U_OUT, U_UP, U_DOWN = 0, 16, 20, 22, 38, 70, 102, 110, 142
NS = 8
ACH = 544
NVEC = 88
N_OWN_TILES = 8


def q_head(cc, half):
    return (cc + 4 * half) if cc < 4 else (8 + (cc - 4) + 4 * half)


class _Rec:
    def __init__(self):
        self.call = None

    def __getattr__(self, name):
        def f(*a, **k):
            self.call = (name, a, k)
            return None
        return f


def _record(fn):
    r = _Rec()
    fn(r)
    assert r.call is not None
    return r.call


class Prog:
    ENGS = ("pe", "act", "dve", "pool", "sp")

    def __init__(self, nc, es):
        self.nc = nc
        self.q = {e: [] for e in self.ENGS}
        self.sem = {e: es.enter_context(nc.semaphore("prog_" + e)) for e in self.ENGS}
        self.cnt = {e: 0 for e in self.ENGS}

    def op(self, eng, fn, waits=(), signal=False):
        tok = None
        if signal:
            self.cnt[eng] += 1
            tok = (self.sem[eng], self.cnt[eng])
        ws = []
        for w in waits:
            if w is None:
                continue
            if isinstance(w, list):
                ws.extend([x for x in w if x is not None])
            else:
                ws.append(w)
        self.q[eng].append((_record(fn), tuple(ws), tok, 1))
        return tok

    def dma(self, eng, fn, dsem, waits=()):
        dsem[1] += 16
        tok = (dsem[0], dsem[1])
        ws = []
        for w in waits:
            if w is None:
                continue
            if isinstance(w, list):
                ws.extend([x for x in w if x is not None])
            else:
                ws.append(w)
        self.q[eng].append((_record(fn), tuple(ws), tok, 16))
        return tok

    def run(self, eng, e, final_waits=()):
        seen = {}
        for (mname, margs, mkw), waits, tok, inc in self.q[eng]:
            for (sem, val) in waits:
                k = sem.num
                if seen.get(k, 0) >= val:
                    continue
                e.wait_ge(sem, val)
                seen[k] = val
            ins = getattr(e, mname)(*margs, **mkw)
            if tok is not None:
                ins.then_inc(tok[0], inc)
        for (sem, val) in final_waits:
            e.wait_ge(sem, val)


class Banks:
    def __init__(self, tensors):
        self.t = tensors
        self.free = [[] for _ in tensors]
        self.held = [False] * len(tensors)
        self.i = 0

    def get(self):
        n = len(self.t)
        for _ in range(n):
            i = self.i
            self.i = (i + 1) % n
            if not self.held[i]:
                self.held[i] = True
                fr = self.free[i]
                self.free[i] = []
                return i, self.t[i], fr
        raise RuntimeError("no free psum bank")

    def release(self, i, toks):
        self.held[i] = False
        self.free[i] = [t for t in toks if t is not None]


def build_program(n_own):
    npass = 1 + n_own
    ntok = HALO + TILE * n_own
    nc = bass.Bass("TRN2", target_bir_lowering=False)

    def din(name, shape):
        return nc.dram_tensor(name, list(shape), F32, kind="ExternalInput").ap()

    def dout(name, shape):
        return nc.dram_tensor(name, list(shape), F32, kind="ExternalOutput").ap()

    xT = din("xT", [D, ntok])
    xsT = din("xsT", [D, NSMP])
    wst = din("wst", [2, NUS, 128, 1024])
    perm_d = din("perm", [128, 128])
    vecs_d = din("vecs", [128, NVEC])
    wdw_d = din("wdw", [128, 2, NCH, TD])
    ropec_d = din("ropec", [128, ntok + NSMP])
    ropes_d = din("ropes", [128, ntok + NSMP])
    kcT_d = din("kcT", [2, 128, 2, 128])
    vc_d = din("vc", [2, 128, 256])
    scT_d = din("scT", [2, 128, 8, 30])
    sinks_d = din("sinks", [128, 32])
    flags_d = din("flags", [128, 4])

    ypT = dout("ypT", [D, TILE * n_own])
    ysT = dout("ysT", [D, NSMP])
    kpT = dout("kpT", [2, 128, 2, 128])
    vp = dout("vp", [2, 128, 256])
    cpT = dout("cpT", [2, 128, 8, 30])
    ksT = dout("ksT", [2, 128, 2, NSMP])
    vs = dout("vs", [2, NSMP, 256])
    csT = dout("csT", [2, 128, 8, 30])

    es = contextlib.ExitStack()
    with es:
        def sb(name, shape, dt=F32):
            return es.enter_context(nc.sbuf_tensor(name, list(shape), dt))

        X = sb("X", [128, NCH, TILE])
        H = sb("H", [128, NCH, TILE], F32R)
        ARENA = sb("ARENA", [128, 12288 + NCH * ACH], F32R)
        KB = sb("KB", [128, 2, 2, 128 + TILE], F32R)
        VB = sb("VB", [128, 2, 5, 256], F32R)
        ACAR = sb("ACAR", [128, 2, NCH, 30], F32R)
        AS = sb("AS", [128, 2, NCH, 30 + NSMP], F32R)
        KC = sb("KC", [128, 2, 2, 128], F32R)
        VC = sb("VC", [128, 2, 256], F32R)
        KS = sb("KS", [128, 2, 2, NSMP], F32R)
        VS = sb("VS", [128, 2, 256], F32R)
        PT = sb("PT", [128, 4, 512], F32R)
        T1 = sb("T1", [128, 2, TILE])
        T2 = sb("T2", [128, 2, TILE])
        SG = sb("SG", [128, 2, TILE])
        SQ = sb("SQ", [128, 2, TILE], F32R)
        RS = sb("RS", [128, TILE])
        MU = sb("MU", [128, TILE])
        M2 = sb("M2", [128, TILE])
        RD = sb("RD", [128, 2, 256])
        RC = sb("RC", [128, TILE])
        RSN = sb("RSN", [128, TILE])
        ESK = sb("ESK", [128, 32])
        ESR = sb("ESR", [128, 2, 4, 256], F32R)
        ONES = sb("ONES", [128, 128], F32R)
        ONEHOT0 = sb("ONEHOT0", [128, 128], F32R)
        PERM = sb("PERM", [128, 128], F32R)
        ZER = sb("ZER", [128, 64])
        VECS = sb("VECS", [128, NVEC])
        WDW = sb("WDW", [128, 2, NCH, TD])
        FLAGS = sb("FLAGS", [128, 4])
        EPSB = sb("EPSB", [128, 1])
        WR = sb("WR", [128, NS, 1024], F32R)

        banks_t = [es.enter_context(nc.psum_tensor("bank%d" % i, [128, 512], F32)) for i in range(8)]
        BK = Banks(banks_t)

        P = Prog(nc, es)

        def newsem(name):
            return [es.enter_context(nc.semaphore(name)), 0]

        wsem = [newsem("wsem%d" % i) for i in range(NS // 2)]
        ld_sem = newsem("ld")
        ldp_sem = newsem("ldp")
        x_sem = newsem("xs")
        r_sem = newsem("rs")
        o_sem = newsem("os")
        fo_sem = newsem("fo")

        def Qv(c, a, b):
            return ARENA[:, c * TILE + a: c * TILE + b]

        def AOv(c, a, b):
            return ARENA[:, 4096 + c * TILE + a: 4096 + c * TILE + b]

        def ZCv(c, a, b):
            return ARENA[:, 8192 + c * TILE + a: 8192 + c * TILE + b]

        def Av(c, a, b):
            return ARENA[:, 12288 + c * ACH + a: 12288 + c * ACH + b]

        def HIDv(i, a, b):
            return ARENA[:, i * TILE + a: i * TILE + b]

        def f32(ap):
            return ap.bitcast(F32)

        total_units = npass * 2 * NUS

        def issue_wdma(s0, waits):
            rem = s0 % (2 * NUS)
            l, u = rem // NUS, rem % NUS
            slot = s0 % NS
            pair = slot // 2
            src = wst[l, u:u + 2].rearrange("u p f -> p u f")
            dst = WR[:, slot:slot + 2, :]
            P.dma("pool", lambda e, dst=dst, src=src: e.dma_start(out=dst, in_=src), wsem[pair], waits)

        def wunit(s):
            slot = s % NS
            pair = slot // 2
            fill = s // NS + 1
            return slot, (wsem[pair][0], 16 * fill)

        def wdone(s, tok):
            if s % 2 == 1:
                nxt = s - 1 + NS
                if nxt < total_units:
                    issue_wdma(nxt, [tok])

        order = []
        wcount = [0]

        def wnext(logical):
            s_ = wcount[0]
            wcount[0] += 1
            if s_ < NUS:
                order.append(logical)
            else:
                assert order[s_ % NUS] == logical, (s_, logical)
            return s_

        P.dma("sp", lambda e: e.dma_start(out=VECS[:, :], in_=vecs_d[:, :]), ld_sem)
        P.dma("sp", lambda e: e.dma_start(out=FLAGS[:, :], in_=flags_d[:, :]), ld_sem)
        P.dma("sp", lambda e: e.dma_start(out=ESK[:, :], in_=sinks_d[:, :]), ld_sem)
        P.dma("sp", lambda e: e.dma_start(out=WDW[:, :, :, :], in_=wdw_d[:, :, :, :]), ld_sem)
        LD = (ld_sem[0], ld_sem[1])
        P.dma("pool", lambda e: e.dma_start(out=KC[:, :, :, :], in_=kcT_d.rearrange("l p c t -> p l c t")), ldp_sem)
        P.dma("pool", lambda e: e.dma_start(out=VC[:, :, :], in_=vc_d.rearrange("l p f -> p l f")), ldp_sem)
        for l in range(2):
            P.dma("pool", lambda e, l=l: e.dma_start(out=AS[:, l, :, 0:30], in_=scT_d[l]), ldp_sem)
        P.dma("pool", lambda e: e.dma_start(out=PERM[:, :], in_=perm_d[:, :]), ldp_sem)
        LDP = (ldp_sem[0], ldp_sem[1])
        for s0 in range(0, NS, 2):
            issue_wdma(s0, [])

        P.op("dve", lambda e: e.memset(T1[:, :, :], 0.0))
        P.op("dve", lambda e: e.memset(ZER[:, :], 0.0))
        P.op("dve", lambda e: e.memset(EPSB[:, :], EPS))
        z_src = T1[:, :, :].rearrange("p a b -> p (a b)")
        c_ones = P.op("dve", lambda e: e.tensor_scalar(out=ONES[:, :], in0=z_src[:, 0:128], scalar1=1.0,
                                                        scalar2=None, op0=ALU.add), signal=True)

        P.op("dve", lambda e: e.tensor_scalar(out=ONEHOT0[:, :], in0=z_src[:, 0:128], scalar1=FLAGS[:, 2:3],
                                              scalar2=None, op0=ALU.add), [LD])

        def zero_fill(flat, ncols):
            tok = None
            for c0 in range(0, ncols, 1024):
                c1 = min(ncols, c0 + 1024)
                tok = P.op("dve", lambda e, c0=c0, c1=c1: e.tensor_copy(out=flat[:, c0:c1], in_=z_src[:, 0:c1 - c0]),
                           signal=True)
            return tok

        zero_fill(KB[:, :, :, :].rearrange("p a b c -> p (a b c)"), 2 * 2 * (128 + TILE))
        zero_fill(VB[:, :, :, :].rearrange("p a b c -> p (a b c)"), 2 * 5 * 256)
        zero_fill(ACAR[:, :, :, :].rearrange("p a b c -> p (a b c)"), 2 * NCH * 30)
        zero_fill(PT[:, :, :].rearrange("p a b -> p (a b)"), 2048)
        c_init = zero_fill(VS[:, :, :].rearrange("p a b -> p (a b)"), 512)
        c_esk = P.op("act", lambda e: e.activation(out=ESK[:, :], in_=ESK[:, :], func=AF.Exp), [LD], signal=True)
        c_esr = None
        for l in range(2):
            for k in range(NKV):
                for g in range(4):
                    i = l * 16 + 4 * k + g
                    c_esr = P.op("dve", lambda e, l=l, k=k, g=g, i=i: e.tensor_scalar(
                        out=ESR[:, l, k, g * 64:(g + 1) * 64], in0=ZER[:, :], scalar1=ESK[:, i:i + 1], scalar2=None,
                        op0=ALU.add), [c_esk], signal=True)

        state = {
            "x_free": [],
            "rope_free": [],
            "y_store": None,
            "kb_ready": [None, None],
            "out_tokens": [],
        }
        sq_free = [[], []]
        t_free = {"T1": [[], []], "T2": [[], []], "SG": [[], []], "PT": [[], [], [], []], "DT": [[], []]}
        rot = {"sq": 0, "t": 0, "sg": 0, "pt": 0, "dt": 0}

        def vec(base, c):
            return VECS[:, base + c: base + c + 1]

        class Stats:
            def __init__(self, ntot):
                self.ntot = ntot
                self.bi, self.bank, self.bfree = BK.get()
                self.n = 0
                self.last = None

            def add(self, src_ap, waits):
                ntot, bank, c = self.ntot, self.bank, self.n
                r = rot["sq"]
                rot["sq"] ^= 1
                a_t = P.op("act", lambda e: e.activation(out=SQ[:, r, 0:ntot], in_=src_ap, func=AF.Square),
                           list(waits) + sq_free[r], signal=True)
                self.last = P.op("pe", lambda e: e.matmul(
                    bank[:, 0:ntot], ONES[:, :], SQ[:, r, 0:ntot], start=(c == 0), stop=(c == NCH - 1)),
                    [a_t, c_ones] + (self.bfree if c == 0 else []), signal=True)
                sq_free[r] = [self.last]
                self.n += 1

            def finish(self):
                ntot, bank = self.ntot, self.bank
                assert self.n == NCH
                a2 = P.op("act", lambda e: e.activation(out=MU[:, 0:ntot], in_=bank[:, 0:ntot], func=AF.Ln,
                                                         bias=EPSB[:, 0:1], scale=1.0 / D), [self.last], signal=True)
                BK.release(self.bi, [a2])
                return P.op("act", lambda e: e.activation(out=RS[:, 0:ntot], in_=MU[:, 0:ntot], func=AF.Exp,
                                                          scale=-0.5), [a2], signal=True)

        def rms_stats(src_chunk, ntot, src_waits):
            st = Stats(ntot)
            for c in range(NCH):
                st.add(src_chunk(c), src_waits)
            return st.finish()

        def proj_group(s, ntot, rhs_chunk, nk, rhs_waits, bank, bfree, start=True, stop=True, col0=0):
            slot, wtok = wunit(s)
            last = None
            for kc in range(nk):
                first = (kc == 0)
                last = P.op("pe", lambda e, kc=kc, slot=slot, first=first: e.matmul(
                    bank[:, col0:col0 + ntot], WR[:, slot, kc * 128:(kc + 1) * 128], rhs_chunk(kc),
                    start=(start and first), stop=(stop and kc == nk - 1)),
                    (rhs_waits(kc) + ([wtok] + bfree if first else [])) if callable(rhs_waits)
                    else (([wtok] + rhs_waits + bfree) if first else []), signal=(kc == nk - 1))
            wdone(s, last)
            return last

        def layer(p, l, n, ns, x_ready, rope_ready, d_rs_in):
            ntot = n + ns
            vb = 40 * l
            V_NM, V_BDW, V_LNG, V_LNB, V_NMLP, V_NF = vb, vb + 8, vb + 16, vb + 24, vb + 32, 80
            nchunks = n // 64
            first_own = (p == 1)

            d_rs = d_rs_in if d_rs_in is not None else rms_stats(lambda c: X[:, c, 0:ntot], ntot, x_ready)
            h_tok = None
            h_toks = []
            for c in range(NCH):
                h_tok = P.op("dve", lambda e, c=c: e.scalar_tensor_tensor(
                    out=H[:, c, 0:ntot], in0=X[:, c, 0:ntot], scalar=vec(V_NM, c), in1=RS[:, 0:ntot],
                    op0=ALU.mult, op1=ALU.mult), [d_rs, LD] + x_ready, signal=True)
                h_toks.append(h_tok)
            h_ready = [h_tok]
            h_first = [True]

            def h_waits(kc):
                return [h_toks[kc]]

            def hch(kc):
                return H[:, kc, 0:ntot]

            car_tok = P.op("act", lambda e: e.activation(
                out=ARENA[:, 12288:12288 + NCH * ACH].rearrange("p (c t) -> p c t", t=ACH)[:, :, 0:30],
                in_=f32(ACAR[:, l, :, :]), func=AF.Copy), [c_init] + x_ready, signal=True)
            a_last = None
            for c in range(NCH):
                iA, bA, fA = BK.get()
                iB, bB, fB = BK.get()
                pa = proj_group(wnext(U_U + 2 * c), ntot, hch, 8, h_waits if c == 0 else h_ready, bA, fA)
                pb = proj_group(wnext(U_U + 2 * c + 1), ntot, hch, 8, h_ready, bB, fB)
                r = rot["sg"]
                rot["sg"] ^= 1
                a1 = P.op("act", lambda e, bB=bB, r=r: e.activation(
                    out=SG[:, r, 0:ntot], in_=bB[:, 0:ntot], func=AF.Sigmoid), [pb] + t_free["SG"][r], signal=True)
                BK.release(iB, [a1])
                d1 = P.op("dve", lambda e, bA=bA, r=r, c=c: e.tensor_tensor(
                    out=Av(c, 30, 30 + n), in0=bA[:, 0:n], in1=SG[:, r, 0:n], op=ALU.mult), [pa, a1], signal=True)
                if ns:
                    d1 = P.op("dve", lambda e, bA=bA, r=r, c=c: e.tensor_tensor(
                        out=AS[:, l, c, 30:30 + ns], in0=bA[:, n:ntot], in1=SG[:, r, n:ntot], op=ALU.mult),
                        [pa, a1, LDP], signal=True)
                BK.release(iA, [d1])
                t_free["SG"][r] = [d1]
                a_last = d1
            a_ready = [a_last, car_tok, LDP]

            def qk_gen(js):
                def finish(cx):
                    j, iA, bA, pa, rq, ac = cx
                    isq = j < 8
                    iB, bB, fB = BK.get()
                    pb = P.op("pe", lambda e: e.matmul(
                        bB[:, 0:ntot], PERM[:, :], SQ[:, rq, 0:ntot], start=True, stop=True),
                        [ac, LDP] + fB, signal=True)
                    sq_free[rq] = [pb]
                    r = rot["t"]
                    rot["t"] ^= 1
                    d1 = P.op("dve", lambda e: e.tensor_tensor(
                        out=T1[:, r, 0:ntot], in0=bA[:, 0:ntot], in1=RC[:, 0:ntot], op=ALU.mult),
                        [pa, ac] + rope_ready + t_free["T1"][r], signal=True)
                    d2 = P.op("dve", lambda e: e.tensor_tensor(
                        out=T2[:, r, 0:ntot], in0=bB[:, 0:ntot], in1=RSN[:, 0:ntot], op=ALU.mult),
                        [pb] + rope_ready + t_free["T2"][r], signal=True)
                    BK.release(iA, [d1, ac])
                    BK.release(iB, [d2])
                    if isq:
                        d3 = P.op("dve", lambda e: e.tensor_tensor(
                            out=Qv(j, 0, ntot), in0=T1[:, r, 0:ntot], in1=T2[:, r, 0:ntot], op=ALU.add),
                            [d1, d2, state["y_store"]], signal=True)
                    else:
                        cc = j - 8
                        d3 = P.op("dve", lambda e: e.tensor_tensor(
                            out=KB[:, l, cc, 128:128 + n], in0=T1[:, r, 0:n], in1=T2[:, r, 0:n], op=ALU.add),
                            [d1, d2, state["kb_ready"][l]], signal=True)
                        if ns:
                            d3 = P.op("dve", lambda e: e.tensor_tensor(
                                out=KS[:, l, cc, 0:ns], in0=T1[:, r, n:ntot], in1=T2[:, r, n:ntot], op=ALU.add),
                                [d1, d2], signal=True)
                    t_free["T1"][r] = [d3]
                    t_free["T2"][r] = [d3]
                    state["rope_last"] = d3

                prev = None
                for j in js:
                    lA = (U_Q + 2 * j if j < 8 else U_K + 2 * (j - 8))
                    iA, bA, fA = BK.get()
                    pa = proj_group(wnext(lA), ntot, hch, 8, h_ready, bA, fA)
                    rq = rot["sq"]
                    rot["sq"] ^= 1
                    ac = P.op("act", lambda e: e.activation(
                        out=SQ[:, rq, 0:ntot], in_=bA[:, 0:ntot], func=AF.Copy), [pa] + sq_free[rq], signal=True)
                    cur = (j, iA, bA, pa, rq, ac)
                    if prev is not None:
                        finish(prev)
                    prev = cur
                    yield
                finish(prev)

            def v_gen():
                sV = wnext(U_V)
                assert sV % 2 == 0
                assert wnext(U_V + 1) == sV + 1
                slotV, wtokV = wunit(sV)
                _, wtokV2 = wunit(sV + 1)
                v_last = None
                pe_last = None
                nblk = n // 128
                for b in range(nblk + (1 if ns else 0)):
                    iV, bV, fV = BK.get()
                    is_s = (b == nblk)
                    m0, m1 = (n, ntot) if is_s else (128 * b, 128 * b + 128)
                    mm = m1 - m0
                    for kc in range(8):
                        pe_last = P.op("pe", lambda e, kc=kc, bV=bV, m0=m0, m1=m1, mm=mm: e.matmul(
                            bV[0:mm, 0:256], H[:, kc, m0:m1], WR[:, slotV:slotV + 2, kc * 128:(kc + 1) * 128],
                            start=(kc == 0), stop=(kc == 7)),
                            ([wtokV, wtokV2] + h_ready + fV) if kc == 0 else [], signal=(kc == 7))
                    if is_s:
                        v_last = P.op("act", lambda e, bV=bV, mm=mm: e.activation(
                            out=VS[0:mm, l, :], in_=bV[0:mm, 0:256], func=AF.Copy), [pe_last, c_init], signal=True)
                    else:
                        v_last = P.op("act", lambda e, bV=bV, b=b: e.activation(
                            out=VB[:, l, 1 + b, :], in_=bV[:, 0:256], func=AF.Copy),
                            [pe_last, state["kb_ready"][l]], signal=True)
                    BK.release(iV, [v_last])
                    yield
                wdone(sV, pe_last)
                wdone(sV + 1, pe_last)
                state["v_last"] = v_last

            def attention():
                rot_pt = [0, 0]

                def sample_group(k):
                    hh = k % 2
                    pr = k // 2
                    R0, R1 = 64 * hh, 64 * hh + 64
                    iS, bS, fS = BK.get()
                    r = rot_pt[0]
                    rot_pt[0] ^= 1
                    nq = ns
                    ncol = 4 * nq
                    qr = ARENA[R0:R1, 0:4096].rearrange("p (c t) -> p c t", t=TILE)[:, 4 * pr:4 * pr + 4, n:ntot]
                    P.op("pe", lambda e: e.matmul(
                        bS[:, 0:ncol], KC[R0:R1, l, pr, :], qr, start=True, stop=True), qk_ready + fS + [LDP])
                    ps = P.op("pe", lambda e: e.matmul(
                        bS[0:ns, 256:256 + ncol], KS[R0:R1, l, pr, 0:ns], qr, start=True, stop=True), [], signal=True)
                    P.op("act", lambda e: e.activation(
                        out=PT[:, r, 0:ncol], in_=bS[:, 0:ncol], func=AF.Exp, bias=ZER[:, 0:1], scale=0.125),
                        [ps] + t_free["PT"][r])
                    ap_ = P.op("act", lambda e: e.activation(
                        out=PT[0:ns, r, 256:256 + ncol], in_=bS[0:ns, 256:256 + ncol], func=AF.Exp,
                        bias=ZER[0:ns, 0:1], scale=0.125), [], signal=True)
                    BK.release(iS, [ap_])
                    iO, bO, fO = BK.get()
                    P.op("pe", lambda e: e.matmul(
                        bO[:, 0:ncol], VC[:, l, pr * 128:(pr + 1) * 128], PT[:, r, 0:ncol],
                        start=True, stop=False), [ap_] + fO + v_ready)
                    P.op("pe", lambda e: e.matmul(
                        bO[:, 0:ncol], VS[0:ns, l, pr * 128:(pr + 1) * 128], PT[0:ns, r, 256:256 + ncol],
                        start=False, stop=True))
                    P.op("pe", lambda e: e.matmul(
                        bO[:, 256:256 + ncol], ONES[:, :], PT[:, r, 0:ncol], start=True, stop=False))
                    P.op("pe", lambda e: e.matmul(
                        bO[:, 256:256 + ncol], ONES[0:ns, :], PT[0:ns, r, 256:256 + ncol],
                        start=False, stop=False))
                    esr_ap = ESR[:, l, k, :].rearrange("p (g q) -> p g q", q=64)[:, :, 0:nq]
                    po = P.op("pe", lambda e: e.matmul(
                        bO[:, 256:256 + ncol], ONEHOT0[:, :], esr_ap, start=False, stop=True), [c_esr], signal=True)
                    t_free["PT"][r] = [po]
                    rd = rot["dt"]
                    rot["dt"] ^= 1
                    d1 = P.op("act", lambda e: e.activation(
                        out=RD[R0:R1, rd, 0:ncol], in_=bO[R0:R1, 256:256 + ncol], func=AF.Ln),
                        [po] + t_free["DT"][rd], signal=True)
                    d2 = P.op("act", lambda e: e.activation(
                        out=RD[R0:R1, rd, 0:ncol], in_=RD[R0:R1, rd, 0:ncol], func=AF.Exp, scale=-1.0),
                        [d1], signal=True)
                    ao = ARENA[R0:R1, 4096:8192].rearrange("p (c t) -> p c t", t=TILE)[:, 4 * pr:4 * pr + 4, n:ntot]
                    d3 = P.op("dve", lambda e: e.tensor_tensor(
                        out=ao, in0=bO[R0:R1, 0:ncol].rearrange("p (g q) -> p g q", q=nq),
                        in1=RD[R0:R1, rd, 0:ncol].rearrange("p (g q) -> p g q", q=nq), op=ALU.mult),
                        [d2], signal=True)
                    t_free["DT"][rd] = [d3]
                    BK.release(iO, [d3])
                    state["ao_last"] = d3

                def stage_a(c, k):
                    hh = k % 2
                    pr = k // 2
                    R0, R1 = 64 * hh, 64 * hh + 64
                    par = c % 2
                    r = 2 * par + rot_pt[par]
                    rot_pt[par] ^= 1
                    iS, bS, fS = BK.get()
                    q0 = 64 * c
                    qr = ARENA[R0:R1, 0:4096].rearrange("p (c t) -> p c t", t=TILE)[:, 4 * pr:4 * pr + 4, q0:q0 + 64]
                    if par == 0:
                        fcol = 64 * c
                        fblk = c // 2
                        hlo = 64 * c + 128
                        hblk = c // 2 + 1
                        HR0, HR1 = 0, 64
                        hM = 64
                        full_is_halo = first_own and c == 0
                        half_is_halo = False
                    else:
                        fcol = 64 * (c + 1)
                        fblk = (c + 1) // 2
                        hlo = 64 * c - 64
                        hblk = (c - 1) // 2
                        HR0, HR1 = 64, 128
                        hM = 128
                        full_is_halo = False
                        half_is_halo = first_own and c == 1
                    P.op("pe", lambda e: e.matmul(
                        bS[:, 0:256], KB[R0:R1, l, pr, fcol:fcol + 128], qr, start=True, stop=True), qk_ready + fS)
                    ps = P.op("pe", lambda e: e.matmul(
                        bS[0:hM, 256:512], KB[R0:R1, l, pr, hlo:hlo + hM], qr, start=True, stop=True), [], signal=True)
                    bf = FLAGS[:, 0:1] if full_is_halo else ZER[:, 0:1]
                    bh = FLAGS[HR0:HR1, 0:1] if half_is_halo else ZER[HR0:HR1, 0:1]
                    P.op("act", lambda e: e.activation(
                        out=PT[:, r, 0:256], in_=bS[:, 0:256], func=AF.Exp, bias=bf, scale=0.125),
                        [ps, LD] + t_free["PT"][r])
                    ap_ = P.op("act", lambda e: e.activation(
                        out=PT[HR0:HR1, r, 256:512], in_=bS[HR0:HR1, 256:512], func=AF.Exp, bias=bh, scale=0.125),
                        [], signal=True)
                    BK.release(iS, [ap_])
                    return dict(ap=ap_, r=r, fblk=fblk, hblk=hblk, R0=R0, R1=R1, pr=pr, k=k, q0=q0)

                def stage_b(cx):
                    r, fblk, hblk, R0, R1, pr, k, q0 = (cx[x] for x in ("r", "fblk", "hblk", "R0", "R1", "pr", "k", "q0"))
                    iO, bO, fO = BK.get()
                    P.op("pe", lambda e: e.matmul(
                        bO[:, 0:256], VB[:, l, fblk, pr * 128:(pr + 1) * 128], PT[:, r, 0:256],
                        start=True, stop=False), [cx["ap"]] + fO + v_ready)
                    P.op("pe", lambda e: e.matmul(
                        bO[:, 0:256], VB[:, l, hblk, pr * 128:(pr + 1) * 128], PT[:, r, 256:512],
                        start=False, stop=True))
                    P.op("pe", lambda e: e.matmul(
                        bO[:, 256:512], ONES[:, :], PT[:, r, 0:256], start=True, stop=False))
                    P.op("pe", lambda e: e.matmul(
                        bO[:, 256:512], ONES[:, :], PT[:, r, 256:512], start=False, stop=False))
                    po = P.op("pe", lambda e: e.matmul(
                        bO[:, 256:512], ONEHOT0[:, :], ESR[:, l, k, :], start=False, stop=True), [c_esr], signal=True)
                    t_free["PT"][r] = [po]
                    rd = rot["dt"]
                    rot["dt"] ^= 1
                    d1 = P.op("act", lambda e: e.activation(
                        out=RD[R0:R1, rd, :], in_=bO[R0:R1, 256:512], func=AF.Ln),
                        [po] + t_free["DT"][rd], signal=True)
                    d2 = P.op("act", lambda e: e.activation(
                        out=RD[R0:R1, rd, :], in_=RD[R0:R1, rd, :], func=AF.Exp, scale=-1.0), [d1], signal=True)
                    ao = ARENA[R0:R1, 4096:8192].rearrange("p (c t) -> p c t", t=TILE)[:, 4 * pr:4 * pr + 4, q0:q0 + 64]
                    d3 = P.op("dve", lambda e: e.tensor_tensor(
                        out=ao, in0=bO[R0:R1, 0:256].rearrange("p (g q) -> p g q", q=64),
                        in1=RD[R0:R1, rd, :].rearrange("p (g q) -> p g q", q=64), op=ALU.mult), [d2], signal=True)
                    t_free["DT"][rd] = [d3]
                    BK.release(iO, [d3])
                    state["ao_last"] = d3

                prev = None
                for c in range(nchunks):
                    for k in range(NKV):
                        cx = stage_a(c, k)
                        if prev is not None:
                            stage_b(prev)
                        prev = cx
                        yield
                stage_b(prev)
                if ns:
                    for k in range(NKV):
                        sample_group(k)
                        yield

            acc_info = {}

            def conv_chain(c):
                r = rot["sg"]
                rot["sg"] ^= 1
                acc = None
                for j in range(TD):
                    w_ap = WDW[:, l, c, j:j + 1]
                    if j == 0:
                        acc = P.op("dve", lambda e: e.tensor_scalar(
                            out=SG[:, r, 0:n], in0=f32(Av(c, 0, n)), scalar1=w_ap, scalar2=vec(V_BDW, c),
                            op0=ALU.mult, op1=ALU.add), a_ready + [LD] + t_free["SG"][r], signal=True)
                        if ns:
                            acc = P.op("dve", lambda e: e.tensor_scalar(
                                out=SG[:, r, n:ntot], in0=f32(AS[:, l, c, 0:ns]), scalar1=w_ap,
                                scalar2=vec(V_BDW, c), op0=ALU.mult, op1=ALU.add), [], signal=True)
                    else:
                        accp = P.op("dve", lambda e: e.scalar_tensor_tensor(
                            out=SG[:, r, 0:n], in0=f32(Av(c, j, j + n)), scalar=w_ap, in1=SG[:, r, 0:n],
                            op0=ALU.mult, op1=ALU.add), [acc], signal=True)
                        if ns:
                            accp = P.op("dve", lambda e: e.scalar_tensor_tensor(
                                out=SG[:, r, n:ntot], in0=f32(AS[:, l, c, j:j + ns]), scalar=w_ap,
                                in1=SG[:, r, n:ntot], op0=ALU.mult, op1=ALU.add), [acc], signal=True)
                        acc = accp
                acc_info[c] = (r, acc)

            def conv():
                for c in range(NCH):
                    if c + 1 < NCH:
                        conv_chain(c + 1)
                    r, acc = acc_info[c]
                    iC, bC, fC = BK.get()
                    pe_t = None
                    for g in range(NCG):
                        s = wnext(U_CONV + 4 * c + g)
                        slot, wtok = wunit(s)
                        ntap = min(8, CONVW - TD - 8 * g)
                        for t in range(ntap):
                            j = TD + 8 * g + t
                            firstu = (t == 0)
                            pe_t = P.op("pe", lambda e, slot=slot, t=t, j=j: e.matmul(
                                bC[:, 0:n], WR[:, slot, t * 128:(t + 1) * 128], Av(c, j, j + n),
                                start=(j == TD), stop=(j == CONVW - 1)),
                                ([wtok] + (a_ready + fC if j == TD else [])) if firstu else [],
                                signal=(t == ntap - 1 and not ns))
                            if ns:
                                pe_t = P.op("pe", lambda e, slot=slot, t=t, j=j: e.matmul(
                                    bC[:, n:ntot], WR[:, slot, t * 128:(t + 1) * 128], AS[:, l, c, j:j + ns],
                                    start=False, stop=(j == CONVW - 1), skip_group_check=True),
                                    [], signal=(t == ntap - 1))
                        wdone(s, pe_t)
                        yield
                    d = P.op("dve", lambda e: e.tensor_tensor(
                        out=ZCv(c, 0, ntot), in0=bC[:, 0:ntot], in1=SG[:, r, 0:ntot], op=ALU.add),
                        [pe_t, acc], signal=True)
                    t_free["SG"][r] = [d]
                    BK.release(iC, [d])
                    state["zc_tok"][c] = d

            conv_chain(0)
            state["zc_tok"] = [None] * NCH
            for _ in conv():
                pass
            conv_pe_done = state["zc_tok"][NCH - 1]

            iM, bM, fM = BK.get()
            iQ, bQ, fQ = BK.get()
            lastm = lastq = None
            for c in range(NCH):
                r = rot["sq"]
                rot["sq"] ^= 1
                zt = state["zc_tok"][c]
                a_t = P.op("act", lambda e, c=c, r=r: e.activation(
                    out=SQ[:, r, 0:ntot], in_=f32(ZCv(c, 0, ntot)), func=AF.Square), [zt] + sq_free[r], signal=True)
                lastm = P.op("pe", lambda e, c=c: e.matmul(
                    bM[:, 0:ntot], ONES[:, :], ZCv(c, 0, ntot), start=(c == 0), stop=(c == NCH - 1)),
                    [zt] + (fM if c == 0 else []), signal=(c == NCH - 1))
                lastq = P.op("pe", lambda e, c=c, r=r: e.matmul(
                    bQ[:, 0:ntot], ONES[:, :], SQ[:, r, 0:ntot], start=(c == 0), stop=(c == NCH - 1)),
                    [a_t] + (fQ if c == 0 else []), signal=True)
                sq_free[r] = [lastq]
            dm = P.op("dve", lambda e: e.tensor_scalar(
                out=MU[:, 0:ntot], in0=bM[:, 0:ntot], scalar1=1.0 / D, scalar2=None, op0=ALU.mult),
                [lastm], signal=True)
            BK.release(iM, [dm])
            dm2 = P.op("dve", lambda e: e.tensor_tensor(
                out=M2[:, 0:ntot], in0=MU[:, 0:ntot], in1=MU[:, 0:ntot], op=ALU.mult), [dm], signal=True)
            dv = P.op("dve", lambda e: e.scalar_tensor_tensor(
                out=M2[:, 0:ntot], in0=bQ[:, 0:ntot], scalar=1.0 / D, in1=M2[:, 0:ntot],
                op0=ALU.mult, op1=ALU.subtract), [lastq, dm2], signal=True)
            BK.release(iQ, [dv])
            asd = P.op("act", lambda e: e.activation(
                out=M2[:, 0:ntot], in_=M2[:, 0:ntot], func=AF.Ln, bias=EPSB[:, 0:1], scale=1.0), [dv], signal=True)
            drs = P.op("act", lambda e: e.activation(out=RS[:, 0:ntot], in_=M2[:, 0:ntot], func=AF.Exp, scale=-0.5),
                       [asd], signal=True)
            ln_toks = []

            def ln_dve():
                for c in range(NCH):
                    r = rot["t"]
                    rot["t"] ^= 1
                    d1 = P.op("dve", lambda e, c=c, r=r: e.tensor_tensor(
                        out=T1[:, r, 0:ntot], in0=f32(ZCv(c, 0, ntot)), in1=MU[:, 0:ntot], op=ALU.subtract),
                        [dm, lastm, lastq] + t_free["T1"][r], signal=True)
                    d2 = P.op("dve", lambda e, c=c, r=r: e.tensor_tensor(
                        out=ZCv(c, 0, ntot), in0=T1[:, r, 0:ntot], in1=RS[:, 0:ntot], op=ALU.mult),
                        [d1, drs], signal=True)
                    t_free["T1"][r] = [d2]
                    ln_toks.append(d2)
                    yield

            def ln_apply():
                z_tok = None
                for c in range(NCH):
                    z_tok = P.op("act", lambda e, c=c: e.activation(
                        out=ZCv(c, 0, ntot), in_=f32(ZCv(c, 0, ntot)), func=AF.Silu, bias=vec(V_LNB, c),
                        scale=vec(V_LNG, c)), [ln_toks[c], LD], signal=True)
                    state["z_tok"] = z_tok
                yield

            alive = [qk_gen(range(8)), ln_dve()]
            while alive:
                for g_ in list(alive):
                    try:
                        next(g_)
                    except StopIteration:
                        alive.remove(g_)
            for _ in qk_gen(range(8, 10)):
                pass
            for _ in ln_apply():
                pass
            z_ready = [state["z_tok"]]
            for _ in v_gen():
                pass
            qk_ready = [state["rope_last"]]
            if l == 1:
                state["rope_free"] = [state["rope_last"]]
            v_ready = [state["v_last"]]
            for _ in attention():
                pass
            ao_ready = [state["ao_last"]]

            kb_tok = P.op("act", lambda e: e.activation(
                out=KB[:, l, :, 0:128], in_=f32(KB[:, l, :, n:n + 128]), func=AF.Copy), ao_ready, signal=True)
            nblk_ = n // 128
            vb_tok = P.op("act", lambda e: e.activation(
                out=VB[:, l, 0, :], in_=f32(VB[:, l, nblk_, :]), func=AF.Copy), ao_ready, signal=True)
            state["kb_ready"][l] = vb_tok
            a_all = ARENA[:, 12288:12288 + NCH * ACH].rearrange("p (c t) -> p c t", t=ACH)
            if p == 0:
                car2 = P.op("dve", lambda e: e.tensor_scalar(
                    out=ACAR[:, l, :, :], in0=f32(a_all[:, :, n:n + 30]), scalar1=FLAGS[:, 1:2], scalar2=None,
                    op0=ALU.mult), [conv_pe_done, LD], signal=True)
            else:
                car2 = P.op("dve", lambda e: e.tensor_copy(
                    out=ACAR[:, l, :, :], in_=f32(a_all[:, :, n:n + 30])), [conv_pe_done], signal=True)

            if p == npass - 1:
                state["out_tokens"].append(P.dma("sp", lambda e: e.dma_start(
                    out=kpT[l], in_=f32(KB[:, l, :, 0:128])), fo_sem, [kb_tok]))
                state["out_tokens"].append(P.dma("sp", lambda e: e.dma_start(
                    out=vp[l], in_=f32(VB[:, l, 0, :])), fo_sem, [vb_tok]))
                state["out_tokens"].append(P.dma("sp", lambda e: e.dma_start(
                    out=cpT[l], in_=f32(ACAR[:, l, :, :])), fo_sem, [car2]))
            if ns:
                state["out_tokens"].append(P.dma("sp", lambda e: e.dma_start(
                    out=ksT[l], in_=f32(KS[:, l, :, :])), fo_sem, qk_ready))
                state["out_tokens"].append(P.dma("sp", lambda e: e.dma_start(
                    out=vs[l], in_=f32(VS[0:ns, l, :])), fo_sem, v_ready))
                state["out_tokens"].append(P.dma("sp", lambda e: e.dma_start(
                    out=csT[l], in_=f32(AS[:, l, :, ns:ns + 30])), fo_sem, a_ready))

            mg_tok = None
            for j in range(NCH):
                lM = U_MERGE + 4 * j
                iA, bA, fA = BK.get()
                iC, bC, fC = BK.get()
                iG, bG, fG = BK.get()
                iH, bH, fH = BK.get()
                pA = proj_group(wnext(lM), ntot, lambda kc: AOv(kc, 0, ntot), 8, ao_ready, bA, fA)
                pC = proj_group(wnext(lM + 1), ntot, lambda kc: ZCv(kc, 0, ntot), 8, z_ready, bC, fC)
                pG = proj_group(wnext(lM + 2), ntot, hch, 8, h_ready, bG, fG)
                pH = proj_group(wnext(lM + 3), ntot, hch, 8, h_ready, bH, fH)
                r = rot["sg"]
                rot["sg"] ^= 1
                r2 = rot["t"]
                rot["t"] ^= 1
                a1 = P.op("act", lambda e, bG=bG, r=r: e.activation(
                    out=SG[:, r, 0:ntot], in_=bG[:, 0:ntot], func=AF.Sigmoid), [pG] + t_free["SG"][r], signal=True)
                a2 = P.op("act", lambda e, bH=bH, r2=r2: e.activation(
                    out=T2[:, r2, 0:ntot], in_=bH[:, 0:ntot], func=AF.Sigmoid), [pH] + t_free["T2"][r2], signal=True)
                BK.release(iG, [a1])
                BK.release(iH, [a2])
                d1 = P.op("dve", lambda e, bA=bA, r=r: e.tensor_tensor(
                    out=SG[:, r, 0:ntot], in0=bA[:, 0:ntot], in1=SG[:, r, 0:ntot], op=ALU.mult), [pA, a1], signal=True)
                d2 = P.op("dve", lambda e, bC=bC, r2=r2: e.tensor_tensor(
                    out=T2[:, r2, 0:ntot], in0=bC[:, 0:ntot], in1=T2[:, r2, 0:ntot], op=ALU.mult), [pC, a2], signal=True)
                BK.release(iA, [d1])
                BK.release(iC, [d2])
                mg_tok = P.op("dve", lambda e, j=j, r=r, r2=r2: e.tensor_tensor(
                    out=Qv(j, 0, ntot), in0=SG[:, r, 0:ntot], in1=T2[:, r2, 0:ntot], op=ALU.add),
                    [d1, d2] + ao_ready, signal=True)
                t_free["SG"][r] = [mg_tok]
                t_free["T2"][r2] = [mg_tok]
            mg_ready = [mg_tok]

            st2 = Stats(ntot)
            xt = [None] * NCH
            for j in range(NCH):
                s = wnext(U_OUT + j)
                iA, bA, fA = BK.get()
                pA = proj_group(s, ntot, lambda kc: Qv(kc, 0, ntot), 8, mg_ready, bA, fA)
                xt[j] = P.op("dve", lambda e, bA=bA, j=j: e.tensor_tensor(
                    out=X[:, j, 0:ntot], in0=bA[:, 0:ntot], in1=X[:, j, 0:ntot], op=ALU.add), [pA, h_tok], signal=True)
                BK.release(iA, [xt[j]])
                if j >= 1:
                    st2.add(X[:, j - 1, 0:ntot], [xt[j - 1]])
            st2.add(X[:, NCH - 1, 0:ntot], [xt[NCH - 1]])
            x2_ready = [xt[NCH - 1]]
            d_rs = st2.finish()

            h2_tok = None
            h2_toks = []
            for c in range(NCH):
                h2_tok = P.op("dve", lambda e, c=c: e.scalar_tensor_tensor(
                    out=H[:, c, 0:ntot], in0=X[:, c, 0:ntot], scalar=vec(V_NMLP, c), in1=RS[:, 0:ntot],
                    op0=ALU.mult, op1=ALU.mult), [d_rs] + x2_ready, signal=True)
                h2_toks.append(h2_tok)
            h2_ready = [h2_tok]

            hid_tok = None
            for i in range(32):
                s = wnext(U_UP + i)
                iA, bA, fA = BK.get()
                pA = proj_group(s, ntot, hch, 8, (lambda kc: [h2_toks[kc]]) if i == 0 else h2_ready, bA, fA)
                r = rot["sg"]
                rot["sg"] ^= 1
                a1 = P.op("act", lambda e, bA=bA, r=r: e.activation(
                    out=SG[:, r, 0:ntot], in_=bA[:, 0:ntot], func=AF.Relu), [pA] + t_free["SG"][r], signal=True)
                BK.release(iA, [a1])
                hid_tok = P.op("dve", lambda e, i=i, r=r: e.tensor_tensor(
                    out=HIDv(i, 0, ntot), in0=SG[:, r, 0:ntot], in1=SG[:, r, 0:ntot], op=ALU.mult),
                    [a1, car2, kb_tok, vb_tok], signal=True)
                t_free["SG"][r] = [hid_tok]
            hid_ready = [hid_tok]

            st3 = Stats(ntot)
            xt = [None] * NCH
            for j in range(NCH):
                iA, bA, fA = BK.get()
                pA = None
                for g in range(4):
                    s = wnext(U_DOWN + 4 * j + g)
                    pA = proj_group(s, ntot, lambda kc, g=g: HIDv(8 * g + kc, 0, ntot), 8, hid_ready, bA,
                                    fA if g == 0 else [], start=(g == 0), stop=(g == 3))
                xt[j] = P.op("dve", lambda e, bA=bA, j=j: e.tensor_tensor(
                    out=X[:, j, 0:ntot], in0=bA[:, 0:ntot], in1=X[:, j, 0:ntot], op=ALU.add), [pA, h2_tok], signal=True)
                BK.release(iA, [xt[j]])
                if j >= 1:
                    st3.add(X[:, j - 1, 0:ntot], [xt[j - 1]])
            st3.add(X[:, NCH - 1, 0:ntot], [xt[NCH - 1]])
            return [xt[NCH - 1]], st3.finish()

        for p in range(npass):
            if p == 0:
                n, ns = HALO, NSMP
                tok0, rcol0 = 0, 0
            else:
                n, ns = TILE, 0
                tok0 = HALO + TILE * (p - 1)
                rcol0 = HALO + NSMP + TILE * (p - 1)
            ntot = n + ns
            xsrc = xT.rearrange("(c q) t -> q c t", q=128)[:, :, tok0:tok0 + n]
            xt = P.dma("sp", lambda e, xsrc=xsrc, n=n: e.dma_start(out=X[:, :, 0:n], in_=xsrc), x_sem, state["x_free"])
            x_ready = [xt]
            if ns:
                xs_src = xsT.rearrange("(c q) t -> q c t", q=128)
                xt2 = P.dma("sp", lambda e, xs_src=xs_src, n=n, ntot=ntot: e.dma_start(
                    out=X[:, :, n:ntot], in_=xs_src), x_sem, state["x_free"])
                x_ready = [xt2]
            rt1 = P.dma("sp", lambda e, rcol0=rcol0, ntot=ntot: e.dma_start(
                out=RC[:, 0:ntot], in_=ropec_d[:, rcol0:rcol0 + ntot]), r_sem, state["rope_free"])
            rt2 = P.dma("sp", lambda e, rcol0=rcol0, ntot=ntot: e.dma_start(
                out=RSN[:, 0:ntot], in_=ropes_d[:, rcol0:rcol0 + ntot]), r_sem, state["rope_free"])
            rope_ready = [rt2]

            xr = x_ready
            d_rs = None
            for l in range(2):
                xr, d_rs = layer(p, l, n, ns, xr, rope_ready, d_rs)

            y_tok = None
            for c in range(NCH):
                y_tok = P.op("dve", lambda e, c=c, ntot=ntot: e.scalar_tensor_tensor(
                    out=X[:, c, 0:ntot], in0=X[:, c, 0:ntot], scalar=vec(80, c), in1=RS[:, 0:ntot],
                    op0=ALU.mult, op1=ALU.mult), [d_rs] + xr, signal=True)
            if p == 0:
                st = P.dma("sp", lambda e, n=n, ntot=ntot: e.dma_start(
                    out=ysT.rearrange("(c q) t -> q c t", q=128), in_=X[:, :, n:ntot]), o_sem, [y_tok])
            else:
                o0 = TILE * (p - 1)
                st = P.dma("sp", lambda e, o0=o0, n=n: e.dma_start(
                    out=ypT.rearrange("(c q) t -> q c t", q=128)[:, :, o0:o0 + n], in_=X[:, :, 0:n]),
                    o_sem, [y_tok])
            state["x_free"] = [st]

        finals = [(o_sem[0], o_sem[1]), (fo_sem[0], fo_sem[1])]

        with nc.Block() as block:
            @block.tensor
            def _(e):
                P.run("pe", e)

            @block.scalar
            def _(e):
                P.run("act", e)

            @block.vector
            def _(e):
                P.run("dve", e)

            @block.gpsimd
            def _(e):
                P.run("pool", e)

            @block.sync
            def _(e):
                P.run("sp", e, final_waits=finals)
    return nc, order


def _units_for_layer(w_in, w_o, w_dw, w_pw2, w_out, w_up, w_down):
    U = np.zeros((NU, 128, 1024), np.float32)

    def put(u, mat):
        U[u] = mat.reshape(8, 128, 128).transpose(1, 0, 2).reshape(128, 1024)

    m = np.arange(128)
    for cc in range(8):
        heads = np.array([q_head(cc, 0)] * 64 + [q_head(cc, 1)] * 64)
        dd = m % 64
        put(U_Q + 2 * cc, w_in[:, heads * 64 + dd])
        put(U_Q + 2 * cc + 1, w_in[:, heads * 64 + (dd + 32) % 64])
    for cc in range(2):
        base = 1024 + cc * 128
        put(U_K + 2 * cc, w_in[:, base + m])
        put(U_K + 2 * cc + 1, w_in[:, base + (m // 64) * 64 + (m % 64 + 32) % 64])
    for cc in range(2):
        put(U_V + cc, w_in[:, 1280 + cc * 128 + m])
    for c in range(8):
        put(U_U + 2 * c, w_in[:, 1536 + c * 128 + m])
        put(U_U + 2 * c + 1, w_in[:, 2560 + c * 128 + m])
    for c in range(8):
        for g in range(4):
            blk = np.zeros((128, 8, 128), np.float32)
            for t in range(8):
                j = TD + 8 * g + t
                if j < CONVW:
                    blk[m, t, m] = w_dw[j, c * 128 + m]
            U[U_CONV + 4 * c + g] = blk.reshape(128, 1024)
    rows = np.zeros(1024, np.int64)
    for cc in range(8):
        for half in range(2):
            rows[cc * 128 + half * 64: cc * 128 + half * 64 + 64] = q_head(cc, half) * 64 + np.arange(64)
    w_o_p = w_o[rows, :]
    for j in range(8):
        put(U_MERGE + 4 * j, w_o_p[:, j * 128 + m])
        put(U_MERGE + 4 * j + 1, w_pw2[:, j * 128 + m])
        put(U_MERGE + 4 * j + 2, w_in[:, 3584 + j * 128 + m])
        put(U_MERGE + 4 * j + 3, w_in[:, 4608 + j * 128 + m])
        put(U_OUT + j, w_out[:, j * 128 + m])
    for i in range(32):
        put(U_UP + i, w_up[:, i * 128 + m])
    for j in range(8):
        for g in range(4):
            put(U_DOWN + 4 * j + g, w_down[g * 1024:(g + 1) * 1024, j * 128 + m])
    return U


def _vec_cols(v):
    return np.ascontiguousarray(v.reshape(8, 128).T)


def _rope_tables(pos):
    half = HD // 2
    inv_freq = (np.float32(10000.0) ** (-np.arange(half, dtype=np.float32) / np.float32(half))).astype(np.float32)
    ang = pos.astype(np.float32)[None, :] * inv_freq[:, None]
    cos = np.cos(ang).astype(np.float32)
    sin = np.sin(ang).astype(np.float32)
    c64 = np.concatenate([cos, cos], 0)
    s64 = np.concatenate([-sin, sin], 0)
    return np.concatenate([c64, c64], 0), np.concatenate([s64, s64], 0)


_CACHE = {}


def kernel(x_prompt, x_sample, cache_k, cache_v, state_conv, norm_mix, w_in, sinks, w_o_attn,
           w_dw, b_dw, ln_conv_g, ln_conv_b, w_pw2, w_out, norm_mlp, w_up, w_down, norm_final):
    f = lambda a: np.asarray(a, dtype=np.float32)
    x_prompt, x_sample, cache_k, cache_v, state_conv = map(f, (x_prompt, x_sample, cache_k, cache_v, state_conv))
    norm_mix, w_in, sinks, w_o_attn, w_dw, b_dw = map(f, (norm_mix, w_in, sinks, w_o_attn, w_dw, b_dw))
    ln_conv_g, ln_conv_b, w_pw2, w_out, norm_mlp, w_up, w_down, norm_final = map(
        f, (ln_conv_g, ln_conv_b, w_pw2, w_out, norm_mlp, w_up, w_down, norm_final))
    B, SEQ, _ = x_prompt.shape
    own = SEQ // 2
    n_own = own // TILE
    ncores = 8
    assert B * 2 == ncores and x_sample.shape[0] == ncores
    if n_own not in _CACHE:
        _CACHE[n_own] = build_program(n_own)
    nc, order = _CACHE[n_own]

    assert len(order) == NUS and len(set(order)) == NUS
    wst = np.stack([_units_for_layer(w_in[l], w_o_attn[l], w_dw[l], w_pw2[l], w_out[l], w_up[l], w_down[l])[order]
                    for l in range(2)], 0)
    vecs = np.zeros((128, NVEC), np.float32)
    for l in range(2):
        for i, v in enumerate((norm_mix[l], b_dw[l], ln_conv_g[l], ln_conv_b[l], norm_mlp[l])):
            vecs[:, 40 * l + 8 * i: 40 * l + 8 * i + 8] = _vec_cols(v)
    vecs[:, 80:88] = _vec_cols(norm_final)
    wdw = np.ascontiguousarray(w_dw.reshape(2, CONVW, NCH, 128).transpose(3, 0, 2, 1)[:, :, :, :TD])
    sinks_b = np.ascontiguousarray(np.broadcast_to(sinks.reshape(1, 32), (128, 32))).astype(np.float32)

    mm_ = np.arange(128)
    perm = np.zeros((128, 128), np.float32)
    perm[(mm_ // 64) * 64 + (mm_ % 64 + 32) % 64, mm_] = 1.0
    in_maps = []
    for core in range(ncores):
        b, hf = core // 2, core % 2
        start = hf * own
        xT = np.zeros((D, HALO + own), np.float32)
        if hf == 1:
            xT[:, :] = x_prompt[b, start - HALO:start + own, :].T
        else:
            xT[:, HALO:] = x_prompt[b, 0:own, :].T
        pos = np.concatenate([np.arange(start - HALO, start), PAST + np.arange(NSMP), np.arange(start, start + own)])
        rc, rs = _rope_tables(pos.astype(np.float32))
        flags = np.zeros((128, 4), np.float32)
        flags[0, 2] = 1.0
        flags[:, 0] = 0.0 if hf == 1 else -30000.0
        flags[:, 1] = 1.0 if hf == 1 else 0.0
        in_maps.append({
            "xT": xT,
            "xsT": np.ascontiguousarray(x_sample[core].T),
            "wst": wst,
            "vecs": vecs,
            "wdw": wdw,
            "ropec": rc, "ropes": rs,
            "kcT": np.ascontiguousarray(cache_k[:, core].reshape(2, 128, 256).transpose(0, 2, 1)
                                        .reshape(2, 2, 128, 128).transpose(0, 2, 1, 3)),
            "vc": np.ascontiguousarray(cache_v[:, core].reshape(2, 128, 256)),
            "scT": np.ascontiguousarray(state_conv[:, core].transpose(0, 2, 1).reshape(2, 8, 128, 30)
                                        .transpose(0, 2, 1, 3)),
            "sinks": sinks_b,
            "perm": perm,
            "flags": flags,
        })
    res = run_bass_kernel_spmd(nc, in_maps, core_ids=list(range(ncores)))
    R = res.results

    y_prompt = np.zeros((B, SEQ, D), np.float32)
    y_sample = np.zeros((ncores, NSMP, D), np.float32)
    k_p = np.zeros((2, B, 128, NKV, HD), np.float32)
    v_p = np.zeros((2, B, 128, NKV, HD), np.float32)
    c_p = np.zeros((2, B, 30, D), np.float32)
    k_s = np.zeros((2, ncores, NSMP, NKV, HD), np.float32)
    v_s = np.zeros((2, ncores, NSMP, NKV, HD), np.float32)
    c_s = np.zeros((2, ncores, 30, D), np.float32)
    for core in range(ncores):
        b, hf = core // 2, core % 2
        r = R[core]
        y_prompt[b, hf * own:(hf + 1) * own, :] = r["ypT"].T
        y_sample[core] = r["ysT"].T
        k_s[:, core] = r["ksT"].transpose(0, 3, 2, 1).reshape(2, NSMP, NKV, HD)
        v_s[:, core] = r["vs"].reshape(2, NSMP, NKV, HD)
        c_s[:, core] = r["csT"].transpose(0, 3, 2, 1).reshape(2, 30, D)
        if hf == 1:
            k_p[:, b] = r["kpT"].transpose(0, 3, 2, 1).reshape(2, 128, NKV, HD)
            v_p[:, b] = r["vp"].reshape(2, 128, NKV, HD)
            c_p[:, b] = r["cpT"].transpose(0, 3, 2, 1).reshape(2, 30, D)
    return (y_prompt, y_sample, k_p, v_p, c_p, k_s, v_s, c_s)
```

```python
import contextlib
import numpy as np
import concourse.bass as bass
import concourse.mybir as mybir
from concourse.bass_utils import run_bass_kernel_spmd

F32 = mybir.dt.float32
F32R = mybir.dt.float32r
BF16 = mybir.dt.bfloat16
AF = mybir.ActivationFunctionType
ALU = mybir.AluOpType

D = 1024
NCH = 8
HD = 64
NQH = 16
NKV = 4
CONVW = 31
DFF = 4096
NIN = 5632
EPS = 1e-6
HALO = 256
TILE = 512
NSMP = 16
PAST = 2048
NU = 174
TD = 7
NCG = 3
NUS = 156
U_Q, U_K, U_V, U_U, U_CONV, U_MERGE, U_OUT, U_UP, U_DOWN = 0, 16, 20, 22, 38, 70, 102, 110, 142
NS = 10
ACH = 544
NVEC = 88
N_OWN_TILES = 8


def q_head(cc, half):
    return (cc + 4 * half) if cc < 4 else (8 + (cc - 4) + 4 * half)


class _Rec:
    def __init__(self):
        self.call = None

    def __getattr__(self, name):
        def f(*a, **k):
            self.call = (name, a, k)
            return None
        return f


def _record(fn):
    r = _Rec()
    fn(r)
    assert r.call is not None
    return r.call


class Prog:
    ENGS = ("pe", "act", "dve", "pool", "sp")

    def __init__(self, nc, es):
        self.nc = nc
        self.q = {e: [] for e in self.ENGS}
        self.sem = {e: es.enter_context(nc.semaphore("prog_" + e)) for e in self.ENGS}
        self.cnt = {e: 0 for e in self.ENGS}

    def op(self, eng, fn, waits=(), signal=False):
        tok = None
        if signal:
            self.cnt[eng] += 1
            tok = (self.sem[eng], self.cnt[eng])
        ws = []
        for w in waits:
            if w is None:
                continue
            if isinstance(w, list):
                ws.extend([x for x in w if x is not None])
            else:
                ws.append(w)
        self.q[eng].append((_record(fn), tuple(ws), tok, 1))
        return tok

    def dma(self, eng, fn, dsem, waits=()):
        dsem[1] += 16
        tok = (dsem[0], dsem[1])
        ws = []
        for w in waits:
            if w is None:
                continue
            if isinstance(w, list):
                ws.extend([x for x in w if x is not None])
            else:
                ws.append(w)
        self.q[eng].append((_record(fn), tuple(ws), tok, 16))
        return tok

    def run(self, eng, e, final_waits=()):
        seen = {}
        for (mname, margs, mkw), waits, tok, inc in self.q[eng]:
            for (sem, val) in waits:
                k = sem.num
                if seen.get(k, 0) >= val:
                    continue
                e.wait_ge(sem, val)
                seen[k] = val
            ins = getattr(e, mname)(*margs, **mkw)
            if tok is not None:
                ins.then_inc(tok[0], inc)
        for (sem, val) in final_waits:
            e.wait_ge(sem, val)


class Banks:
    def __init__(self, tensors):
        self.t = tensors
        self.free = [[] for _ in tensors]
        self.held = [False] * len(tensors)
        self.i = 0

    def get(self):
        n = len(self.t)
        for _ in range(n):
            i = self.i
            self.i = (i + 1) % n
            if not self.held[i]:
                self.held[i] = True
                fr = self.free[i]
                self.free[i] = []
                return i, self.t[i], fr
        raise RuntimeError("no free psum bank")

    def release(self, i, toks):
        self.held[i] = False
        self.free[i] = [t for t in toks if t is not None]


def build_program(n_own):
    npass = 1 + n_own
    ntok = HALO + TILE * n_own
    nc = bass.Bass("TRN2", target_bir_lowering=False)

    def din(name, shape):
        return nc.dram_tensor(name, list(shape), F32, kind="ExternalInput").ap()

    def dout(name, shape):
        return nc.dram_tensor(name, list(shape), F32, kind="ExternalOutput").ap()

    xT = din("xT", [D, ntok])
    xsT = din("xsT", [D, NSMP])
    wst = din("wst", [2, NUS, 128, 1024])
    perm_d = din("perm", [128, 128])
    vecs_d = din("vecs", [128, NVEC])
    wdw_d = din("wdw", [128, 2, NCH, TD])
    ropec_d = din("ropec", [128, ntok + NSMP])
    ropes_d = din("ropes", [128, ntok + NSMP])
    kcT_d = din("kcT", [2, 128, 2, 128])
    vc_d = din("vc", [2, 128, 256])
    scT_d = din("scT", [2, 128, 8, 30])
    sinks_d = din("sinks", [128, 32])
    flags_d = din("flags", [128, 4])

    ypT = dout("ypT", [D, TILE * n_own])
    ysT = dout("ysT", [D, NSMP])
    kpT = dout("kpT", [2, 128, 2, 128])
    vp = dout("vp", [2, 128, 256])
    cpT = dout("cpT", [2, 128, 8, 30])
    ksT = dout("ksT", [2, 128, 2, NSMP])
    vs = dout("vs", [2, NSMP, 256])
    csT = dout("csT", [2, 128, 8, 30])

    es = contextlib.ExitStack()
    with es:
        def sb(name, shape, dt=F32):
            return es.enter_context(nc.sbuf_tensor(name, list(shape), dt))

        X = sb("X", [128, NCH, TILE])
        H = sb("H", [128, NCH, TILE], F32R)
        ARENA = sb("ARENA", [128, 12288 + NCH * ACH], F32R)
        KB = sb("KB", [128, 2, 2, 128 + TILE], F32R)
        VB = sb("VB", [128, 2, 5, 256], BF16)
        VF = sb("VF", [128, 2, 256])
        VSF = sb("VSF", [128, 2, 256])
        ACAR = sb("ACAR", [128, 2, NCH, 30], F32R)
        AS = sb("AS", [128, 2, NCH, 30 + NSMP], F32R)
        KC = sb("KC", [128, 2, 2, 128], F32R)
        VC = sb("VC", [128, 2, 256], BF16)
        KS = sb("KS", [128, 2, 2, NSMP], F32R)
        VS = sb("VS", [128, 2, 256], BF16)
        PT = sb("PT", [128, 4, 512], BF16)
        T1 = sb("T1", [128, 2, TILE])
        T2 = sb("T2", [128, 2, TILE])
        SG = sb("SG", [128, 2, TILE])
        SQ = sb("SQ", [128, 2, TILE], F32R)
        RS = sb("RS", [128, TILE])
        MU = sb("MU", [128, TILE])
        M2 = sb("M2", [128, TILE])
        RD = sb("RD", [128, 2, 256])
        RC = sb("RC", [128, TILE])
        RSN = sb("RSN", [128, TILE])
        ESK = sb("ESK", [128, 32])
        ESR = sb("ESR", [128, 2, 4, 256], BF16)
        ONES = sb("ONES", [128, 128], F32R)
        ONEHOT0 = sb("ONEHOT0", [128, 128], BF16)
        ONESB = sb("ONESB", [128, 128], BF16)
        PERM = sb("PERM", [128, 128], F32R)
        ZER = sb("ZER", [128, 64])
        VECS = sb("VECS", [128, NVEC])
        WDW = sb("WDW", [128, 2, NCH, TD])
        FLAGS = sb("FLAGS", [128, 4])
        EPSB = sb("EPSB", [128, 1])
        WR = sb("WR", [128, NS, 1024], F32R)

        banks_t = [es.enter_context(nc.psum_tensor("bank%d" % i, [128, 512], F32)) for i in range(8)]
        BK = Banks(banks_t)

        P = Prog(nc, es)

        def newsem(name):
            return [es.enter_context(nc.semaphore(name)), 0]

        wsem = [newsem("wsem%d" % i) for i in range(NS // 2)]
        ld_sem = newsem("ld")
        ldp_sem = newsem("ldp")
        x_sem = newsem("xs")
        r_sem = newsem("rs")
        o_sem = newsem("os")
        fo_sem = newsem("fo")

        def Qv(c, a, b):
            return ARENA[:, c * TILE + a: c * TILE + b]

        def AOv(c, a, b):
            return ARENA[:, 4096 + c * TILE + a: 4096 + c * TILE + b]

        def ZCv(c, a, b):
            return ARENA[:, 8192 + c * TILE + a: 8192 + c * TILE + b]

        def Av(c, a, b):
            return ARENA[:, 12288 + c * ACH + a: 12288 + c * ACH + b]

        def HIDv(i, a, b):
            return ARENA[:, i * TILE + a: i * TILE + b]

        def f32(ap):
            return ap.bitcast(F32)

        total_units = npass * 2 * NUS

        def issue_wdma(s0, waits):
            rem = s0 % (2 * NUS)
            l, u = rem // NUS, rem % NUS
            slot = s0 % NS
            pair = slot // 2
            src = wst[l, u:u + 2].rearrange("u p f -> p u f")
            dst = WR[:, slot:slot + 2, :]
            P.dma("pool", lambda e, dst=dst, src=src: e.dma_start(out=dst, in_=src), wsem[pair], waits)

        def wunit(s):
            slot = s % NS
            pair = slot // 2
            fill = s // NS + 1
            return slot, (wsem[pair][0], 16 * fill)

        def wdone(s, tok):
            if s % 2 == 1:
                nxt = s - 1 + NS
                if nxt < total_units:
                    issue_wdma(nxt, [tok])

        order = []
        wcount = [0]

        def wnext(logical):
            s_ = wcount[0]
            wcount[0] += 1
            if s_ < NUS:
                order.append(logical)
            else:
                assert order[s_ % NUS] == logical, (s_, logical)
            return s_

        P.dma("sp", lambda e: e.dma_start(out=VECS[:, :], in_=vecs_d[:, :]), ld_sem)
        P.dma("sp", lambda e: e.dma_start(out=FLAGS[:, :], in_=flags_d[:, :]), ld_sem)
        P.dma("sp", lambda e: e.dma_start(out=ESK[:, :], in_=sinks_d[:, :]), ld_sem)
        P.dma("sp", lambda e: e.dma_start(out=WDW[:, :, :, :], in_=wdw_d[:, :, :, :]), ld_sem)
        LD = (ld_sem[0], ld_sem[1])
        P.dma("pool", lambda e: e.dma_start(out=KC[:, :, :, :], in_=kcT_d.rearrange("l p c t -> p l c t")), ldp_sem)
        P.dma("pool", lambda e: e.dma_start(out=VC[:, :, :], in_=vc_d.rearrange("l p f -> p l f")), ldp_sem)
        for l in range(2):
            P.dma("pool", lambda e, l=l: e.dma_start(out=AS[:, l, :, 0:30], in_=scT_d[l]), ldp_sem)
        P.dma("pool", lambda e: e.dma_start(out=PERM[:, :], in_=perm_d[:, :]), ldp_sem)
        LDP = (ldp_sem[0], ldp_sem[1])
        for s0 in range(0, NS, 2):
            issue_wdma(s0, [])

        P.op("dve", lambda e: e.memset(T1[:, :, :], 0.0))
        P.op("dve", lambda e: e.memset(ZER[:, :], 0.0))
        P.op("dve", lambda e: e.memset(EPSB[:, :], EPS))
        z_src = T1[:, :, :].rearrange("p a b -> p (a b)")
        c_ones = P.op("dve", lambda e: e.tensor_scalar(out=ONES[:, :], in0=z_src[:, 0:128], scalar1=1.0,
                                                        scalar2=None, op0=ALU.add), signal=True)

        P.op("dve", lambda e: e.tensor_scalar(out=ONESB[:, :], in0=z_src[:, 0:128], scalar1=1.0,
                                              scalar2=None, op0=ALU.add))
        P.op("dve", lambda e: e.tensor_scalar(out=ONEHOT0[:, :], in0=z_src[:, 0:128], scalar1=FLAGS[:, 2:3],
                                              scalar2=None, op0=ALU.add), [LD])

        def zero_fill(flat, ncols):
            tok = None
            for c0 in range(0, ncols, 1024):
                c1 = min(ncols, c0 + 1024)
                tok = P.op("dve", lambda e, c0=c0, c1=c1: e.tensor_copy(out=flat[:, c0:c1], in_=z_src[:, 0:c1 - c0]),
                           signal=True)
            return tok

        zero_fill(KB[:, :, :, :].rearrange("p a b c -> p (a b c)"), 2 * 2 * (128 + TILE))
        zero_fill(VB[:, :, :, :].rearrange("p a b c -> p (a b c)"), 2 * 5 * 256)
        zero_fill(ACAR[:, :, :, :].rearrange("p a b c -> p (a b c)"), 2 * NCH * 30)
        zero_fill(PT[:, :, :].rearrange("p a b -> p (a b)"), 2048)
        c_init = zero_fill(VS[:, :, :].rearrange("p a b -> p (a b)"), 512)
        c_esk = P.op("act", lambda e: e.activation(out=ESK[:, :], in_=ESK[:, :], func=AF.Exp), [LD], signal=True)
        c_esr = None
        for l in range(2):
            for k in range(NKV):
                for g in range(4):
                    i = l * 16 + 4 * k + g
                    c_esr = P.op("dve", lambda e, l=l, k=k, g=g, i=i: e.tensor_scalar(
                        out=ESR[:, l, k, g * 64:(g + 1) * 64], in0=ZER[:, :], scalar1=ESK[:, i:i + 1], scalar2=None,
                        op0=ALU.add), [c_esk], signal=True)

        state = {
            "x_free": [],
            "rope_free": [],
            "y_store": None,
            "kb_ready": [None, None],
            "out_tokens": [],
        }
        sq_free = [[], []]
        t_free = {"T1": [[], []], "T2": [[], []], "SG": [[], []], "PT": [[], [], [], []], "DT": [[], []]}
        rot = {"sq": 0, "t": 0, "sg": 0, "pt": 0, "dt": 0}

        def vec(base, c):
            return VECS[:, base + c: base + c + 1]

        class Stats:
            def __init__(self, ntot):
                self.ntot = ntot
                self.bi, self.bank, self.bfree = BK.get()
                self.n = 0
                self.last = None

            def add(self, src_ap, waits):
                ntot, bank, c = self.ntot, self.bank, self.n
                r = rot["sq"]
                rot["sq"] ^= 1
                a_t = P.op("act", lambda e: e.activation(out=SQ[:, r, 0:ntot], in_=src_ap, func=AF.Square),
                           list(waits) + sq_free[r], signal=True)
                self.last = P.op("pe", lambda e: e.matmul(
                    bank[:, 0:ntot], ONES[:, :], SQ[:, r, 0:ntot], start=(c == 0), stop=(c == NCH - 1)),
                    [a_t, c_ones] + (self.bfree if c == 0 else []), signal=True)
                sq_free[r] = [self.last]
                self.n += 1

            def finish(self):
                ntot, bank = self.ntot, self.bank
                assert self.n == NCH
                a2 = P.op("act", lambda e: e.activation(out=MU[:, 0:ntot], in_=bank[:, 0:ntot], func=AF.Ln,
                                                         bias=EPSB[:, 0:1], scale=1.0 / D), [self.last], signal=True)
                BK.release(self.bi, [a2])
                return P.op("act", lambda e: e.activation(out=RS[:, 0:ntot], in_=MU[:, 0:ntot], func=AF.Exp,
                                                          scale=-0.5), [a2], signal=True)

        def rms_stats(src_chunk, ntot, src_waits):
            st = Stats(ntot)
            for c in range(NCH):
                st.add(src_chunk(c), src_waits)
            return st.finish()

        def proj_group(s, ntot, rhs_chunk, nk, rhs_waits, bank, bfree, start=True, stop=True, col0=0):
            slot, wtok = wunit(s)
            last = None
            for kc in range(nk):
                first = (kc == 0)
                last = P.op("pe", lambda e, kc=kc, slot=slot, first=first: e.matmul(
                    bank[:, col0:col0 + ntot], WR[:, slot, kc * 128:(kc + 1) * 128], rhs_chunk(kc),
                    start=(start and first), stop=(stop and kc == nk - 1)),
                    (rhs_waits(kc) + ([wtok] + bfree if first else [])) if callable(rhs_waits)
                    else (([wtok] + rhs_waits + bfree) if first else []), signal=(kc == nk - 1))
            wdone(s, last)
            return last

        def layer(p, l, n, ns, x_ready, rope_ready, d_rs_in):
            ntot = n + ns
            vb = 40 * l
            V_NM, V_BDW, V_LNG, V_LNB, V_NMLP, V_NF = vb, vb + 8, vb + 16, vb + 24, vb + 32, 80
            nchunks = n // 64
            first_own = (p == 1)

            d_rs = d_rs_in if d_rs_in is not None else rms_stats(lambda c: X[:, c, 0:ntot], ntot, x_ready)
            h_tok = None
            h_toks = []
            for c in range(NCH):
                h_tok = P.op("dve", lambda e, c=c: e.scalar_tensor_tensor(
                    out=H[:, c, 0:ntot], in0=X[:, c, 0:ntot], scalar=vec(V_NM, c), in1=RS[:, 0:ntot],
                    op0=ALU.mult, op1=ALU.mult), [d_rs, LD] + x_ready, signal=True)
                h_toks.append(h_tok)
            h_ready = [h_tok]
            h_first = [True]

            def h_waits(kc):
                return [h_toks[kc]]

            def hch(kc):
                return H[:, kc, 0:ntot]

            car_tok = P.op("act", lambda e: e.activation(
                out=ARENA[:, 12288:12288 + NCH * ACH].rearrange("p (c t) -> p c t", t=ACH)[:, :, 0:30],
                in_=f32(ACAR[:, l, :, :]), func=AF.Copy), [c_init] + x_ready, signal=True)
            a_last = None
            for c in range(NCH):
                iA, bA, fA = BK.get()
                iB, bB, fB = BK.get()
                pa = proj_group(wnext(U_U + 2 * c), ntot, hch, 8, h_waits if c == 0 else h_ready, bA, fA)
                pb = proj_group(wnext(U_U + 2 * c + 1), ntot, hch, 8, h_ready, bB, fB)
                r = rot["sg"]
                rot["sg"] ^= 1
                a1 = P.op("act", lambda e, bB=bB, r=r: e.activation(
                    out=SG[:, r, 0:ntot], in_=bB[:, 0:ntot], func=AF.Sigmoid), [pb] + t_free["SG"][r], signal=True)
                BK.release(iB, [a1])
                d1 = P.op("dve", lambda e, bA=bA, r=r, c=c: e.tensor_tensor(
                    out=Av(c, 30, 30 + n), in0=bA[:, 0:n], in1=SG[:, r, 0:n], op=ALU.mult), [pa, a1], signal=True)
                if ns:
                    d1 = P.op("dve", lambda e, bA=bA, r=r, c=c: e.tensor_tensor(
                        out=AS[:, l, c, 30:30 + ns], in0=bA[:, n:ntot], in1=SG[:, r, n:ntot], op=ALU.mult),
                        [pa, a1, LDP], signal=True)
                BK.release(iA, [d1])
                t_free["SG"][r] = [d1]
                a_last = d1
            a_ready = [a_last, car_tok, LDP]

            def qk_gen(js):
                def finish(cx):
                    j, iA, bA, pa, rq, ac = cx
                    isq = j < 8
                    iB, bB, fB = BK.get()
                    pb = P.op("pe", lambda e: e.matmul(
                        bB[:, 0:ntot], PERM[:, :], SQ[:, rq, 0:ntot], start=True, stop=True),
                        [ac, LDP] + fB, signal=True)
                    sq_free[rq] = [pb]
                    r = rot["t"]
                    rot["t"] ^= 1
                    d1 = P.op("dve", lambda e: e.tensor_tensor(
                        out=T1[:, r, 0:ntot], in0=bA[:, 0:ntot], in1=RC[:, 0:ntot], op=ALU.mult),
                        [pa, ac] + rope_ready + t_free["T1"][r], signal=True)
                    d2 = P.op("dve", lambda e: e.tensor_tensor(
                        out=T2[:, r, 0:ntot], in0=bB[:, 0:ntot], in1=RSN[:, 0:ntot], op=ALU.mult),
                        [pb] + rope_ready + t_free["T2"][r], signal=True)
                    BK.release(iA, [d1, ac])
                    BK.release(iB, [d2])
                    if isq:
                        d3 = P.op("dve", lambda e: e.tensor_tensor(
                            out=Qv(j, 0, ntot), in0=T1[:, r, 0:ntot], in1=T2[:, r, 0:ntot], op=ALU.add),
                            [d1, d2, state["y_store"]], signal=True)
                    else:
                        cc = j - 8
                        d3 = P.op("dve", lambda e: e.tensor_tensor(
                            out=KB[:, l, cc, 128:128 + n], in0=T1[:, r, 0:n], in1=T2[:, r, 0:n], op=ALU.add),
                            [d1, d2, state["kb_ready"][l]], signal=True)
                        if ns:
                            d3 = P.op("dve", lambda e: e.tensor_tensor(
                                out=KS[:, l, cc, 0:ns], in0=T1[:, r, n:ntot], in1=T2[:, r, n:ntot], op=ALU.add),
                                [d1, d2], signal=True)
                    t_free["T1"][r] = [d3]
                    t_free["T2"][r] = [d3]
                    state["rope_last"] = d3

                prev = None
                for j in js:
                    lA = (U_Q + 2 * j if j < 8 else U_K + 2 * (j - 8))
                    iA, bA, fA = BK.get()
                    pa = proj_group(wnext(lA), ntot, hch, 8, h_ready, bA, fA)
                    rq = rot["sq"]
                    rot["sq"] ^= 1
                    ac = P.op("act", lambda e: e.activation(
                        out=SQ[:, rq, 0:ntot], in_=bA[:, 0:ntot], func=AF.Copy), [pa] + sq_free[rq], signal=True)
                    cur = (j, iA, bA, pa, rq, ac)
                    if prev is not None:
                        finish(prev)
                    prev = cur
                    yield
                finish(prev)

            def v_gen():
                sV = wnext(U_V)
                assert sV % 2 == 0
                assert wnext(U_V + 1) == sV + 1
                slotV, wtokV = wunit(sV)
                _, wtokV2 = wunit(sV + 1)
                v_last = None
                pe_last = None
                nblk = n // 128
                for b in range(nblk + (1 if ns else 0)):
                    iV, bV, fV = BK.get()
                    is_s = (b == nblk)
                    m0, m1 = (n, ntot) if is_s else (128 * b, 128 * b + 128)
                    mm = m1 - m0
                    for kc in range(8):
                        pe_last = P.op("pe", lambda e, kc=kc, bV=bV, m0=m0, m1=m1, mm=mm: e.matmul(
                            bV[0:mm, 0:256], H[:, kc, m0:m1], WR[:, slotV:slotV + 2, kc * 128:(kc + 1) * 128],
                            start=(kc == 0), stop=(kc == 7)),
                            ([wtokV, wtokV2] + h_ready + fV) if kc == 0 else [], signal=(kc == 7))
                    if is_s:
                        state["vsf_tok"] = P.op("act", lambda e, bV=bV, mm=mm: e.activation(
                            out=VSF[0:mm, l, :], in_=bV[0:mm, 0:256], func=AF.Copy), [pe_last], signal=True)
                        v_last = P.op("act", lambda e, bV=bV, mm=mm: e.activation(
                            out=VS[0:mm, l, :], in_=bV[0:mm, 0:256], func=AF.Copy), [pe_last, c_init], signal=True)
                    else:
                        if p == npass - 1 and b == nblk - 1:
                            state["vf_tok"] = P.op("act", lambda e, bV=bV: e.activation(
                                out=VF[:, l, :], in_=bV[:, 0:256], func=AF.Copy), [pe_last], signal=True)
                        v_last = P.op("act", lambda e, bV=bV, b=b: e.activation(
                            out=VB[:, l, 1 + b, :], in_=bV[:, 0:256], func=AF.Copy),
                            [pe_last, state["kb_ready"][l]], signal=True)
                    BK.release(iV, [v_last])
                    yield
                wdone(sV, pe_last)
                wdone(sV + 1, pe_last)
                state["v_last"] = v_last

            def attention():
                rot_pt = [0, 0]

                def sample_group(k):
                    hh = k % 2
                    pr = k // 2
                    R0, R1 = 64 * hh, 64 * hh + 64
                    iS, bS, fS = BK.get()
                    r = rot_pt[0]
                    rot_pt[0] ^= 1
                    nq = ns
                    ncol = 4 * nq
                    qr = ARENA[R0:R1, 0:4096].rearrange("p (c t) -> p c t", t=TILE)[:, 4 * pr:4 * pr + 4, n:ntot]
                    P.op("pe", lambda e: e.matmul(
                        bS[:, 0:ncol], KC[R0:R1, l, pr, :], qr, start=True, stop=True), qk_ready + fS + [LDP])
                    ps = P.op("pe", lambda e: e.matmul(
                        bS[0:ns, 256:256 + ncol], KS[R0:R1, l, pr, 0:ns], qr, start=True, stop=True), [], signal=True)
                    P.op("act", lambda e: e.activation(
                        out=PT[:, r, 0:ncol], in_=bS[:, 0:ncol], func=AF.Exp, bias=ZER[:, 0:1], scale=0.125),
                        [ps] + t_free["PT"][r])
                    ap_ = P.op("act", lambda e: e.activation(
                        out=PT[0:ns, r, 256:256 + ncol], in_=bS[0:ns, 256:256 + ncol], func=AF.Exp,
                        bias=ZER[0:ns, 0:1], scale=0.125), [], signal=True)
                    BK.release(iS, [ap_])
                    iO, bO, fO = BK.get()
                    P.op("pe", lambda e: e.matmul(
                        bO[:, 0:ncol], VC[:, l, pr * 128:(pr + 1) * 128], PT[:, r, 0:ncol],
                        start=True, stop=False), [ap_] + fO + v_ready)
                    P.op("pe", lambda e: e.matmul(
                        bO[:, 0:ncol], VS[0:ns, l, pr * 128:(pr + 1) * 128], PT[0:ns, r, 256:256 + ncol],
                        start=False, stop=True))
                    P.op("pe", lambda e: e.matmul(
                        bO[:, 256:256 + ncol], ONESB[:, :], PT[:, r, 0:ncol], start=True, stop=False))
                    P.op("pe", lambda e: e.matmul(
                        bO[:, 256:256 + ncol], ONESB[0:ns, :], PT[0:ns, r, 256:256 + ncol],
                        start=False, stop=False))
                    esr_ap = ESR[:, l, k, :].rearrange("p (g q) -> p g q", q=64)[:, :, 0:nq]
                    po = P.op("pe", lambda e: e.matmul(
                        bO[:, 256:256 + ncol], ONEHOT0[:, :], esr_ap, start=False, stop=True), [c_esr], signal=True)
                    t_free["PT"][r] = [po]
                    rd = rot["dt"]
                    rot["dt"] ^= 1
                    d1 = P.op("act", lambda e: e.activation(
                        out=RD[R0:R1, rd, 0:ncol], in_=bO[R0:R1, 256:256 + ncol], func=AF.Ln),
                        [po] + t_free["DT"][rd], signal=True)
                    d2 = P.op("act", lambda e: e.activation(
                        out=RD[R0:R1, rd, 0:ncol], in_=RD[R0:R1, rd, 0:ncol], func=AF.Exp, scale=-1.0),
                        [d1], signal=True)
                    ao = ARENA[R0:R1, 4096:8192].rearrange("p (c t) -> p c t", t=TILE)[:, 4 * pr:4 * pr + 4, n:ntot]
                    d3 = P.op("dve", lambda e: e.tensor_tensor(
                        out=ao, in0=bO[R0:R1, 0:ncol].rearrange("p (g q) -> p g q", q=nq),
                        in1=RD[R0:R1, rd, 0:ncol].rearrange("p (g q) -> p g q", q=nq), op=ALU.mult),
                        [d2], signal=True)
                    t_free["DT"][rd] = [d3]
                    BK.release(iO, [d3])
                    state["ao_last"] = d3

                def stage_a(c, k):
                    hh = k % 2
                    pr = k // 2
                    R0, R1 = 64 * hh, 64 * hh + 64
                    par = c % 2
                    r = 2 * par + rot_pt[par]
                    rot_pt[par] ^= 1
                    iS, bS, fS = BK.get()
                    q0 = 64 * c
                    qr = ARENA[R0:R1, 0:4096].rearrange("p (c t) -> p c t", t=TILE)[:, 4 * pr:4 * pr + 4, q0:q0 + 64]
                    if par == 0:
                        fcol = 64 * c
                        fblk = c // 2
                        hlo = 64 * c + 128
                        hblk = c // 2 + 1
                        HR0, HR1 = 0, 64
                        hM = 64
                        full_is_halo = first_own and c == 0
                        half_is_halo = False
                    else:
                        fcol = 64 * (c + 1)
                        fblk = (c + 1) // 2
                        hlo = 64 * c - 64
                        hblk = (c - 1) // 2
                        HR0, HR1 = 64, 128
                        hM = 128
                        full_is_halo = False
                        half_is_halo = first_own and c == 1
                    P.op("pe", lambda e: e.matmul(
                        bS[:, 0:256], KB[R0:R1, l, pr, fcol:fcol + 128], qr, start=True, stop=True), qk_ready + fS)
                    ps = P.op("pe", lambda e: e.matmul(
                        bS[0:hM, 256:512], KB[R0:R1, l, pr, hlo:hlo + hM], qr, start=True, stop=True), [], signal=True)
                    bf = FLAGS[:, 0:1] if full_is_halo else ZER[:, 0:1]
                    bh = FLAGS[HR0:HR1, 0:1] if half_is_halo else ZER[HR0:HR1, 0:1]
                    P.op("act", lambda e: e.activation(
                        out=PT[:, r, 0:256], in_=bS[:, 0:256], func=AF.Exp, bias=bf, scale=0.125),
                        [ps, LD] + t_free["PT"][r])
                    ap_ = P.op("act", lambda e: e.activation(
                        out=PT[HR0:HR1, r, 256:512], in_=bS[HR0:HR1, 256:512], func=AF.Exp, bias=bh, scale=0.125),
                        [], signal=True)
                    BK.release(iS, [ap_])
                    return dict(ap=ap_, r=r, fblk=fblk, hblk=hblk, R0=R0, R1=R1, pr=pr, k=k, q0=q0)

                def stage_b(cx):
                    r, fblk, hblk, R0, R1, pr, k, q0 = (cx[x] for x in ("r", "fblk", "hblk", "R0", "R1", "pr", "k", "q0"))
                    iO, bO, fO = BK.get()
                    P.op("pe", lambda e: e.matmul(
                        bO[:, 0:256], VB[:, l, fblk, pr * 128:(pr + 1) * 128], PT[:, r, 0:256],
                        start=True, stop=False), [cx["ap"]] + fO + v_ready)
                    P.op("pe", lambda e: e.matmul(
                        bO[:, 0:256], VB[:, l, hblk, pr * 128:(pr + 1) * 128], PT[:, r, 256:512],
                        start=False, stop=True))
                    P.op("pe", lambda e: e.matmul(
                        bO[:, 256:512], ONESB[:, :], PT[:, r, 0:256], start=True, stop=False))
                    P.op("pe", lambda e: e.matmul(
                        bO[:, 256:512], ONESB[:, :], PT[:, r, 256:512], start=False, stop=False))
                    po = P.op("pe", lambda e: e.matmul(
                        bO[:, 256:512], ONEHOT0[:, :], ESR[:, l, k, :], start=False, stop=True), [c_esr], signal=True)
                    t_free["PT"][r] = [po]
                    rd = rot["dt"]
                    rot["dt"] ^= 1
                    d1 = P.op("act", lambda e: e.activation(
                        out=RD[R0:R1, rd, :], in_=bO[R0:R1, 256:512], func=AF.Ln),
                        [po] + t_free["DT"][rd], signal=True)
                    d2 = P.op("act", lambda e: e.activation(
                        out=RD[R0:R1, rd, :], in_=RD[R0:R1, rd, :], func=AF.Exp, scale=-1.0), [d1], signal=True)
                    ao = ARENA[R0:R1, 4096:8192].rearrange("p (c t) -> p c t", t=TILE)[:, 4 * pr:4 * pr + 4, q0:q0 + 64]
                    d3 = P.op("dve", lambda e: e.tensor_tensor(
                        out=ao, in0=bO[R0:R1, 0:256].rearrange("p (g q) -> p g q", q=64),
                        in1=RD[R0:R1, rd, :].rearrange("p (g q) -> p g q", q=64), op=ALU.mult), [d2], signal=True)
                    t_free["DT"][rd] = [d3]
                    BK.release(iO, [d3])
                    state["ao_last"] = d3

                prev = None
                for c in range(nchunks):
                    for k in range(NKV):
                        cx = stage_a(c, k)
                        if prev is not None:
                            stage_b(prev)
                        prev = cx
                        yield
                stage_b(prev)
                if ns:
                    for k in range(NKV):
                        sample_group(k)
                        yield

            acc_info = {}

            def conv_chain(c):
                r = rot["sg"]
                rot["sg"] ^= 1
                acc = None
                for j in range(TD):
                    w_ap = WDW[:, l, c, j:j + 1]
                    if j == 0:
                        acc = P.op("dve", lambda e: e.tensor_scalar(
                            out=SG[:, r, 0:n], in0=f32(Av(c, 0, n)), scalar1=w_ap, scalar2=vec(V_BDW, c),
                            op0=ALU.mult, op1=ALU.add), a_ready + [LD] + t_free["SG"][r], signal=True)
                        if ns:
                            acc = P.op("dve", lambda e: e.tensor_scalar(
                                out=SG[:, r, n:ntot], in0=f32(AS[:, l, c, 0:ns]), scalar1=w_ap,
                                scalar2=vec(V_BDW, c), op0=ALU.mult, op1=ALU.add), [], signal=True)
                    else:
                        accp = P.op("dve", lambda e: e.scalar_tensor_tensor(
                            out=SG[:, r, 0:n], in0=f32(Av(c, j, j + n)), scalar=w_ap, in1=SG[:, r, 0:n],
                            op0=ALU.mult, op1=ALU.add), [acc], signal=True)
                        if ns:
                            accp = P.op("dve", lambda e: e.scalar_tensor_tensor(
                                out=SG[:, r, n:ntot], in0=f32(AS[:, l, c, j:j + ns]), scalar=w_ap,
                                in1=SG[:, r, n:ntot], op0=ALU.mult, op1=ALU.add), [acc], signal=True)
                        acc = accp
                acc_info[c] = (r, acc)

            def conv():
                for c in range(NCH):
                    if c + 1 < NCH:
                        conv_chain(c + 1)
                    r, acc = acc_info[c]
                    iC, bC, fC = BK.get()
                    pe_t = None
                    for g in range(NCG):
                        s = wnext(U_CONV + 4 * c + g)
                        slot, wtok = wunit(s)
                        ntap = min(8, CONVW - TD - 8 * g)
                        for t in range(ntap):
                            j = TD + 8 * g + t
                            firstu = (t == 0)
                            pe_t = P.op("pe", lambda e, slot=slot, t=t, j=j: e.matmul(
                                bC[:, 0:n], WR[:, slot, t * 128:(t + 1) * 128], Av(c, j, j + n),
                                start=(j == TD), stop=(j == CONVW - 1)),
                                ([wtok] + (a_ready + fC if j == TD else [])) if firstu else [],
                                signal=(t == ntap - 1 and not ns))
                            if ns:
                                pe_t = P.op("pe", lambda e, slot=slot, t=t, j=j: e.matmul(
                                    bC[:, n:ntot], WR[:, slot, t * 128:(t + 1) * 128], AS[:, l, c, j:j + ns],
                                    start=False, stop=(j == CONVW - 1), skip_group_check=True),
                                    [], signal=(t == ntap - 1))
                        wdone(s, pe_t)
                        yield
                    d = P.op("dve", lambda e: e.tensor_tensor(
                        out=ZCv(c, 0, ntot), in0=bC[:, 0:ntot], in1=SG[:, r, 0:ntot], op=ALU.add),
                        [pe_t, acc], signal=True)
                    t_free["SG"][r] = [d]
                    BK.release(iC, [d])
                    state["zc_tok"][c] = d

            conv_chain(0)
            state["zc_tok"] = [None] * NCH
            for _ in conv():
                pass
            conv_pe_done = state["zc_tok"][NCH - 1]

            iM, bM, fM = BK.get()
            iQ, bQ, fQ = BK.get()
            lastm = lastq = None
            for c in range(NCH):
                r = rot["sq"]
                rot["sq"] ^= 1
                zt = state["zc_tok"][c]
                a_t = P.op("act", lambda e, c=c, r=r: e.activation(
                    out=SQ[:, r, 0:ntot], in_=f32(ZCv(c, 0, ntot)), func=AF.Square), [zt] + sq_free[r], signal=True)
                lastm = P.op("pe", lambda e, c=c: e.matmul(
                    bM[:, 0:ntot], ONES[:, :], ZCv(c, 0, ntot), start=(c == 0), stop=(c == NCH - 1)),
                    [zt] + (fM if c == 0 else []), signal=(c == NCH - 1))
                lastq = P.op("pe", lambda e, c=c, r=r: e.matmul(
                    bQ[:, 0:ntot], ONES[:, :], SQ[:, r, 0:ntot], start=(c == 0), stop=(c == NCH - 1)),
                    [a_t] + (fQ if c == 0 else []), signal=True)
                sq_free[r] = [lastq]
            dm = P.op("dve", lambda e: e.tensor_scalar(
                out=MU[:, 0:ntot], in0=bM[:, 0:ntot], scalar1=1.0 / D, scalar2=None, op0=ALU.mult),
                [lastm], signal=True)
            BK.release(iM, [dm])
            dm2 = P.op("dve", lambda e: e.tensor_tensor(
                out=M2[:, 0:ntot], in0=MU[:, 0:ntot], in1=MU[:, 0:ntot], op=ALU.mult), [dm], signal=True)
            dv = P.op("dve", lambda e: e.scalar_tensor_tensor(
                out=M2[:, 0:ntot], in0=bQ[:, 0:ntot], scalar=1.0 / D, in1=M2[:, 0:ntot],
                op0=ALU.mult, op1=ALU.subtract), [lastq, dm2], signal=True)
            BK.release(iQ, [dv])
            asd = P.op("act", lambda e: e.activation(
                out=M2[:, 0:ntot], in_=M2[:, 0:ntot], func=AF.Ln, bias=EPSB[:, 0:1], scale=1.0), [dv], signal=True)
            drs = P.op("act", lambda e: e.activation(out=RS[:, 0:ntot], in_=M2[:, 0:ntot], func=AF.Exp, scale=-0.5),
                       [asd], signal=True)
            ln_toks = []

            def ln_dve():
                for c in range(NCH):
                    r = rot["t"]
                    rot["t"] ^= 1
                    d1 = P.op("dve", lambda e, c=c, r=r: e.tensor_tensor(
                        out=T1[:, r, 0:ntot], in0=f32(ZCv(c, 0, ntot)), in1=MU[:, 0:ntot], op=ALU.subtract),
                        [dm, lastm, lastq] + t_free["T1"][r], signal=True)
                    d2 = P.op("dve", lambda e, c=c, r=r: e.tensor_tensor(
                        out=ZCv(c, 0, ntot), in0=T1[:, r, 0:ntot], in1=RS[:, 0:ntot], op=ALU.mult),
                        [d1, drs], signal=True)
                    t_free["T1"][r] = [d2]
                    ln_toks.append(d2)
                    yield

            def ln_apply():
                z_tok = None
                for c in range(NCH):
                    z_tok = P.op("act", lambda e, c=c: e.activation(
                        out=ZCv(c, 0, ntot), in_=f32(ZCv(c, 0, ntot)), func=AF.Silu, bias=vec(V_LNB, c),
                        scale=vec(V_LNG, c)), [ln_toks[c], LD], signal=True)
                    state["z_tok"] = z_tok
                yield

            alive = [qk_gen(range(8)), ln_dve()]
            while alive:
                for g_ in list(alive):
                    try:
                        next(g_)
                    except StopIteration:
                        alive.remove(g_)
            for _ in qk_gen(range(8, 10)):
                pass
            for _ in ln_apply():
                pass
            z_ready = [state["z_tok"]]
            for _ in v_gen():
                pass
            qk_ready = [state["rope_last"]]
            if l == 1:
                state["rope_free"] = [state["rope_last"]]
            v_ready = [state["v_last"]]
            for _ in attention():
                pass
            ao_ready = [state["ao_last"]]

            kb_tok = P.op("act", lambda e: e.activation(
                out=KB[:, l, :, 0:128], in_=f32(KB[:, l, :, n:n + 128]), func=AF.Copy), ao_ready, signal=True)
            nblk_ = n // 128
            vb_tok = P.op("act", lambda e: e.activation(
                out=VB[:, l, 0, :], in_=VB[:, l, nblk_, :], func=AF.Copy), ao_ready, signal=True)
            state["kb_ready"][l] = vb_tok
            a_all = ARENA[:, 12288:12288 + NCH * ACH].rearrange("p (c t) -> p c t", t=ACH)
            if p == 0:
                car2 = P.op("dve", lambda e: e.tensor_scalar(
                    out=ACAR[:, l, :, :], in0=f32(a_all[:, :, n:n + 30]), scalar1=FLAGS[:, 1:2], scalar2=None,
                    op0=ALU.mult), [conv_pe_done, LD], signal=True)
            else:
                car2 = P.op("dve", lambda e: e.tensor_copy(
                    out=ACAR[:, l, :, :], in_=f32(a_all[:, :, n:n + 30])), [conv_pe_done], signal=True)

            if p == npass - 1:
                state["out_tokens"].append(P.dma("sp", lambda e: e.dma_start(
                    out=kpT[l], in_=f32(KB[:, l, :, 0:128])), fo_sem, [kb_tok]))
                state["out_tokens"].append(P.dma("sp", lambda e: e.dma_start(
                    out=vp[l], in_=VF[:, l, :]), fo_sem, [state["vf_tok"]]))
                state["out_tokens"].append(P.dma("sp", lambda e: e.dma_start(
                    out=cpT[l], in_=f32(ACAR[:, l, :, :])), fo_sem, [car2]))
            if ns:
                state["out_tokens"].append(P.dma("sp", lambda e: e.dma_start(
                    out=ksT[l], in_=f32(KS[:, l, :, :])), fo_sem, qk_ready))
                state["out_tokens"].append(P.dma("sp", lambda e: e.dma_start(
                    out=vs[l], in_=VSF[0:ns, l, :]), fo_sem, [state["vsf_tok"]]))
                state["out_tokens"].append(P.dma("sp", lambda e: e.dma_start(
                    out=csT[l], in_=f32(AS[:, l, :, ns:ns + 30])), fo_sem, a_ready))

            mg_tok = None
            for j in range(NCH):
                lM = U_MERGE + 4 * j
                iA, bA, fA = BK.get()
                iC, bC, fC = BK.get()
                iG, bG, fG = BK.get()
                iH, bH, fH = BK.get()
                pA = proj_group(wnext(lM), ntot, lambda kc: AOv(kc, 0, ntot), 8, ao_ready, bA, fA)
                pC = proj_group(wnext(lM + 1), ntot, lambda kc: ZCv(kc, 0, ntot), 8, z_ready, bC, fC)
                pG = proj_group(wnext(lM + 2), ntot, hch, 8, h_ready, bG, fG)
                pH = proj_group(wnext(lM + 3), ntot, hch, 8, h_ready, bH, fH)
                r = rot["sg"]
                rot["sg"] ^= 1
                r2 = rot["t"]
                rot["t"] ^= 1
                a1 = P.op("act", lambda e, bG=bG, r=r: e.activation(
                    out=SG[:, r, 0:ntot], in_=bG[:, 0:ntot], func=AF.Sigmoid), [pG] + t_free["SG"][r], signal=True)
                a2 = P.op("act", lambda e, bH=bH, r2=r2: e.activation(
                    out=T2[:, r2, 0:ntot], in_=bH[:, 0:ntot], func=AF.Sigmoid), [pH] + t_free["T2"][r2], signal=True)
                BK.release(iG, [a1])
                BK.release(iH, [a2])
                d1 = P.op("dve", lambda e, bA=bA, r=r: e.tensor_tensor(
                    out=SG[:, r, 0:ntot], in0=bA[:, 0:ntot], in1=SG[:, r, 0:ntot], op=ALU.mult), [pA, a1], signal=True)
                d2 = P.op("dve", lambda e, bC=bC, r2=r2: e.tensor_tensor(
                    out=T2[:, r2, 0:ntot], in0=bC[:, 0:ntot], in1=T2[:, r2, 0:ntot], op=ALU.mult), [pC, a2], signal=True)
                BK.release(iA, [d1])
                BK.release(iC, [d2])
                mg_tok = P.op("dve", lambda e, j=j, r=r, r2=r2: e.tensor_tensor(
                    out=Qv(j, 0, ntot), in0=SG[:, r, 0:ntot], in1=T2[:, r2, 0:ntot], op=ALU.add),
                    [d1, d2] + ao_ready, signal=True)
                t_free["SG"][r] = [mg_tok]
                t_free["T2"][r2] = [mg_tok]
            mg_ready = [mg_tok]

            st2 = Stats(ntot)
            xt = [None] * NCH
            for j in range(NCH):
                s = wnext(U_OUT + j)
                iA, bA, fA = BK.get()
                pA = proj_group(s, ntot, lambda kc: Qv(kc, 0, ntot), 8, mg_ready, bA, fA)
                xt[j] = P.op("dve", lambda e, bA=bA, j=j: e.tensor_tensor(
                    out=X[:, j, 0:ntot], in0=bA[:, 0:ntot], in1=X[:, j, 0:ntot], op=ALU.add), [pA, h_tok], signal=True)
                BK.release(iA, [xt[j]])
                if j >= 1:
                    st2.add(X[:, j - 1, 0:ntot], [xt[j - 1]])
            st2.add(X[:, NCH - 1, 0:ntot], [xt[NCH - 1]])
            x2_ready = [xt[NCH - 1]]
            d_rs = st2.finish()

            h2_tok = None
            h2_toks = []
            for c in range(NCH):
                h2_tok = P.op("dve", lambda e, c=c: e.scalar_tensor_tensor(
                    out=H[:, c, 0:ntot], in0=X[:, c, 0:ntot], scalar=vec(V_NMLP, c), in1=RS[:, 0:ntot],
                    op0=ALU.mult, op1=ALU.mult), [d_rs] + x2_ready, signal=True)
                h2_toks.append(h2_tok)
            h2_ready = [h2_tok]

            hid_tok = None
            for i in range(32):
                s = wnext(U_UP + i)
                iA, bA, fA = BK.get()
                pA = proj_group(s, ntot, hch, 8, (lambda kc: [h2_toks[kc]]) if i == 0 else h2_ready, bA, fA)
                r = rot["sg"]
                rot["sg"] ^= 1
                a1 = P.op("act", lambda e, bA=bA, r=r: e.activation(
                    out=SG[:, r, 0:ntot], in_=bA[:, 0:ntot], func=AF.Relu), [pA] + t_free["SG"][r], signal=True)
                BK.release(iA, [a1])
                hid_tok = P.op("dve", lambda e, i=i, r=r: e.tensor_tensor(
                    out=HIDv(i, 0, ntot), in0=SG[:, r, 0:ntot], in1=SG[:, r, 0:ntot], op=ALU.mult),
                    [a1, car2, kb_tok, vb_tok], signal=True)
                t_free["SG"][r] = [hid_tok]
            hid_ready = [hid_tok]

            st3 = Stats(ntot)
            xt = [None] * NCH
            for j in range(NCH):
                iA, bA, fA = BK.get()
                pA = None
                for g in range(4):
                    s = wnext(U_DOWN + 4 * j + g)
                    pA = proj_group(s, ntot, lambda kc, g=g: HIDv(8 * g + kc, 0, ntot), 8, hid_ready, bA,
                                    fA if g == 0 else [], start=(g == 0), stop=(g == 3))
                xt[j] = P.op("dve", lambda e, bA=bA, j=j: e.tensor_tensor(
                    out=X[:, j, 0:ntot], in0=bA[:, 0:ntot], in1=X[:, j, 0:ntot], op=ALU.add), [pA, h2_tok], signal=True)
                BK.release(iA, [xt[j]])
                if j >= 1:
                    st3.add(X[:, j - 1, 0:ntot], [xt[j - 1]])
            st3.add(X[:, NCH - 1, 0:ntot], [xt[NCH - 1]])
            return [xt[NCH - 1]], st3.finish()

        for p in range(npass):
            if p == 0:
                n, ns = HALO, NSMP
                tok0, rcol0 = 0, 0
            else:
                n, ns = TILE, 0
                tok0 = HALO + TILE * (p - 1)
                rcol0 = HALO + NSMP + TILE * (p - 1)
            ntot = n + ns
            xsrc = xT.rearrange("(c q) t -> q c t", q=128)[:, :, tok0:tok0 + n]
            xt = P.dma("sp", lambda e, xsrc=xsrc, n=n: e.dma_start(out=X[:, :, 0:n], in_=xsrc), x_sem, state["x_free"])
            x_ready = [xt]
            if ns:
                xs_src = xsT.rearrange("(c q) t -> q c t", q=128)
                xt2 = P.dma("sp", lambda e, xs_src=xs_src, n=n, ntot=ntot: e.dma_start(
                    out=X[:, :, n:ntot], in_=xs_src), x_sem, state["x_free"])
                x_ready = [xt2]
            rt1 = P.dma("sp", lambda e, rcol0=rcol0, ntot=ntot: e.dma_start(
                out=RC[:, 0:ntot], in_=ropec_d[:, rcol0:rcol0 + ntot]), r_sem, state["rope_free"])
            rt2 = P.dma("sp", lambda e, rcol0=rcol0, ntot=ntot: e.dma_start(
                out=RSN[:, 0:ntot], in_=ropes_d[:, rcol0:rcol0 + ntot]), r_sem, state["rope_free"])
            rope_ready = [rt2]

            xr = x_ready
            d_rs = None
            for l in range(2):
                xr, d_rs = layer(p, l, n, ns, xr, rope_ready, d_rs)

            y_tok = None
            for c in range(NCH):
                y_tok = P.op("dve", lambda e, c=c, ntot=ntot: e.scalar_tensor_tensor(
                    out=X[:, c, 0:ntot], in0=X[:, c, 0:ntot], scalar=vec(80, c), in1=RS[:, 0:ntot],
                    op0=ALU.mult, op1=ALU.mult), [d_rs] + xr, signal=True)
            if p == 0:
                st = P.dma("sp", lambda e, n=n, ntot=ntot: e.dma_start(
                    out=ysT.rearrange("(c q) t -> q c t", q=128), in_=X[:, :, n:ntot]), o_sem, [y_tok])
            else:
                o0 = TILE * (p - 1)
                st = P.dma("sp", lambda e, o0=o0, n=n: e.dma_start(
                    out=ypT.rearrange("(c q) t -> q c t", q=128)[:, :, o0:o0 + n], in_=X[:, :, 0:n]),
                    o_sem, [y_tok])
            state["x_free"] = [st]

        finals = [(o_sem[0], o_sem[1]), (fo_sem[0], fo_sem[1])]

        with nc.Block() as block:
            @block.tensor
            def _(e):
                P.run("pe", e)

            @block.scalar
            def _(e):
                P.run("act", e)

            @block.vector
            def _(e):
                P.run("dve", e)

            @block.gpsimd
            def _(e):
                P.run("pool", e)

            @block.sync
            def _(e):
                P.run("sp", e, final_waits=finals)
    return nc, order


def _units_for_layer(w_in, w_o, w_dw, w_pw2, w_out, w_up, w_down):
    U = np.zeros((NU, 128, 1024), np.float32)

    def put(u, mat):
        U[u] = mat.reshape(8, 128, 128).transpose(1, 0, 2).reshape(128, 1024)

    m = np.arange(128)
    for cc in range(8):
        heads = np.array([q_head(cc, 0)] * 64 + [q_head(cc, 1)] * 64)
        dd = m % 64
        put(U_Q + 2 * cc, w_in[:, heads * 64 + dd])
        put(U_Q + 2 * cc + 1, w_in[:, heads * 64 + (dd + 32) % 64])
    for cc in range(2):
        base = 1024 + cc * 128
        put(U_K + 2 * cc, w_in[:, base + m])
        put(U_K + 2 * cc + 1, w_in[:, base + (m // 64) * 64 + (m % 64 + 32) % 64])
    for cc in range(2):
        put(U_V + cc, w_in[:, 1280 + cc * 128 + m])
    for c in range(8):
        put(U_U + 2 * c, w_in[:, 1536 + c * 128 + m])
        put(U_U + 2 * c + 1, w_in[:, 2560 + c * 128 + m])
    for c in range(8):
        for g in range(4):
            blk = np.zeros((128, 8, 128), np.float32)
            for t in range(8):
                j = TD + 8 * g + t
                if j < CONVW:
                    blk[m, t, m] = w_dw[j, c * 128 + m]
            U[U_CONV + 4 * c + g] = blk.reshape(128, 1024)
    rows = np.zeros(1024, np.int64)
    for cc in range(8):
        for half in range(2):
            rows[cc * 128 + half * 64: cc * 128 + half * 64 + 64] = q_head(cc, half) * 64 + np.arange(64)
    w_o_p = w_o[rows, :]
    for j in range(8):
        put(U_MERGE + 4 * j, w_o_p[:, j * 128 + m])
        put(U_MERGE + 4 * j + 1, w_pw2[:, j * 128 + m])
        put(U_MERGE + 4 * j + 2, w_in[:, 3584 + j * 128 + m])
        put(U_MERGE + 4 * j + 3, w_in[:, 4608 + j * 128 + m])
        put(U_OUT + j, w_out[:, j * 128 + m])
    for i in range(32):
        put(U_UP + i, w_up[:, i * 128 + m])
    for j in range(8):
        for g in range(4):
            put(U_DOWN + 4 * j + g, w_down[g * 1024:(g + 1) * 1024, j * 128 + m])
    return U


def _vec_cols(v):
    return np.ascontiguousarray(v.reshape(8, 128).T)


def _rope_tables(pos):
    half = HD // 2
    inv_freq = (np.float32(10000.0) ** (-np.arange(half, dtype=np.float32) / np.float32(half))).astype(np.float32)
    ang = pos.astype(np.float32)[None, :] * inv_freq[:, None]
    cos = np.cos(ang).astype(np.float32)
    sin = np.sin(ang).astype(np.float32)
    c64 = np.concatenate([cos, cos], 0)
    s64 = np.concatenate([-sin, sin], 0)
    return np.concatenate([c64, c64], 0), np.concatenate([s64, s64], 0)


_CACHE = {}


def kernel(x_prompt, x_sample, cache_k, cache_v, state_conv, norm_mix, w_in, sinks, w_o_attn,
           w_dw, b_dw, ln_conv_g, ln_conv_b, w_pw2, w_out, norm_mlp, w_up, w_down, norm_final):
    f = lambda a: np.asarray(a, dtype=np.float32)
    x_prompt, x_sample, cache_k, cache_v, state_conv = map(f, (x_prompt, x_sample, cache_k, cache_v, state_conv))
    norm_mix, w_in, sinks, w_o_attn, w_dw, b_dw = map(f, (norm_mix, w_in, sinks, w_o_attn, w_dw, b_dw))
    ln_conv_g, ln_conv_b, w_pw2, w_out, norm_mlp, w_up, w_down, norm_final = map(
        f, (ln_conv_g, ln_conv_b, w_pw2, w_out, norm_mlp, w_up, w_down, norm_final))
    B, SEQ, _ = x_prompt.shape
    own = SEQ // 2
    n_own = own // TILE
    ncores = 8
    assert B * 2 == ncores and x_sample.shape[0] == ncores
    if n_own not in _CACHE:
        _CACHE[n_own] = build_program(n_own)
    nc, order = _CACHE[n_own]

    assert len(order) == NUS and len(set(order)) == NUS
    wst = np.stack([_units_for_layer(w_in[l], w_o_attn[l], w_dw[l], w_pw2[l], w_out[l], w_up[l], w_down[l])[order]
                    for l in range(2)], 0)
    vecs = np.zeros((128, NVEC), np.float32)
    for l in range(2):
        for i, v in enumerate((norm_mix[l], b_dw[l], ln_conv_g[l], ln_conv_b[l], norm_mlp[l])):
            vecs[:, 40 * l + 8 * i: 40 * l + 8 * i + 8] = _vec_cols(v)
    vecs[:, 80:88] = _vec_cols(norm_final)
    wdw = np.ascontiguousarray(w_dw.reshape(2, CONVW, NCH, 128).transpose(3, 0, 2, 1)[:, :, :, :TD])
    sinks_b = np.ascontiguousarray(np.broadcast_to(sinks.reshape(1, 32), (128, 32))).astype(np.float32)

    mm_ = np.arange(128)
    perm = np.zeros((128, 128), np.float32)
    perm[(mm_ // 64) * 64 + (mm_ % 64 + 32) % 64, mm_] = 1.0
    in_maps = []
    for core in range(ncores):
        b, hf = core // 2, core % 2
        start = hf * own
        xT = np.zeros((D, HALO + own), np.float32)
        if hf == 1:
            xT[:, :] = x_prompt[b, start - HALO:start + own, :].T
        else:
            xT[:, HALO:] = x_prompt[b, 0:own, :].T
        pos = np.concatenate([np.arange(start - HALO, start), PAST + np.arange(NSMP), np.arange(start, start + own)])
        rc, rs = _rope_tables(pos.astype(np.float32))
        flags = np.zeros((128, 4), np.float32)
        flags[0, 2] = 1.0
        flags[:, 0] = 0.0 if hf == 1 else -30000.0
        flags[:, 1] = 1.0 if hf == 1 else 0.0
        in_maps.append({
            "xT": xT,
            "xsT": np.ascontiguousarray(x_sample[core].T),
            "wst": wst,
            "vecs": vecs,
            "wdw": wdw,
            "ropec": rc, "ropes": rs,
            "kcT": np.ascontiguousarray(cache_k[:, core].reshape(2, 128, 256).transpose(0, 2, 1)
                                        .reshape(2, 2, 128, 128).transpose(0, 2, 1, 3)),
            "vc": np.ascontiguousarray(cache_v[:, core].reshape(2, 128, 256)),
            "scT": np.ascontiguousarray(state_conv[:, core].transpose(0, 2, 1).reshape(2, 8, 128, 30)
                                        .transpose(0, 2, 1, 3)),
            "sinks": sinks_b,
            "perm": perm,
            "flags": flags,
        })
    res = run_bass_kernel_spmd(nc, in_maps, core_ids=list(range(ncores)))
    R = res.results

    y_prompt = np.zeros((B, SEQ, D), np.float32)
    y_sample = np.zeros((ncores, NSMP, D), np.float32)
    k_p = np.zeros((2, B, 128, NKV, HD), np.float32)
    v_p = np.zeros((2, B, 128, NKV, HD), np.float32)
    c_p = np.zeros((2, B, 30, D), np.float32)
    k_s = np.zeros((2, ncores, NSMP, NKV, HD), np.float32)
    v_s = np.zeros((2, ncores, NSMP, NKV, HD), np.float32)
    c_s = np.zeros((2, ncores, 30, D), np.float32)
    for core in range(ncores):
        b, hf = core // 2, core % 2
        r = R[core]
        y_prompt[b, hf * own:(hf + 1) * own, :] = r["ypT"].T
        y_sample[core] = r["ysT"].T
        k_s[:, core] = r["ksT"].transpose(0, 3, 2, 1).reshape(2, NSMP, NKV, HD)
        v_s[:, core] = r["vs"].reshape(2, NSMP, NKV, HD)
        c_s[:, core] = r["csT"].transpose(0, 3, 2, 1).reshape(2, 30, D)
        if hf == 1:
            k_p[:, b] = r["kpT"].transpose(0, 3, 2, 1).reshape(2, 128, NKV, HD)
            v_p[:, b] = r["vp"].reshape(2, 128, NKV, HD)
            c_p[:, b] = r["cpT"].transpose(0, 3, 2, 1).reshape(2, 30, D)
    return (y_prompt, y_sample, k_p, v_p, c_p, k_s, v_s, c_s)
```

```python
import contextlib
import numpy as np
import concourse.bass as bass
import concourse.mybir as mybir
from concourse.bass_utils import run_bass_kernel_spmd

F32 = mybir.dt.float32
F32R = mybir.dt.float32r
BF16 = mybir.dt.bfloat16
AF = mybir.ActivationFunctionType
ALU = mybir.AluOpType

D = 1024
NCH = 8
HD = 64
NQH = 16
NKV = 4
CONVW = 31
DFF = 4096
NIN = 5632
EPS = 1e-6
HALO = 256
TILE = 512
NSMP = 16
PAST = 2048
NU = 174
TD = 7
NCG = 3
NUS = 156
U_Q, U_K, U_V, U_U, U_CONV, U_MERGE, U_OUT, U_UP, U_DOWN = 0, 16, 20, 22, 38, 70, 102, 110, 142
NS = 10
ACH = 544
NVEC = 88
N_OWN_TILES = 8


def q_head(cc, half):
    return (cc + 4 * half) if cc < 4 else (8 + (cc - 4) + 4 * half)


class _Rec:
    def __init__(self):
        self.call = None

    def __getattr__(self, name):
        def f(*a, **k):
            self.call = (name, a, k)
            return None
        return f


def _record(fn):
    r = _Rec()
    fn(r)
    assert r.call is not None
    return r.call


class Prog:
    ENGS = ("pe", "act", "dve", "pool", "sp")

    def __init__(self, nc, es):
        self.nc = nc
        self.q = {e: [] for e in self.ENGS}
        self.sem = {e: es.enter_context(nc.semaphore("prog_" + e)) for e in self.ENGS}
        self.cnt = {e: 0 for e in self.ENGS}

    def op(self, eng, fn, waits=(), signal=False):
        tok = None
        if signal:
            self.cnt[eng] += 1
            tok = (self.sem[eng], self.cnt[eng])
        ws = []
        for w in waits:
            if w is None:
                continue
            if isinstance(w, list):
                ws.extend([x for x in w if x is not None])
            else:
                ws.append(w)
        self.q[eng].append((_record(fn), tuple(ws), tok, 1))
        return tok

    def dma(self, eng, fn, dsem, waits=()):
        dsem[1] += 16
        tok = (dsem[0], dsem[1])
        ws = []
        for w in waits:
            if w is None:
                continue
            if isinstance(w, list):
                ws.extend([x for x in w if x is not None])
            else:
                ws.append(w)
        self.q[eng].append((_record(fn), tuple(ws), tok, 16))
        return tok

    def run(self, eng, e, final_waits=()):
        seen = {}
        for (mname, margs, mkw), waits, tok, inc in self.q[eng]:
            for (sem, val) in waits:
                k = sem.num
                if seen.get(k, 0) >= val:
                    continue
                e.wait_ge(sem, val)
                seen[k] = val
            ins = getattr(e, mname)(*margs, **mkw)
            if tok is not None:
                ins.then_inc(tok[0], inc)
        for (sem, val) in final_waits:
            e.wait_ge(sem, val)


class Banks:
    def __init__(self, tensors):
        self.t = tensors
        self.free = [[] for _ in tensors]
        self.held = [False] * len(tensors)
        self.i = 0

    def get(self):
        n = len(self.t)
        for _ in range(n):
            i = self.i
            self.i = (i + 1) % n
            if not self.held[i]:
                self.held[i] = True
                fr = self.free[i]
                self.free[i] = []
                return i, self.t[i], fr
        raise RuntimeError("no free psum bank")

    def release(self, i, toks):
        self.held[i] = False
        self.free[i] = [t for t in toks if t is not None]


def build_program(n_own):
    npass = 1 + n_own
    ntok = HALO + TILE * n_own
    nc = bass.Bass("TRN2", target_bir_lowering=False)

    def din(name, shape):
        return nc.dram_tensor(name, list(shape), F32, kind="ExternalInput").ap()

    def dout(name, shape):
        return nc.dram_tensor(name, list(shape), F32, kind="ExternalOutput").ap()

    xT = din("xT", [D, ntok])
    xsT = din("xsT", [D, NSMP])
    wst = din("wst", [2, NUS, 128, 1024])
    perm_d = din("perm", [128, 128])
    vecs_d = din("vecs", [128, NVEC])
    wdw_d = din("wdw", [128, 2, NCH, TD])
    ropec_d = din("ropec", [128, ntok + NSMP])
    ropes_d = din("ropes", [128, ntok + NSMP])
    kcT_d = din("kcT", [2, 128, 2, 128])
    vc_d = din("vc", [2, 128, 256])
    scT_d = din("scT", [2, 128, 8, 30])
    sinks_d = din("sinks", [128, 32])
    flags_d = din("flags", [128, 4])

    ypT = dout("ypT", [D, TILE * n_own])
    ysT = dout("ysT", [D, NSMP])
    kpT = dout("kpT", [2, 128, 2, 128])
    vp = dout("vp", [2, 128, 256])
    cpT = dout("cpT", [2, 128, 8, 30])
    ksT = dout("ksT", [2, 128, 2, NSMP])
    vs = dout("vs", [2, NSMP, 256])
    csT = dout("csT", [2, 128, 8, 30])

    es = contextlib.ExitStack()
    with es:
        def sb(name, shape, dt=F32):
            return es.enter_context(nc.sbuf_tensor(name, list(shape), dt))

        X = sb("X", [128, NCH, TILE])
        H = sb("H", [128, NCH, TILE], F32R)
        ARENA = sb("ARENA", [128, 12288 + NCH * ACH], F32R)
        KB = sb("KB", [128, 2, 2, 128 + TILE], F32R)
        VB = sb("VB", [128, 2, 5, 256], BF16)
        VF = sb("VF", [128, 2, 256])
        VSF = sb("VSF", [128, 2, 256])
        ACAR = sb("ACAR", [128, 2, NCH, 30], F32R)
        AS = sb("AS", [128, 2, NCH, 30 + NSMP], F32R)
        KC = sb("KC", [128, 2, 2, 128], F32R)
        VC = sb("VC", [128, 2, 256], BF16)
        KS = sb("KS", [128, 2, 2, NSMP], F32R)
        VS = sb("VS", [128, 2, 256], BF16)
        PT = sb("PT", [128, 4, 512], BF16)
        T1 = sb("T1", [128, 2, TILE])
        T2 = sb("T2", [128, 2, TILE])
        SG = sb("SG", [128, 2, TILE])
        SQ = sb("SQ", [128, 2, TILE], F32R)
        RS = sb("RS", [128, TILE])
        MU = sb("MU", [128, TILE])
        M2 = sb("M2", [128, TILE])
        RD = sb("RD", [128, 2, 256])
        RC = sb("RC", [128, TILE])
        RSN = sb("RSN", [128, TILE])
        ESK = sb("ESK", [128, 32])
        ESR = sb("ESR", [128, 2, 4, 256], BF16)
        ONES = sb("ONES", [128, 128], F32R)
        ONEHOT0 = sb("ONEHOT0", [128, 128], BF16)
        ONESB = sb("ONESB", [128, 128], BF16)
        PERM = sb("PERM", [128, 128], F32R)
        ZER = sb("ZER", [128, 64])
        VECS = sb("VECS", [128, NVEC])
        WDW = sb("WDW", [128, 2, NCH, TD])
        FLAGS = sb("FLAGS", [128, 4])
        EPSB = sb("EPSB", [128, 1])
        WR = sb("WR", [128, NS, 1024], F32R)

        banks_t = [es.enter_context(nc.psum_tensor("bank%d" % i, [128, 512], F32)) for i in range(8)]
        BK = Banks(banks_t)

        P = Prog(nc, es)

        def newsem(name):
            return [es.enter_context(nc.semaphore(name)), 0]

        wsem = [newsem("wsem%d" % i) for i in range(NS // 2)]
        ld_sem = newsem("ld")
        ldp_sem = newsem("ldp")
        x_sem = newsem("xs")
        r_sem = newsem("rs")
        o_sem = newsem("os")
        fo_sem = newsem("fo")

        def Qv(c, a, b):
            return ARENA[:, c * TILE + a: c * TILE + b]

        def AOv(c, a, b):
            return ARENA[:, 4096 + c * TILE + a: 4096 + c * TILE + b]

        def ZCv(c, a, b):
            return ARENA[:, 8192 + c * TILE + a: 8192 + c * TILE + b]

        def Av(c, a, b):
            return ARENA[:, 12288 + c * ACH + a: 12288 + c * ACH + b]

        def HIDv(i, a, b):
            return ARENA[:, i * TILE + a: i * TILE + b]

        def f32(ap):
            return ap.bitcast(F32)

        total_units = npass * 2 * NUS

        def issue_wdma(s0, waits):
            rem = s0 % (2 * NUS)
            l, u = rem // NUS, rem % NUS
            slot = s0 % NS
            pair = slot // 2
            src = wst[l, u:u + 2].rearrange("u p f -> p u f")
            dst = WR[:, slot:slot + 2, :]
            P.dma("pool", lambda e, dst=dst, src=src: e.dma_start(out=dst, in_=src), wsem[pair], waits)

        def wunit(s):
            slot = s % NS
            pair = slot // 2
            fill = s // NS + 1
            return slot, (wsem[pair][0], 16 * fill)

        def wdone(s, tok):
            if s % 2 == 1:
                nxt = s - 1 + NS
                if nxt < total_units:
                    issue_wdma(nxt, [tok])

        order = []
        wcount = [0]

        def wnext(logical):
            s_ = wcount[0]
            wcount[0] += 1
            if s_ < NUS:
                order.append(logical)
            else:
                assert order[s_ % NUS] == logical, (s_, logical)
            return s_

        P.dma("sp", lambda e: e.dma_start(out=VECS[:, :], in_=vecs_d[:, :]), ld_sem)
        P.dma("sp", lambda e: e.dma_start(out=FLAGS[:, :], in_=flags_d[:, :]), ld_sem)
        P.dma("sp", lambda e: e.dma_start(out=ESK[:, :], in_=sinks_d[:, :]), ld_sem)
        P.dma("sp", lambda e: e.dma_start(out=WDW[:, :, :, :], in_=wdw_d[:, :, :, :]), ld_sem)
        LD = (ld_sem[0], ld_sem[1])
        P.dma("pool", lambda e: e.dma_start(out=KC[:, :, :, :], in_=kcT_d.rearrange("l p c t -> p l c t")), ldp_sem)
        P.dma("pool", lambda e: e.dma_start(out=VC[:, :, :], in_=vc_d.rearrange("l p f -> p l f")), ldp_sem)
        for l in range(2):
            P.dma("pool", lambda e, l=l: e.dma_start(out=AS[:, l, :, 0:30], in_=scT_d[l]), ldp_sem)
        P.dma("pool", lambda e: e.dma_start(out=PERM[:, :], in_=perm_d[:, :]), ldp_sem)
        LDP = (ldp_sem[0], ldp_sem[1])
        for s0 in range(0, NS, 2):
            issue_wdma(s0, [])

        P.op("dve", lambda e: e.memset(T1[:, :, :], 0.0))
        P.op("dve", lambda e: e.memset(ZER[:, :], 0.0))
        P.op("dve", lambda e: e.memset(EPSB[:, :], EPS))
        z_src = T1[:, :, :].rearrange("p a b -> p (a b)")
        c_ones = P.op("dve", lambda e: e.tensor_scalar(out=ONES[:, :], in0=z_src[:, 0:128], scalar1=1.0,
                                                        scalar2=None, op0=ALU.add), signal=True)

        P.op("dve", lambda e: e.tensor_scalar(out=ONESB[:, :], in0=z_src[:, 0:128], scalar1=1.0,
                                              scalar2=None, op0=ALU.add))
        P.op("dve", lambda e: e.tensor_scalar(out=ONEHOT0[:, :], in0=z_src[:, 0:128], scalar1=FLAGS[:, 2:3],
                                              scalar2=None, op0=ALU.add), [LD])

        def zero_fill(flat, ncols):
            tok = None
            for c0 in range(0, ncols, 1024):
                c1 = min(ncols, c0 + 1024)
                tok = P.op("dve", lambda e, c0=c0, c1=c1: e.tensor_copy(out=flat[:, c0:c1], in_=z_src[:, 0:c1 - c0]),
                           signal=True)
            return tok

        zero_fill(KB[:, :, :, :].rearrange("p a b c -> p (a b c)"), 2 * 2 * (128 + TILE))
        zero_fill(VB[:, :, :, :].rearrange("p a b c -> p (a b c)"), 2 * 5 * 256)
        zero_fill(ACAR[:, :, :, :].rearrange("p a b c -> p (a b c)"), 2 * NCH * 30)
        zero_fill(PT[:, :, :].rearrange("p a b -> p (a b)"), 2048)
        c_init = zero_fill(VS[:, :, :].rearrange("p a b -> p (a b)"), 512)
        c_esk = P.op("act", lambda e: e.activation(out=ESK[:, :], in_=ESK[:, :], func=AF.Exp), [LD], signal=True)
        c_esr = None
        for l in range(2):
            for k in range(NKV):
                for g in range(4):
                    i = l * 16 + 4 * k + g
                    c_esr = P.op("dve", lambda e, l=l, k=k, g=g, i=i: e.tensor_scalar(
                        out=ESR[:, l, k, g * 64:(g + 1) * 64], in0=ZER[:, :], scalar1=ESK[:, i:i + 1], scalar2=None,
                        op0=ALU.add), [c_esk], signal=True)

        state = {
            "x_free": [],
            "rope_free": [],
            "y_store": None,
            "kb_ready": [None, None],
            "out_tokens": [],
        }
        sq_free = [[], []]
        t_free = {"T1": [[], []], "T2": [[], []], "SG": [[], []], "PT": [[], [], [], []], "DT": [[], []]}
        rot = {"sq": 0, "t": 0, "sg": 0, "pt": 0, "dt": 0}

        def vec(base, c):
            return VECS[:, base + c: base + c + 1]

        class Stats:
            def __init__(self, ntot):
                self.ntot = ntot
                self.bi, self.bank, self.bfree = BK.get()
                self.n = 0
                self.last = None

            def add(self, src_ap, waits):
                ntot, bank, c = self.ntot, self.bank, self.n
                r = rot["sq"]
                rot["sq"] ^= 1
                a_t = P.op("act", lambda e: e.activation(out=SQ[:, r, 0:ntot], in_=src_ap, func=AF.Square),
                           list(waits) + sq_free[r], signal=True)
                self.last = P.op("pe", lambda e: e.matmul(
                    bank[:, 0:ntot], ONES[:, :], SQ[:, r, 0:ntot], start=(c == 0), stop=(c == NCH - 1)),
                    [a_t, c_ones] + (self.bfree if c == 0 else []), signal=True)
                sq_free[r] = [self.last]
                self.n += 1

            def finish(self):
                ntot, bank = self.ntot, self.bank
                assert self.n == NCH
                a2 = P.op("act", lambda e: e.activation(out=MU[:, 0:ntot], in_=bank[:, 0:ntot], func=AF.Ln,
                                                         bias=EPSB[:, 0:1], scale=1.0 / D), [self.last], signal=True)
                BK.release(self.bi, [a2])
                return P.op("act", lambda e: e.activation(out=RS[:, 0:ntot], in_=MU[:, 0:ntot], func=AF.Exp,
                                                          scale=-0.5), [a2], signal=True)

        def rms_stats(src_chunk, ntot, src_waits):
            st = Stats(ntot)
            for c in range(NCH):
                st.add(src_chunk(c), src_waits)
            return st.finish()

        def proj_group(s, ntot, rhs_chunk, nk, rhs_waits, bank, bfree, start=True, stop=True, col0=0):
            slot, wtok = wunit(s)
            last = None
            for kc in range(nk):
                first = (kc == 0)
                last = P.op("pe", lambda e, kc=kc, slot=slot, first=first: e.matmul(
                    bank[:, col0:col0 + ntot], WR[:, slot, kc * 128:(kc + 1) * 128], rhs_chunk(kc),
                    start=(start and first), stop=(stop and kc == nk - 1)),
                    (rhs_waits(kc) + ([wtok] + bfree if first else [])) if callable(rhs_waits)
                    else (([wtok] + rhs_waits + bfree) if first else []), signal=(kc == nk - 1))
            wdone(s, last)
            return last

        def layer(p, l, n, ns, x_ready, rope_ready, d_rs_in):
            ntot = n + ns
            vb = 40 * l
            V_NM, V_BDW, V_LNG, V_LNB, V_NMLP, V_NF = vb, vb + 8, vb + 16, vb + 24, vb + 32, 80
            nchunks = n // 64
            first_own = (p == 1)

            d_rs = d_rs_in if d_rs_in is not None else rms_stats(lambda c: X[:, c, 0:ntot], ntot, x_ready)
            h_tok = None
            h_toks = []
            for c in range(NCH):
                h_tok = P.op("dve", lambda e, c=c: e.scalar_tensor_tensor(
                    out=H[:, c, 0:ntot], in0=X[:, c, 0:ntot], scalar=vec(V_NM, c), in1=RS[:, 0:ntot],
                    op0=ALU.mult, op1=ALU.mult), [d_rs, LD] + x_ready, signal=True)
                h_toks.append(h_tok)
            h_ready = [h_tok]
            h_first = [True]

            def h_waits(kc):
                return [h_toks[kc]]

            def hch(kc):
                return H[:, kc, 0:ntot]

            car_tok = P.op("act", lambda e: e.activation(
                out=ARENA[:, 12288:12288 + NCH * ACH].rearrange("p (c t) -> p c t", t=ACH)[:, :, 0:30],
                in_=f32(ACAR[:, l, :, :]), func=AF.Copy), [c_init] + x_ready, signal=True)
            a_last = None
            for c in range(NCH):
                iA, bA, fA = BK.get()
                iB, bB, fB = BK.get()
                pa = proj_group(wnext(U_U + 2 * c), ntot, hch, 8, h_waits if c == 0 else h_ready, bA, fA)
                pb = proj_group(wnext(U_U + 2 * c + 1), ntot, hch, 8, h_ready, bB, fB)
                r = rot["sg"]
                rot["sg"] ^= 1
                a1 = P.op("act", lambda e, bB=bB, r=r: e.activation(
                    out=SG[:, r, 0:ntot], in_=bB[:, 0:ntot], func=AF.Sigmoid), [pb] + t_free["SG"][r], signal=True)
                BK.release(iB, [a1])
                d1 = P.op("dve", lambda e, bA=bA, r=r, c=c: e.tensor_tensor(
                    out=Av(c, 30, 30 + n), in0=bA[:, 0:n], in1=SG[:, r, 0:n], op=ALU.mult), [pa, a1], signal=True)
                if ns:
                    d1 = P.op("dve", lambda e, bA=bA, r=r, c=c: e.tensor_tensor(
                        out=AS[:, l, c, 30:30 + ns], in0=bA[:, n:ntot], in1=SG[:, r, n:ntot], op=ALU.mult),
                        [pa, a1, LDP], signal=True)
                BK.release(iA, [d1])
                t_free["SG"][r] = [d1]
                a_last = d1
            a_ready = [a_last, car_tok, LDP]

            def qk_gen(js):
                def finish(cx):
                    j, iA, bA, pa, rq, ac = cx
                    isq = j < 8
                    iB, bB, fB = BK.get()
                    pb = P.op("pe", lambda e: e.matmul(
                        bB[:, 0:ntot], PERM[:, :], SQ[:, rq, 0:ntot], start=True, stop=True),
                        [ac, LDP] + fB, signal=True)
                    sq_free[rq] = [pb]
                    r = rot["t"]
                    rot["t"] ^= 1
                    d1 = P.op("dve", lambda e: e.tensor_tensor(
                        out=T1[:, r, 0:ntot], in0=bA[:, 0:ntot], in1=RC[:, 0:ntot], op=ALU.mult),
                        [pa, ac] + rope_ready + t_free["T1"][r], signal=True)
                    d2 = P.op("dve", lambda e: e.tensor_tensor(
                        out=T2[:, r, 0:ntot], in0=bB[:, 0:ntot], in1=RSN[:, 0:ntot], op=ALU.mult),
                        [pb] + rope_ready + t_free["T2"][r], signal=True)
                    BK.release(iA, [d1, ac])
                    BK.release(iB, [d2])
                    if isq:
                        d3 = P.op("dve", lambda e: e.tensor_tensor(
                            out=Qv(j, 0, ntot), in0=T1[:, r, 0:ntot], in1=T2[:, r, 0:ntot], op=ALU.add),
                            [d1, d2, state["y_store"]], signal=True)
                    else:
                        cc = j - 8
                        d3 = P.op("dve", lambda e: e.tensor_tensor(
                            out=KB[:, l, cc, 128:128 + n], in0=T1[:, r, 0:n], in1=T2[:, r, 0:n], op=ALU.add),
                            [d1, d2, state["kb_ready"][l]], signal=True)
                        if ns:
                            d3 = P.op("dve", lambda e: e.tensor_tensor(
                                out=KS[:, l, cc, 0:ns], in0=T1[:, r, n:ntot], in1=T2[:, r, n:ntot], op=ALU.add),
                                [d1, d2], signal=True)
                    t_free["T1"][r] = [d3]
                    t_free["T2"][r] = [d3]
                    state["rope_last"] = d3

                prev = None
                for j in js:
                    lA = (U_Q + 2 * j if j < 8 else U_K + 2 * (j - 8))
                    iA, bA, fA = BK.get()
                    pa = proj_group(wnext(lA), ntot, hch, 8, h_ready, bA, fA)
                    rq = rot["sq"]
                    rot["sq"] ^= 1
                    ac = P.op("act", lambda e: e.activation(
                        out=SQ[:, rq, 0:ntot], in_=bA[:, 0:ntot], func=AF.Copy), [pa] + sq_free[rq], signal=True)
                    cur = (j, iA, bA, pa, rq, ac)
                    if prev is not None:
                        finish(prev)
                    prev = cur
                    yield
                finish(prev)

            def v_gen():
                sV = wnext(U_V)
                assert sV % 2 == 0
                assert wnext(U_V + 1) == sV + 1
                slotV, wtokV = wunit(sV)
                _, wtokV2 = wunit(sV + 1)
                v_last = None
                pe_last = None
                nblk = n // 128
                for b in range(nblk + (1 if ns else 0)):
                    iV, bV, fV = BK.get()
                    is_s = (b == nblk)
                    m0, m1 = (n, ntot) if is_s else (128 * b, 128 * b + 128)
                    mm = m1 - m0
                    for kc in range(8):
                        pe_last = P.op("pe", lambda e, kc=kc, bV=bV, m0=m0, m1=m1, mm=mm: e.matmul(
                            bV[0:mm, 0:256], H[:, kc, m0:m1], WR[:, slotV:slotV + 2, kc * 128:(kc + 1) * 128],
                            start=(kc == 0), stop=(kc == 7)),
                            ([wtokV, wtokV2] + h_ready + fV) if kc == 0 else [], signal=(kc == 7))
                    if is_s:
                        state["vsf_tok"] = P.op("act", lambda e, bV=bV, mm=mm: e.activation(
                            out=VSF[0:mm, l, :], in_=bV[0:mm, 0:256], func=AF.Copy), [pe_last], signal=True)
                        v_last = P.op("act", lambda e, bV=bV, mm=mm: e.activation(
                            out=VS[0:mm, l, :], in_=bV[0:mm, 0:256], func=AF.Copy), [pe_last, c_init], signal=True)
                    else:
                        if p == npass - 1 and b == nblk - 1:
                            state["vf_tok"] = P.op("act", lambda e, bV=bV: e.activation(
                                out=VF[:, l, :], in_=bV[:, 0:256], func=AF.Copy), [pe_last], signal=True)
                        v_last = P.op("act", lambda e, bV=bV, b=b: e.activation(
                            out=VB[:, l, 1 + b, :], in_=bV[:, 0:256], func=AF.Copy),
                            [pe_last, state["kb_ready"][l]], signal=True)
                    BK.release(iV, [v_last])
                    yield
                wdone(sV, pe_last)
                wdone(sV + 1, pe_last)
                state["v_last"] = v_last

            def attention():
                rot_pt = [0, 0]

                def sample_group(k):
                    hh = k % 2
                    pr = k // 2
                    R0, R1 = 64 * hh, 64 * hh + 64
                    iS, bS, fS = BK.get()
                    r = rot_pt[0]
                    rot_pt[0] ^= 1
                    nq = ns
                    ncol = 4 * nq
                    qr = ARENA[R0:R1, 0:4096].rearrange("p (c t) -> p c t", t=TILE)[:, 4 * pr:4 * pr + 4, n:ntot]
                    P.op("pe", lambda e: e.matmul(
                        bS[:, 0:ncol], KC[R0:R1, l, pr, :], qr, start=True, stop=True), qk_ready + fS + [LDP])
                    ps = P.op("pe", lambda e: e.matmul(
                        bS[0:ns, 256:256 + ncol], KS[R0:R1, l, pr, 0:ns], qr, start=True, stop=True), [], signal=True)
                    P.op("act", lambda e: e.activation(
                        out=PT[:, r, 0:ncol], in_=bS[:, 0:ncol], func=AF.Exp, bias=ZER[:, 0:1], scale=0.125),
                        [ps] + t_free["PT"][r])
                    ap_ = P.op("act", lambda e: e.activation(
                        out=PT[0:ns, r, 256:256 + ncol], in_=bS[0:ns, 256:256 + ncol], func=AF.Exp,
                        bias=ZER[0:ns, 0:1], scale=0.125), [], signal=True)
                    BK.release(iS, [ap_])
                    iO, bO, fO = BK.get()
                    P.op("pe", lambda e: e.matmul(
                        bO[:, 0:ncol], VC[:, l, pr * 128:(pr + 1) * 128], PT[:, r, 0:ncol],
                        start=True, stop=False), [ap_] + fO + v_ready)
                    P.op("pe", lambda e: e.matmul(
                        bO[:, 0:ncol], VS[0:ns, l, pr * 128:(pr + 1) * 128], PT[0:ns, r, 256:256 + ncol],
                        start=False, stop=True))
                    P.op("pe", lambda e: e.matmul(
                        bO[:, 256:256 + ncol], ONESB[:, :], PT[:, r, 0:ncol], start=True, stop=False))
                    P.op("pe", lambda e: e.matmul(
                        bO[:, 256:256 + ncol], ONESB[0:ns, :], PT[0:ns, r, 256:256 + ncol],
                        start=False, stop=False))
                    esr_ap = ESR[:, l, k, :].rearrange("p (g q) -> p g q", q=64)[:, :, 0:nq]
                    po = P.op("pe", lambda e: e.matmul(
                        bO[:, 256:256 + ncol], ONEHOT0[:, :], esr_ap, start=False, stop=True), [c_esr], signal=True)
                    t_free["PT"][r] = [po]
                    rd = rot["dt"]
                    rot["dt"] ^= 1
                    d1 = P.op("act", lambda e: e.activation(
                        out=RD[R0:R1, rd, 0:ncol], in_=bO[R0:R1, 256:256 + ncol], func=AF.Ln),
                        [po] + t_free["DT"][rd], signal=True)
                    d2 = P.op("act", lambda e: e.activation(
                        out=RD[R0:R1, rd, 0:ncol], in_=RD[R0:R1, rd, 0:ncol], func=AF.Exp, scale=-1.0),
                        [d1], signal=True)
                    ao = ARENA[R0:R1, 4096:8192].rearrange("p (c t) -> p c t", t=TILE)[:, 4 * pr:4 * pr + 4, n:ntot]
                    d3 = P.op("dve", lambda e: e.tensor_tensor(
                        out=ao, in0=bO[R0:R1, 0:ncol].rearrange("p (g q) -> p g q", q=nq),
                        in1=RD[R0:R1, rd, 0:ncol].rearrange("p (g q) -> p g q", q=nq), op=ALU.mult),
                        [d2], signal=True)
                    t_free["DT"][rd] = [d3]
                    BK.release(iO, [d3])
                    state["ao_last"] = d3

                def stage_a(c, k):
                    hh = k % 2
                    pr = k // 2
                    R0, R1 = 64 * hh, 64 * hh + 64
                    par = c % 2
                    r = 2 * par + rot_pt[par]
                    rot_pt[par] ^= 1
                    iS, bS, fS = BK.get()
                    q0 = 64 * c
                    qr = ARENA[R0:R1, 0:4096].rearrange("p (c t) -> p c t", t=TILE)[:, 4 * pr:4 * pr + 4, q0:q0 + 64]
                    if par == 0:
                        fcol = 64 * c
                        fblk = c // 2
                        hlo = 64 * c + 128
                        hblk = c // 2 + 1
                        HR0, HR1 = 0, 64
                        hM = 64
                        full_is_halo = first_own and c == 0
                        half_is_halo = False
                    else:
                        fcol = 64 * (c + 1)
                        fblk = (c + 1) // 2
                        hlo = 64 * c - 64
                        hblk = (c - 1) // 2
                        HR0, HR1 = 64, 128
                        hM = 128
                        full_is_halo = False
                        half_is_halo = first_own and c == 1
                    P.op("pe", lambda e: e.matmul(
                        bS[:, 0:256], KB[R0:R1, l, pr, fcol:fcol + 128], qr, start=True, stop=True), qk_ready + fS)
                    ps = P.op("pe", lambda e: e.matmul(
                        bS[0:hM, 256:512], KB[R0:R1, l, pr, hlo:hlo + hM], qr, start=True, stop=True), [], signal=True)
                    bf = FLAGS[:, 0:1] if full_is_halo else ZER[:, 0:1]
                    bh = FLAGS[HR0:HR1, 0:1] if half_is_halo else ZER[HR0:HR1, 0:1]
                    P.op("act", lambda e: e.activation(
                        out=PT[:, r, 0:256], in_=bS[:, 0:256], func=AF.Exp, bias=bf, scale=0.125),
                        [ps, LD] + t_free["PT"][r])
                    ap_ = P.op("act", lambda e: e.activation(
                        out=PT[HR0:HR1, r, 256:512], in_=bS[HR0:HR1, 256:512], func=AF.Exp, bias=bh, scale=0.125),
                        [], signal=True)
                    BK.release(iS, [ap_])
                    return dict(ap=ap_, r=r, fblk=fblk, hblk=hblk, R0=R0, R1=R1, pr=pr, k=k, q0=q0)

                def stage_b(cx):
                    r, fblk, hblk, R0, R1, pr, k, q0 = (cx[x] for x in ("r", "fblk", "hblk", "R0", "R1", "pr", "k", "q0"))
                    iO, bO, fO = BK.get()
                    P.op("pe", lambda e: e.matmul(
                        bO[:, 0:256], VB[:, l, fblk, pr * 128:(pr + 1) * 128], PT[:, r, 0:256],
                        start=True, stop=False), [cx["ap"]] + fO + v_ready)
                    P.op("pe", lambda e: e.matmul(
                        bO[:, 0:256], VB[:, l, hblk, pr * 128:(pr + 1) * 128], PT[:, r, 256:512],
                        start=False, stop=True))
                    P.op("pe", lambda e: e.matmul(
                        bO[:, 256:512], ONESB[:, :], PT[:, r, 0:256], start=True, stop=False))
                    P.op("pe", lambda e: e.matmul(
                        bO[:, 256:512], ONESB[:, :], PT[:, r, 256:512], start=False, stop=False))
                    po = P.op("pe", lambda e: e.matmul(
                        bO[:, 256:512], ONEHOT0[:, :], ESR[:, l, k, :], start=False, stop=True), [c_esr], signal=True)
                    t_free["PT"][r] = [po]
                    rd = rot["dt"]
                    rot["dt"] ^= 1
                    d1 = P.op("act", lambda e: e.activation(
                        out=RD[R0:R1, rd, :], in_=bO[R0:R1, 256:512], func=AF.Ln),
                        [po] + t_free["DT"][rd], signal=True)
                    d2 = P.op("act", lambda e: e.activation(
                        out=RD[R0:R1, rd, :], in_=RD[R0:R1, rd, :], func=AF.Exp, scale=-1.0), [d1], signal=True)
                    ao = ARENA[R0:R1, 4096:8192].rearrange("p (c t) -> p c t", t=TILE)[:, 4 * pr:4 * pr + 4, q0:q0 + 64]
                    d3 = P.op("dve", lambda e: e.tensor_tensor(
                        out=ao, in0=bO[R0:R1, 0:256].rearrange("p (g q) -> p g q", q=64),
                        in1=RD[R0:R1, rd, :].rearrange("p (g q) -> p g q", q=64), op=ALU.mult), [d2], signal=True)
                    t_free["DT"][rd] = [d3]
                    BK.release(iO, [d3])
                    state["ao_last"] = d3

                prev = None
                for c in range(nchunks):
                    for k in range(NKV):
                        cx = stage_a(c, k)
                        if prev is not None:
                            stage_b(prev)
                        prev = cx
                        yield
                stage_b(prev)
                if ns:
                    for k in range(NKV):
                        sample_group(k)
                        yield

            acc_info = {}

            def conv_chain(c):
                r = rot["sg"]
                rot["sg"] ^= 1
                acc = None
                for j in range(TD):
                    w_ap = WDW[:, l, c, j:j + 1]
                    if j == 0:
                        acc = P.op("dve", lambda e: e.tensor_scalar(
                            out=SG[:, r, 0:n], in0=f32(Av(c, 0, n)), scalar1=w_ap, scalar2=vec(V_BDW, c),
                            op0=ALU.mult, op1=ALU.add), a_ready + [LD] + t_free["SG"][r], signal=True)
                        if ns:
                            acc = P.op("dve", lambda e: e.tensor_scalar(
                                out=SG[:, r, n:ntot], in0=f32(AS[:, l, c, 0:ns]), scalar1=w_ap,
                                scalar2=vec(V_BDW, c), op0=ALU.mult, op1=ALU.add), [], signal=True)
                    else:
                        accp = P.op("dve", lambda e: e.scalar_tensor_tensor(
                            out=SG[:, r, 0:n], in0=f32(Av(c, j, j + n)), scalar=w_ap, in1=SG[:, r, 0:n],
                            op0=ALU.mult, op1=ALU.add), [acc], signal=True)
                        if ns:
                            accp = P.op("dve", lambda e: e.scalar_tensor_tensor(
                                out=SG[:, r, n:ntot], in0=f32(AS[:, l, c, j:j + ns]), scalar=w_ap,
                                in1=SG[:, r, n:ntot], op0=ALU.mult, op1=ALU.add), [acc], signal=True)
                        acc = accp
                acc_info[c] = (r, acc)

            def conv():
                for c in range(NCH):
                    if c + 1 < NCH:
                        conv_chain(c + 1)
                    r, acc = acc_info[c]
                    iC, bC, fC = BK.get()
                    pe_t = None
                    for g in range(NCG):
                        s = wnext(U_CONV + 4 * c + g)
                        slot, wtok = wunit(s)
                        ntap = min(8, CONVW - TD - 8 * g)
                        for t in range(ntap):
                            j = TD + 8 * g + t
                            firstu = (t == 0)
                            pe_t = P.op("pe", lambda e, slot=slot, t=t, j=j: e.matmul(
                                bC[:, 0:n], WR[:, slot, t * 128:(t + 1) * 128], Av(c, j, j + n),
                                start=(j == TD), stop=(j == CONVW - 1)),
                                ([wtok] + (a_ready + fC if j == TD else [])) if firstu else [],
                                signal=(t == ntap - 1 and not ns))
                            if ns:
                                pe_t = P.op("pe", lambda e, slot=slot, t=t, j=j: e.matmul(
                                    bC[:, n:ntot], WR[:, slot, t * 128:(t + 1) * 128], AS[:, l, c, j:j + ns],
                                    start=False, stop=(j == CONVW - 1), skip_group_check=True),
                                    [], signal=(t == ntap - 1))
                        wdone(s, pe_t)
                        yield
                    d = P.op("dve", lambda e: e.tensor_tensor(
                        out=ZCv(c, 0, ntot), in0=bC[:, 0:ntot], in1=SG[:, r, 0:ntot], op=ALU.add),
                        [pe_t, acc], signal=True)
                    t_free["SG"][r] = [d]
                    BK.release(iC, [d])
                    state["zc_tok"][c] = d

            conv_chain(0)
            state["zc_tok"] = [None] * NCH
            for _ in conv():
                pass
            conv_pe_done = state["zc_tok"][NCH - 1]

            iM, bM, fM = BK.get()
            iQ, bQ, fQ = BK.get()
            lastm = lastq = None
            for c in range(NCH):
                r = rot["sq"]
                rot["sq"] ^= 1
                zt = state["zc_tok"][c]
                a_t = P.op("act", lambda e, c=c, r=r: e.activation(
                    out=SQ[:, r, 0:ntot], in_=f32(ZCv(c, 0, ntot)), func=AF.Square), [zt] + sq_free[r], signal=True)
                lastm = P.op("pe", lambda e, c=c: e.matmul(
                    bM[:, 0:ntot], ONES[:, :], ZCv(c, 0, ntot), start=(c == 0), stop=(c == NCH - 1)),
                    [zt] + (fM if c == 0 else []), signal=(c == NCH - 1))
                lastq = P.op("pe", lambda e, c=c, r=r: e.matmul(
                    bQ[:, 0:ntot], ONES[:, :], SQ[:, r, 0:ntot], start=(c == 0), stop=(c == NCH - 1)),
                    [a_t] + (fQ if c == 0 else []), signal=True)
                sq_free[r] = [lastq]
            dm = P.op("dve", lambda e: e.tensor_scalar(
                out=MU[:, 0:ntot], in0=bM[:, 0:ntot], scalar1=1.0 / D, scalar2=None, op0=ALU.mult),
                [lastm], signal=True)
            BK.release(iM, [dm])
            dm2 = P.op("dve", lambda e: e.tensor_tensor(
                out=M2[:, 0:ntot], in0=MU[:, 0:ntot], in1=MU[:, 0:ntot], op=ALU.mult), [dm], signal=True)
            dv = P.op("dve", lambda e: e.scalar_tensor_tensor(
                out=M2[:, 0:ntot], in0=bQ[:, 0:ntot], scalar=1.0 / D, in1=M2[:, 0:ntot],
                op0=ALU.mult, op1=ALU.subtract), [lastq, dm2], signal=True)
            BK.release(iQ, [dv])
            asd = P.op("act", lambda e: e.activation(
                out=M2[:, 0:ntot], in_=M2[:, 0:ntot], func=AF.Ln, bias=EPSB[:, 0:1], scale=1.0), [dv], signal=True)
            drs = P.op("act", lambda e: e.activation(out=RS[:, 0:ntot], in_=M2[:, 0:ntot], func=AF.Exp, scale=-0.5),
                       [asd], signal=True)
            ln_toks = []

            def ln_dve():
                for c in range(NCH):
                    r = rot["t"]
                    rot["t"] ^= 1
                    d1 = P.op("dve", lambda e, c=c, r=r: e.tensor_tensor(
                        out=T1[:, r, 0:ntot], in0=f32(ZCv(c, 0, ntot)), in1=MU[:, 0:ntot], op=ALU.subtract),
                        [dm, lastm, lastq] + t_free["T1"][r], signal=True)
                    d2 = P.op("dve", lambda e, c=c, r=r: e.tensor_tensor(
                        out=ZCv(c, 0, ntot), in0=T1[:, r, 0:ntot], in1=RS[:, 0:ntot], op=ALU.mult),
                        [d1, drs], signal=True)
                    t_free["T1"][r] = [d2]
                    ln_toks.append(d2)
                    yield

            def ln_apply():
                z_tok = None
                for c in range(NCH):
                    z_tok = P.op("act", lambda e, c=c: e.activation(
                        out=ZCv(c, 0, ntot), in_=f32(ZCv(c, 0, ntot)), func=AF.Silu, bias=vec(V_LNB, c),
                        scale=vec(V_LNG, c)), [ln_toks[c], LD], signal=True)
                    state["z_tok"] = z_tok
                yield

            alive = [qk_gen(range(8)), ln_dve()]
            while alive:
                for g_ in list(alive):
                    try:
                        next(g_)
                    except StopIteration:
                        alive.remove(g_)
            for _ in qk_gen(range(8, 10)):
                pass
            for _ in ln_apply():
                pass
            z_ready = [state["z_tok"]]
            for _ in v_gen():
                pass
            qk_ready = [state["rope_last"]]
            if l == 1:
                state["rope_free"] = [state["rope_last"]]
            v_ready = [state["v_last"]]
            for _ in attention():
                pass
            ao_ready = [state["ao_last"]]

            kb_tok = P.op("act", lambda e: e.activation(
                out=KB[:, l, :, 0:128], in_=f32(KB[:, l, :, n:n + 128]), func=AF.Copy), ao_ready, signal=True)
            nblk_ = n // 128
            vb_tok = P.op("act", lambda e: e.activation(
                out=VB[:, l, 0, :], in_=VB[:, l, nblk_, :], func=AF.Copy), ao_ready, signal=True)
            state["kb_ready"][l] = vb_tok
            a_all = ARENA[:, 12288:12288 + NCH * ACH].rearrange("p (c t) -> p c t", t=ACH)
            if p == 0:
                car2 = P.op("dve", lambda e: e.tensor_scalar(
                    out=ACAR[:, l, :, :], in0=f32(a_all[:, :, n:n + 30]), scalar1=FLAGS[:, 1:2], scalar2=None,
                    op0=ALU.mult), [conv_pe_done, LD], signal=True)
            else:
                car2 = P.op("dve", lambda e: e.tensor_copy(
                    out=ACAR[:, l, :, :], in_=f32(a_all[:, :, n:n + 30])), [conv_pe_done], signal=True)

            if p == npass - 1:
                state["out_tokens"].append(P.dma("sp", lambda e: e.dma_start(
                    out=kpT[l], in_=f32(KB[:, l, :, 0:128])), fo_sem, [kb_tok]))
                state["out_tokens"].append(P.dma("sp", lambda e: e.dma_start(
                    out=vp[l], in_=VF[:, l, :]), fo_sem, [state["vf_tok"]]))
                state["out_tokens"].append(P.dma("sp", lambda e: e.dma_start(
                    out=cpT[l], in_=f32(ACAR[:, l, :, :])), fo_sem, [car2]))
            if ns:
                state["out_tokens"].append(P.dma("sp", lambda e: e.dma_start(
                    out=ksT[l], in_=f32(KS[:, l, :, :])), fo_sem, qk_ready))
                state["out_tokens"].append(P.dma("sp", lambda e: e.dma_start(
                    out=vs[l], in_=VSF[0:ns, l, :]), fo_sem, [state["vsf_tok"]]))
                state["out_tokens"].append(P.dma("sp", lambda e: e.dma_start(
                    out=csT[l], in_=f32(AS[:, l, :, ns:ns + 30])), fo_sem, a_ready))

            mg_tok = None
            mg_toks = []
            for j in range(NCH):
                lM = U_MERGE + 4 * j
                iA, bA, fA = BK.get()
                iC, bC, fC = BK.get()
                iG, bG, fG = BK.get()
                iH, bH, fH = BK.get()
                pG = proj_group(wnext(lM + 2), ntot, hch, 8, h_ready, bG, fG)
                pH = proj_group(wnext(lM + 3), ntot, hch, 8, h_ready, bH, fH)
                pA = proj_group(wnext(lM), ntot, lambda kc: AOv(kc, 0, ntot), 8, ao_ready, bA, fA)
                pC = proj_group(wnext(lM + 1), ntot, lambda kc: ZCv(kc, 0, ntot), 8, z_ready, bC, fC)
                r = rot["sg"]
                rot["sg"] ^= 1
                r2 = rot["t"]
                rot["t"] ^= 1
                a1 = P.op("act", lambda e, bG=bG, r=r: e.activation(
                    out=SG[:, r, 0:ntot], in_=bG[:, 0:ntot], func=AF.Sigmoid), [pG] + t_free["SG"][r], signal=True)
                a2 = P.op("act", lambda e, bH=bH, r2=r2: e.activation(
                    out=T2[:, r2, 0:ntot], in_=bH[:, 0:ntot], func=AF.Sigmoid), [pH] + t_free["T2"][r2], signal=True)
                BK.release(iG, [a1])
                BK.release(iH, [a2])
                d1 = P.op("dve", lambda e, bA=bA, r=r: e.tensor_tensor(
                    out=SG[:, r, 0:ntot], in0=bA[:, 0:ntot], in1=SG[:, r, 0:ntot], op=ALU.mult), [pA, a1], signal=True)
                d2 = P.op("dve", lambda e, bC=bC, r2=r2: e.tensor_tensor(
                    out=T2[:, r2, 0:ntot], in0=bC[:, 0:ntot], in1=T2[:, r2, 0:ntot], op=ALU.mult), [pC, a2], signal=True)
                BK.release(iA, [d1])
                BK.release(iC, [d2])
                mg_tok = P.op("dve", lambda e, j=j, r=r, r2=r2: e.tensor_tensor(
                    out=Qv(j, 0, ntot), in0=SG[:, r, 0:ntot], in1=T2[:, r2, 0:ntot], op=ALU.add),
                    [d1, d2] + ao_ready, signal=True)
                t_free["SG"][r] = [mg_tok]
                t_free["T2"][r2] = [mg_tok]
                mg_toks.append(mg_tok)
            mg_ready = [mg_tok]

            st2 = Stats(ntot)
            xt = [None] * NCH
            for j in range(NCH):
                s = wnext(U_OUT + j)
                iA, bA, fA = BK.get()
                pA = proj_group(s, ntot, lambda kc: Qv(kc, 0, ntot), 8, (lambda kc: [mg_toks[kc]]) if j == 0 else mg_ready, bA, fA)
                xt[j] = P.op("dve", lambda e, bA=bA, j=j: e.tensor_tensor(
                    out=X[:, j, 0:ntot], in0=bA[:, 0:ntot], in1=X[:, j, 0:ntot], op=ALU.add), [pA, h_tok], signal=True)
                BK.release(iA, [xt[j]])
                if j >= 1:
                    st2.add(X[:, j - 1, 0:ntot], [xt[j - 1]])
            st2.add(X[:, NCH - 1, 0:ntot], [xt[NCH - 1]])
            x2_ready = [xt[NCH - 1]]
            d_rs = st2.finish()

            h2_tok = None
            h2_toks = []
            for c in range(NCH):
                h2_tok = P.op("dve", lambda e, c=c: e.scalar_tensor_tensor(
                    out=H[:, c, 0:ntot], in0=X[:, c, 0:ntot], scalar=vec(V_NMLP, c), in1=RS[:, 0:ntot],
                    op0=ALU.mult, op1=ALU.mult), [d_rs] + x2_ready, signal=True)
                h2_toks.append(h2_tok)
            h2_ready = [h2_tok]

            hid_tok = None
            hid_toks = []
            for i in range(32):
                s = wnext(U_UP + i)
                iA, bA, fA = BK.get()
                pA = proj_group(s, ntot, hch, 8, (lambda kc: [h2_toks[kc]]) if i == 0 else h2_ready, bA, fA)
                r = rot["sg"]
                rot["sg"] ^= 1
                a1 = P.op("act", lambda e, bA=bA, r=r: e.activation(
                    out=SG[:, r, 0:ntot], in_=bA[:, 0:ntot], func=AF.Relu), [pA] + t_free["SG"][r], signal=True)
                BK.release(iA, [a1])
                hid_tok = P.op("dve", lambda e, i=i, r=r: e.tensor_tensor(
                    out=HIDv(i, 0, ntot), in0=SG[:, r, 0:ntot], in1=SG[:, r, 0:ntot], op=ALU.mult),
                    [a1, car2, kb_tok, vb_tok], signal=True)
                t_free["SG"][r] = [hid_tok]
                hid_toks.append(hid_tok)
            hid_ready = [hid_tok]

            st3 = Stats(ntot)
            xt = [None] * NCH
            for j in range(NCH):
                iA, bA, fA = BK.get()
                pA = None
                for g in range(4):
                    s = wnext(U_DOWN + 4 * j + g)
                    pA = proj_group(s, ntot, lambda kc, g=g: HIDv(8 * g + kc, 0, ntot), 8,
                                    (lambda kc, g=g: [hid_toks[8 * g + kc]]) if j == 0 else hid_ready, bA,
                                    fA if g == 0 else [], start=(g == 0), stop=(g == 3))
                xt[j] = P.op("dve", lambda e, bA=bA, j=j: e.tensor_tensor(
                    out=X[:, j, 0:ntot], in0=bA[:, 0:ntot], in1=X[:, j, 0:ntot], op=ALU.add), [pA, h2_tok], signal=True)
                BK.release(iA, [xt[j]])
                if j >= 1:
                    st3.add(X[:, j - 1, 0:ntot], [xt[j - 1]])
            st3.add(X[:, NCH - 1, 0:ntot], [xt[NCH - 1]])
            return [xt[NCH - 1]], st3.finish()

        for p in range(npass):
            if p == 0:
                n, ns = HALO, NSMP
                tok0, rcol0 = 0, 0
            else:
                n, ns = TILE, 0
                tok0 = HALO + TILE * (p - 1)
                rcol0 = HALO + NSMP + TILE * (p - 1)
            ntot = n + ns
            xsrc = xT.rearrange("(c q) t -> q c t", q=128)[:, :, tok0:tok0 + n]
            xt = P.dma("sp", lambda e, xsrc=xsrc, n=n: e.dma_start(out=X[:, :, 0:n], in_=xsrc), x_sem, state["x_free"])
            x_ready = [xt]
            if ns:
                xs_src = xsT.rearrange("(c q) t -> q c t", q=128)
                xt2 = P.dma("sp", lambda e, xs_src=xs_src, n=n, ntot=ntot: e.dma_start(
                    out=X[:, :, n:ntot], in_=xs_src), x_sem, state["x_free"])
                x_ready = [xt2]
            rt1 = P.dma("sp", lambda e, rcol0=rcol0, ntot=ntot: e.dma_start(
                out=RC[:, 0:ntot], in_=ropec_d[:, rcol0:rcol0 + ntot]), r_sem, state["rope_free"])
            rt2 = P.dma("sp", lambda e, rcol0=rcol0, ntot=ntot: e.dma_start(
                out=RSN[:, 0:ntot], in_=ropes_d[:, rcol0:rcol0 + ntot]), r_sem, state["rope_free"])
            rope_ready = [rt2]

            xr = x_ready
            d_rs = None
            for l in range(2):
                xr, d_rs = layer(p, l, n, ns, xr, rope_ready, d_rs)

            y_tok = None
            for c in range(NCH):
                y_tok = P.op("dve", lambda e, c=c, ntot=ntot: e.scalar_tensor_tensor(
                    out=X[:, c, 0:ntot], in0=X[:, c, 0:ntot], scalar=vec(80, c), in1=RS[:, 0:ntot],
                    op0=ALU.mult, op1=ALU.mult), [d_rs] + xr, signal=True)
            if p == 0:
                st = P.dma("sp", lambda e, n=n, ntot=ntot: e.dma_start(
                    out=ysT.rearrange("(c q) t -> q c t", q=128), in_=X[:, :, n:ntot]), o_sem, [y_tok])
            else:
                o0 = TILE * (p - 1)
                st = P.dma("sp", lambda e, o0=o0, n=n: e.dma_start(
                    out=ypT.rearrange("(c q) t -> q c t", q=128)[:, :, o0:o0 + n], in_=X[:, :, 0:n]),
                    o_sem, [y_tok])
            state["x_free"] = [st]

        finals = [(o_sem[0], o_sem[1]), (fo_sem[0], fo_sem[1])]

        with nc.Block() as block:
            @block.tensor
            def _(e):
                P.run("pe", e)

            @block.scalar
            def _(e):
                P.run("act", e)

            @block.vector
            def _(e):
                P.run("dve", e)

            @block.gpsimd
            def _(e):
                P.run("pool", e)

            @block.sync
            def _(e):
                P.run("sp", e, final_waits=finals)
    return nc, order


def _units_for_layer(w_in, w_o, w_dw, w_pw2, w_out, w_up, w_down):
    U = np.zeros((NU, 128, 1024), np.float32)

    def put(u, mat):
        U[u] = mat.reshape(8, 128, 128).transpose(1, 0, 2).reshape(128, 1024)

    m = np.arange(128)
    for cc in range(8):
        heads = np.array([q_head(cc, 0)] * 64 + [q_head(cc, 1)] * 64)
        dd = m % 64
        put(U_Q + 2 * cc, w_in[:, heads * 64 + dd])
        put(U_Q + 2 * cc + 1, w_in[:, heads * 64 + (dd + 32) % 64])
    for cc in range(2):
        base = 1024 + cc * 128
        put(U_K + 2 * cc, w_in[:, base + m])
        put(U_K + 2 * cc + 1, w_in[:, base + (m // 64) * 64 + (m % 64 + 32) % 64])
    for cc in range(2):
        put(U_V + cc, w_in[:, 1280 + cc * 128 + m])
    for c in range(8):
        put(U_U + 2 * c, w_in[:, 1536 + c * 128 + m])
        put(U_U + 2 * c + 1, w_in[:, 2560 + c * 128 + m])
    for c in range(8):
        for g in range(4):
            blk = np.zeros((128, 8, 128), np.float32)
            for t in range(8):
                j = TD + 8 * g + t
                if j < CONVW:
                    blk[m, t, m] = w_dw[j, c * 128 + m]
            U[U_CONV + 4 * c + g] = blk.reshape(128, 1024)
    rows = np.zeros(1024, np.int64)
    for cc in range(8):
        for half in range(2):
            rows[cc * 128 + half * 64: cc * 128 + half * 64 + 64] = q_head(cc, half) * 64 + np.arange(64)
    w_o_p = w_o[rows, :]
    for j in range(8):
        put(U_MERGE + 4 * j, w_o_p[:, j * 128 + m])
        put(U_MERGE + 4 * j + 1, w_pw2[:, j * 128 + m])
        put(U_MERGE + 4 * j + 2, w_in[:, 3584 + j * 128 + m])
        put(U_MERGE + 4 * j + 3, w_in[:, 4608 + j * 128 + m])
        put(U_OUT + j, w_out[:, j * 128 + m])
    for i in range(32):
        put(U_UP + i, w_up[:, i * 128 + m])
    for j in range(8):
        for g in range(4):
            put(U_DOWN + 4 * j + g, w_down[g * 1024:(g + 1) * 1024, j * 128 + m])
    return U


def _vec_cols(v):
    return np.ascontiguousarray(v.reshape(8, 128).T)


def _rope_tables(pos):
    half = HD // 2
    inv_freq = (np.float32(10000.0) ** (-np.arange(half, dtype=np.float32) / np.float32(half))).astype(np.float32)
    ang = pos.astype(np.float32)[None, :] * inv_freq[:, None]
    cos = np.cos(ang).astype(np.float32)
    sin = np.sin(ang).astype(np.float32)
    c64 = np.concatenate([cos, cos], 0)
    s64 = np.concatenate([-sin, sin], 0)
    return np.concatenate([c64, c64], 0), np.concatenate([s64, s64], 0)


_CACHE = {}


def kernel(x_prompt, x_sample, cache_k, cache_v, state_conv, norm_mix, w_in, sinks, w_o_attn,
           w_dw, b_dw, ln_conv_g, ln_conv_b, w_pw2, w_out, norm_mlp, w_up, w_down, norm_final):
    f = lambda a: np.asarray(a, dtype=np.float32)
    x_prompt, x_sample, cache_k, cache_v, state_conv = map(f, (x_prompt, x_sample, cache_k, cache_v, state_conv))
    norm_mix, w_in, sinks, w_o_attn, w_dw, b_dw = map(f, (norm_mix, w_in, sinks, w_o_attn, w_dw, b_dw))
    ln_conv_g, ln_conv_b, w_pw2, w_out, norm_mlp, w_up, w_down, norm_final = map(
        f, (ln_conv_g, ln_conv_b, w_pw2, w_out, norm_mlp, w_up, w_down, norm_final))
    B, SEQ, _ = x_prompt.shape
    own = SEQ // 2
    n_own = own // TILE
    ncores = 8
    assert B * 2 == ncores and x_sample.shape[0] == ncores
    if n_own not in _CACHE:
        _CACHE[n_own] = build_program(n_own)
    nc, order = _CACHE[n_own]

    assert len(order) == NUS and len(set(order)) == NUS
    wst = np.stack([_units_for_layer(w_in[l], w_o_attn[l], w_dw[l], w_pw2[l], w_out[l], w_up[l], w_down[l])[order]
                    for l in range(2)], 0)
    vecs = np.zeros((128, NVEC), np.float32)
    for l in range(2):
        for i, v in enumerate((norm_mix[l], b_dw[l], ln_conv_g[l], ln_conv_b[l], norm_mlp[l])):
            vecs[:, 40 * l + 8 * i: 40 * l + 8 * i + 8] = _vec_cols(v)
    vecs[:, 80:88] = _vec_cols(norm_final)
    wdw = np.ascontiguousarray(w_dw.reshape(2, CONVW, NCH, 128).transpose(3, 0, 2, 1)[:, :, :, :TD])
    sinks_b = np.ascontiguousarray(np.broadcast_to(sinks.reshape(1, 32), (128, 32))).astype(np.float32)

    mm_ = np.arange(128)
    perm = np.zeros((128, 128), np.float32)
    perm[(mm_ // 64) * 64 + (mm_ % 64 + 32) % 64, mm_] = 1.0
    in_maps = []
    for core in range(ncores):
        b, hf = core // 2, core % 2
        start = hf * own
        xT = np.zeros((D, HALO + own), np.float32)
        if hf == 1:
            xT[:, :] = x_prompt[b, start - HALO:start + own, :].T
        else:
            xT[:, HALO:] = x_prompt[b, 0:own, :].T
        pos = np.concatenate([np.arange(start - HALO, start), PAST + np.arange(NSMP), np.arange(start, start + own)])
        rc, rs = _rope_tables(pos.astype(np.float32))
        flags = np.zeros((128, 4), np.float32)
        flags[0, 2] = 1.0
        flags[:, 0] = 0.0 if hf == 1 else -30000.0
        flags[:, 1] = 1.0 if hf == 1 else 0.0
        in_maps.append({
            "xT": xT,
            "xsT": np.ascontiguousarray(x_sample[core].T),
            "wst": wst,
            "vecs": vecs,
            "wdw": wdw,
            "ropec": rc, "ropes": rs,
            "kcT": np.ascontiguousarray(cache_k[:, core].reshape(2, 128, 256).transpose(0, 2, 1)
                                        .reshape(2, 2, 128, 128).transpose(0, 2, 1, 3)),
            "vc": np.ascontiguousarray(cache_v[:, core].reshape(2, 128, 256)),
            "scT": np.ascontiguousarray(state_conv[:, core].transpose(0, 2, 1).reshape(2, 8, 128, 30)
                                        .transpose(0, 2, 1, 3)),
            "sinks": sinks_b,
            "perm": perm,
            "flags": flags,
        })
    res = run_bass_kernel_spmd(nc, in_maps, core_ids=list(range(ncores)))
    R = res.results

    y_prompt = np.zeros((B, SEQ, D), np.float32)
    y_sample = np.zeros((ncores, NSMP, D), np.float32)
    k_p = np.zeros((2, B, 128, NKV, HD), np.float32)
    v_p = np.zeros((2, B, 128, NKV, HD), np.float32)
    c_p = np.zeros((2, B, 30, D), np.float32)
    k_s = np.zeros((2, ncores, NSMP, NKV, HD), np.float32)
    v_s = np.zeros((2, ncores, NSMP, NKV, HD), np.float32)
    c_s = np.zeros((2, ncores, 30, D), np.float32)
    for core in range(ncores):
        b, hf = core // 2, core % 2
        r = R[core]
        y_prompt[b, hf * own:(hf + 1) * own, :] = r["ypT"].T
        y_sample[core] = r["ysT"].T
        k_s[:, core] = r["ksT"].transpose(0, 3, 2, 1).reshape(2, NSMP, NKV, HD)
        v_s[:, core] = r["vs"].reshape(2, NSMP, NKV, HD)
        c_s[:, core] = r["csT"].transpose(0, 3, 2, 1).reshape(2, 30, D)
        if hf == 1:
            k_p[:, b] = r["kpT"].transpose(0, 3, 2, 1).reshape(2, 128, NKV, HD)
            v_p[:, b] = r["vp"].reshape(2, 128, NKV, HD)
            c_p[:, b] = r["cpT"].transpose(0, 3, 2, 1).reshape(2, 30, D)
    return (y_prompt, y_sample, k_p, v_p, c_p, k_s, v_s, c_s)
```

```python
import contextlib
import numpy as np
import concourse.bass as bass
import concourse.mybir as mybir
from concourse.bass_utils import run_bass_kernel_spmd

F32 = mybir.dt.float32
F32R = mybir.dt.float32r
MMDT = mybir.dt.bfloat16
BF16 = mybir.dt.bfloat16
AF = mybir.ActivationFunctionType
ALU = mybir.AluOpType

D = 1024
NCH = 8
HD = 64
NQH = 16
NKV = 4
CONVW = 31
DFF = 4096
NIN = 5632
EPS = 1e-6
HALO = 256
TILE = 512
NSMP = 16
PAST = 2048
NU = 174
TD = 7
NCG = 3
NUS = 156
U_Q, U_K, U_V, U_U, U_CONV, U_MERGE, U_OUT, U_UP, U_DOWN = 0, 16, 20, 22, 38, 70, 102, 110, 142
NS = 16
ACH = 544
NVEC = 88
N_OWN_TILES = 8


def q_head(cc, half):
    return (cc + 4 * half) if cc < 4 else (8 + (cc - 4) + 4 * half)


class _Rec:
    def __init__(self):
        self.call = None

    def __getattr__(self, name):
        def f(*a, **k):
            self.call = (name, a, k)
            return None
        return f


def _record(fn):
    r = _Rec()
    fn(r)
    assert r.call is not None
    return r.call


class Prog:
    ENGS = ("pe", "act", "dve", "pool", "sp")

    def __init__(self, nc, es):
        self.nc = nc
        self.q = {e: [] for e in self.ENGS}
        self.sem = {e: es.enter_context(nc.semaphore("prog_" + e)) for e in self.ENGS}
        self.cnt = {e: 0 for e in self.ENGS}

    def op(self, eng, fn, waits=(), signal=False):
        tok = None
        if signal:
            self.cnt[eng] += 1
            tok = (self.sem[eng], self.cnt[eng])
        ws = []
        for w in waits:
            if w is None:
                continue
            if isinstance(w, list):
                ws.extend([x for x in w if x is not None])
            else:
                ws.append(w)
        self.q[eng].append((_record(fn), tuple(ws), tok, 1))
        return tok

    def dma(self, eng, fn, dsem, waits=()):
        dsem[1] += 16
        tok = (dsem[0], dsem[1])
        ws = []
        for w in waits:
            if w is None:
                continue
            if isinstance(w, list):
                ws.extend([x for x in w if x is not None])
            else:
                ws.append(w)
        self.q[eng].append((_record(fn), tuple(ws), tok, 16))
        return tok

    def run(self, eng, e, final_waits=()):
        seen = {}
        for (mname, margs, mkw), waits, tok, inc in self.q[eng]:
            for (sem, val) in waits:
                k = sem.num
                if seen.get(k, 0) >= val:
                    continue
                e.wait_ge(sem, val)
                seen[k] = val
            ins = getattr(e, mname)(*margs, **mkw)
            if tok is not None:
                ins.then_inc(tok[0], inc)
        for (sem, val) in final_waits:
            e.wait_ge(sem, val)


class Banks:
    def __init__(self, tensors):
        self.t = tensors
        self.free = [[] for _ in tensors]
        self.held = [False] * len(tensors)
        self.i = 0

    def get(self):
        n = len(self.t)
        for _ in range(n):
            i = self.i
            self.i = (i + 1) % n
            if not self.held[i]:
                self.held[i] = True
                fr = self.free[i]
                self.free[i] = []
                return i, self.t[i], fr
        raise RuntimeError("no free psum bank")

    def release(self, i, toks):
        self.held[i] = False
        self.free[i] = [t for t in toks if t is not None]


def build_program(n_own):
    npass = 1 + n_own
    ntok = HALO + TILE * n_own
    nc = bass.Bass("TRN2", target_bir_lowering=False)

    def din(name, shape):
        return nc.dram_tensor(name, list(shape), F32, kind="ExternalInput").ap()

    def dout(name, shape):
        return nc.dram_tensor(name, list(shape), F32, kind="ExternalOutput").ap()

    xT = din("xT", [D, ntok])
    xsT = din("xsT", [D, NSMP])
    wst = din("wst", [2, NUS, 128, 1024])
    perm_d = din("perm", [128, 128])
    vecs_d = din("vecs", [128, NVEC])
    wdw_d = din("wdw", [128, 2, NCH, TD])
    ropec_d = din("ropec", [128, ntok + NSMP])
    ropes_d = din("ropes", [128, ntok + NSMP])
    kcT_d = din("kcT", [2, 128, 2, 128])
    vc_d = din("vc", [2, 128, 256])
    scT_d = din("scT", [2, 128, 8, 30])
    sinks_d = din("sinks", [128, 32])
    flags_d = din("flags", [128, 4])

    ypT = dout("ypT", [D, TILE * n_own])
    ysT = dout("ysT", [D, NSMP])
    kpT = dout("kpT", [2, 128, 2, 128])
    vp = dout("vp", [2, 128, 256])
    cpT = dout("cpT", [2, 128, 8, 30])
    ksT = dout("ksT", [2, 128, 2, NSMP])
    vs = dout("vs", [2, NSMP, 256])
    csT = dout("csT", [2, 128, 8, 30])

    es = contextlib.ExitStack()
    with es:
        def sb(name, shape, dt=F32):
            return es.enter_context(nc.sbuf_tensor(name, list(shape), dt))

        X = sb("X", [128, NCH, TILE])
        H = sb("H", [128, NCH, TILE], MMDT)
        ARENA = sb("ARENA", [128, 12288 + NCH * ACH], MMDT)
        KB = sb("KB", [128, 2, 2, 128 + TILE], MMDT)
        VB = sb("VB", [128, 2, 5, 256], BF16)
        VF = sb("VF", [128, 2, 256])
        VSF = sb("VSF", [128, 2, 256])
        ACAR = sb("ACAR", [128, 2, NCH, 30], MMDT)
        AS = sb("AS", [128, 2, NCH, 30 + NSMP], MMDT)
        KC = sb("KC", [128, 2, 2, 128], MMDT)
        VC = sb("VC", [128, 2, 256], BF16)
        KS = sb("KS", [128, 2, 2, NSMP], MMDT)
        VS = sb("VS", [128, 2, 256], BF16)
        PT = sb("PT", [128, 4, 512], BF16)
        T1 = sb("T1", [128, 2, TILE])
        T2 = sb("T2", [128, 2, TILE])
        SG = sb("SG", [128, 2, TILE])
        SQ = sb("SQ", [128, 2, TILE], MMDT)
        RS = sb("RS", [128, TILE])
        MU = sb("MU", [128, TILE])
        M2 = sb("M2", [128, TILE])
        RD = sb("RD", [128, 2, 256])
        RC = sb("RC", [128, TILE])
        RSN = sb("RSN", [128, TILE])
        ESK = sb("ESK", [128, 32])
        ESR = sb("ESR", [128, 2, 4, 256], BF16)
        ONES = sb("ONES", [128, 128], MMDT)
        ONEHOT0 = sb("ONEHOT0", [128, 128], BF16)
        ONESB = sb("ONESB", [128, 128], BF16)
        PERM = sb("PERM", [128, 128], MMDT)
        ZER = sb("ZER", [128, 64])
        VECS = sb("VECS", [128, NVEC])
        WDW = sb("WDW", [128, 2, NCH, TD])
        FLAGS = sb("FLAGS", [128, 4])
        EPSB = sb("EPSB", [128, 1])
        WR = sb("WR", [128, NS, 1024], MMDT)

        banks_t = [es.enter_context(nc.psum_tensor("bank%d" % i, [128, 512], F32)) for i in range(8)]
        BK = Banks(banks_t)

        P = Prog(nc, es)

        def newsem(name):
            return [es.enter_context(nc.semaphore(name)), 0]

        wsem = [newsem("wsem%d" % i) for i in range(NS // 2)]
        ld_sem = newsem("ld")
        ldp_sem = newsem("ldp")
        x_sem = newsem("xs")
        r_sem = newsem("rs")
        o_sem = newsem("os")
        fo_sem = newsem("fo")

        def Qv(c, a, b):
            return ARENA[:, c * TILE + a: c * TILE + b]

        def AOv(c, a, b):
            return ARENA[:, 4096 + c * TILE + a: 4096 + c * TILE + b]

        def ZCv(c, a, b):
            return ARENA[:, 8192 + c * TILE + a: 8192 + c * TILE + b]

        def Av(c, a, b):
            return ARENA[:, 12288 + c * ACH + a: 12288 + c * ACH + b]

        def HIDv(i, a, b):
            return ARENA[:, i * TILE + a: i * TILE + b]

        def f32(ap):
            return ap.bitcast(F32) if ap.dtype == F32R else ap

        total_units = npass * 2 * NUS

        def issue_wdma(s0, waits):
            rem = s0 % (2 * NUS)
            l, u = rem // NUS, rem % NUS
            slot = s0 % NS
            pair = slot // 2
            src = wst[l, u:u + 2].rearrange("u p f -> p u f")
            dst = WR[:, slot:slot + 2, :]
            P.dma("pool", lambda e, dst=dst, src=src: e.dma_start(out=dst, in_=src), wsem[pair], waits)

        def wunit(s):
            slot = s % NS
            pair = slot // 2
            fill = s // NS + 1
            return slot, (wsem[pair][0], 16 * fill)

        def wdone(s, tok):
            if s % 2 == 1:
                nxt = s - 1 + NS
                if nxt < total_units:
                    issue_wdma(nxt, [tok])

        order = []
        wcount = [0]

        def wnext(logical):
            s_ = wcount[0]
            wcount[0] += 1
            if s_ < NUS:
                order.append(logical)
            else:
                assert order[s_ % NUS] == logical, (s_, logical)
            return s_

        P.dma("sp", lambda e: e.dma_start(out=VECS[:, :], in_=vecs_d[:, :]), ld_sem)
        P.dma("sp", lambda e: e.dma_start(out=FLAGS[:, :], in_=flags_d[:, :]), ld_sem)
        P.dma("sp", lambda e: e.dma_start(out=ESK[:, :], in_=sinks_d[:, :]), ld_sem)
        P.dma("sp", lambda e: e.dma_start(out=WDW[:, :, :, :], in_=wdw_d[:, :, :, :]), ld_sem)
        LD = (ld_sem[0], ld_sem[1])
        P.dma("pool", lambda e: e.dma_start(out=KC[:, :, :, :], in_=kcT_d.rearrange("l p c t -> p l c t")), ldp_sem)
        P.dma("pool", lambda e: e.dma_start(out=VC[:, :, :], in_=vc_d.rearrange("l p f -> p l f")), ldp_sem)
        for l in range(2):
            P.dma("pool", lambda e, l=l: e.dma_start(out=AS[:, l, :, 0:30], in_=scT_d[l]), ldp_sem)
        P.dma("pool", lambda e: e.dma_start(out=PERM[:, :], in_=perm_d[:, :]), ldp_sem)
        LDP = (ldp_sem[0], ldp_sem[1])
        for s0 in range(0, NS, 2):
            issue_wdma(s0, [])

        P.op("dve", lambda e: e.memset(T1[:, :, :], 0.0))
        P.op("dve", lambda e: e.memset(ZER[:, :], 0.0))
        P.op("dve", lambda e: e.memset(EPSB[:, :], EPS))
        z_src = T1[:, :, :].rearrange("p a b -> p (a b)")
        c_ones = P.op("dve", lambda e: e.tensor_scalar(out=ONES[:, :], in0=z_src[:, 0:128], scalar1=1.0,
                                                        scalar2=None, op0=ALU.add), signal=True)

        P.op("dve", lambda e: e.tensor_scalar(out=ONESB[:, :], in0=z_src[:, 0:128], scalar1=1.0,
                                              scalar2=None, op0=ALU.add))
        P.op("dve", lambda e: e.tensor_scalar(out=ONEHOT0[:, :], in0=z_src[:, 0:128], scalar1=FLAGS[:, 2:3],
                                              scalar2=None, op0=ALU.add), [LD])

        def zero_fill(flat, ncols):
            tok = None
            for c0 in range(0, ncols, 1024):
                c1 = min(ncols, c0 + 1024)
                tok = P.op("dve", lambda e, c0=c0, c1=c1: e.tensor_copy(out=flat[:, c0:c1], in_=z_src[:, 0:c1 - c0]),
                           signal=True)
            return tok

        zero_fill(KB[:, :, :, :].rearrange("p a b c -> p (a b c)"), 2 * 2 * (128 + TILE))
        zero_fill(VB[:, :, :, :].rearrange("p a b c -> p (a b c)"), 2 * 5 * 256)
        zero_fill(ACAR[:, :, :, :].rearrange("p a b c -> p (a b c)"), 2 * NCH * 30)
        zero_fill(PT[:, :, :].rearrange("p a b -> p (a b)"), 2048)
        c_init = zero_fill(VS[:, :, :].rearrange("p a b -> p (a b)"), 512)
        c_esk = P.op("act", lambda e: e.activation(out=ESK[:, :], in_=ESK[:, :], func=AF.Exp), [LD], signal=True)
        c_esr = None
        for l in range(2):
            for k in range(NKV):
                for g in range(4):
                    i = l * 16 + 4 * k + g
                    c_esr = P.op("dve", lambda e, l=l, k=k, g=g, i=i: e.tensor_scalar(
                        out=ESR[:, l, k, g * 64:(g + 1) * 64], in0=ZER[:, :], scalar1=ESK[:, i:i + 1], scalar2=None,
                        op0=ALU.add), [c_esk], signal=True)

        state = {
            "x_free": [],
            "rope_free": [],
            "y_store": None,
            "kb_ready": [None, None],
            "out_tokens": [],
        }
        sq_free = [[], []]
        t_free = {"T1": [[], []], "T2": [[], []], "SG": [[], []], "PT": [[], [], [], []], "DT": [[], []]}
        rot = {"sq": 0, "t": 0, "sg": 0, "pt": 0, "dt": 0}

        def vec(base, c):
            return VECS[:, base + c: base + c + 1]

        class Stats:
            def __init__(self, ntot):
                self.ntot = ntot
                self.bi, self.bank, self.bfree = BK.get()
                self.n = 0
                self.last = None

            def add(self, src_ap, waits):
                ntot, bank, c = self.ntot, self.bank, self.n
                r = rot["sq"]
                rot["sq"] ^= 1
                a_t = P.op("act", lambda e: e.activation(out=SQ[:, r, 0:ntot], in_=src_ap, func=AF.Square),
                           list(waits) + sq_free[r], signal=True)
                self.last = P.op("pe", lambda e: e.matmul(
                    bank[:, 0:ntot], ONES[:, :], SQ[:, r, 0:ntot], start=(c == 0), stop=(c == NCH - 1)),
                    [a_t, c_ones] + (self.bfree if c == 0 else []), signal=True)
                sq_free[r] = [self.last]
                self.n += 1

            def finish(self):
                ntot, bank = self.ntot, self.bank
                assert self.n == NCH
                a2 = P.op("act", lambda e: e.activation(out=MU[:, 0:ntot], in_=bank[:, 0:ntot], func=AF.Ln,
                                                         bias=EPSB[:, 0:1], scale=1.0 / D), [self.last], signal=True)
                BK.release(self.bi, [a2])
                return P.op("act", lambda e: e.activation(out=RS[:, 0:ntot], in_=MU[:, 0:ntot], func=AF.Exp,
                                                          scale=-0.5), [a2], signal=True)

        def rms_stats(src_chunk, ntot, src_waits):
            st = Stats(ntot)
            for c in range(NCH):
                st.add(src_chunk(c), src_waits)
            return st.finish()

        def proj_group(s, ntot, rhs_chunk, nk, rhs_waits, bank, bfree, start=True, stop=True, col0=0):
            slot, wtok = wunit(s)
            last = None
            for kc in range(nk):
                first = (kc == 0)
                last = P.op("pe", lambda e, kc=kc, slot=slot, first=first: e.matmul(
                    bank[:, col0:col0 + ntot], WR[:, slot, kc * 128:(kc + 1) * 128], rhs_chunk(kc),
                    start=(start and first), stop=(stop and kc == nk - 1)),
                    (rhs_waits(kc) + ([wtok] + bfree if first else [])) if callable(rhs_waits)
                    else (([wtok] + rhs_waits + bfree) if first else []), signal=(kc == nk - 1))
            wdone(s, last)
            return last

        def layer(p, l, n, ns, x_ready, rope_ready, d_rs_in):
            ntot = n + ns
            vb = 40 * l
            V_NM, V_BDW, V_LNG, V_LNB, V_NMLP, V_NF = vb, vb + 8, vb + 16, vb + 24, vb + 32, 80
            nchunks = n // 64
            first_own = (p == 1)

            d_rs = d_rs_in if d_rs_in is not None else rms_stats(lambda c: X[:, c, 0:ntot], ntot, x_ready)
            h_tok = None
            h_toks = []
            for c in range(NCH):
                h_tok = P.op("dve", lambda e, c=c: e.scalar_tensor_tensor(
                    out=H[:, c, 0:ntot], in0=X[:, c, 0:ntot], scalar=vec(V_NM, c), in1=RS[:, 0:ntot],
                    op0=ALU.mult, op1=ALU.mult), [d_rs, LD] + x_ready, signal=True)
                h_toks.append(h_tok)
            h_ready = [h_tok]
            h_first = [True]

            def h_waits(kc):
                return [h_toks[kc]]

            def hch(kc):
                return H[:, kc, 0:ntot]

            car_tok = P.op("act", lambda e: e.activation(
                out=ARENA[:, 12288:12288 + NCH * ACH].rearrange("p (c t) -> p c t", t=ACH)[:, :, 0:30],
                in_=f32(ACAR[:, l, :, :]), func=AF.Copy), [c_init] + x_ready, signal=True)
            a_last = None
            for c in range(NCH):
                iA, bA, fA = BK.get()
                iB, bB, fB = BK.get()
                pa = proj_group(wnext(U_U + 2 * c), ntot, hch, 8, h_waits if c == 0 else h_ready, bA, fA)
                pb = proj_group(wnext(U_U + 2 * c + 1), ntot, hch, 8, h_ready, bB, fB)
                r = rot["sg"]
                rot["sg"] ^= 1
                a1 = P.op("act", lambda e, bB=bB, r=r: e.activation(
                    out=SG[:, r, 0:ntot], in_=bB[:, 0:ntot], func=AF.Sigmoid), [pb] + t_free["SG"][r], signal=True)
                BK.release(iB, [a1])
                d1 = P.op("dve", lambda e, bA=bA, r=r, c=c: e.tensor_tensor(
                    out=Av(c, 30, 30 + n), in0=bA[:, 0:n], in1=SG[:, r, 0:n], op=ALU.mult), [pa, a1], signal=True)
                if ns:
                    d1 = P.op("dve", lambda e, bA=bA, r=r, c=c: e.tensor_tensor(
                        out=AS[:, l, c, 30:30 + ns], in0=bA[:, n:ntot], in1=SG[:, r, n:ntot], op=ALU.mult),
                        [pa, a1, LDP], signal=True)
                BK.release(iA, [d1])
                t_free["SG"][r] = [d1]
                a_last = d1
            a_ready = [a_last, car_tok, LDP]

            def qk_gen(js):
                def finish(cx):
                    j, iA, bA, pa, rq, ac = cx
                    isq = j < 8
                    iB, bB, fB = BK.get()
                    pb = P.op("pe", lambda e: e.matmul(
                        bB[:, 0:ntot], PERM[:, :], SQ[:, rq, 0:ntot], start=True, stop=True),
                        [ac, LDP] + fB, signal=True)
                    sq_free[rq] = [pb]
                    r = rot["t"]
                    rot["t"] ^= 1
                    d1 = P.op("dve", lambda e: e.tensor_tensor(
                        out=T1[:, r, 0:ntot], in0=bA[:, 0:ntot], in1=RC[:, 0:ntot], op=ALU.mult),
                        [pa, ac] + rope_ready + t_free["T1"][r], signal=True)
                    d2 = P.op("dve", lambda e: e.tensor_tensor(
                        out=T2[:, r, 0:ntot], in0=bB[:, 0:ntot], in1=RSN[:, 0:ntot], op=ALU.mult),
                        [pb] + rope_ready + t_free["T2"][r], signal=True)
                    BK.release(iA, [d1, ac])
                    BK.release(iB, [d2])
                    if isq:
                        d3 = P.op("dve", lambda e: e.tensor_tensor(
                            out=Qv(j, 0, ntot), in0=T1[:, r, 0:ntot], in1=T2[:, r, 0:ntot], op=ALU.add),
                            [d1, d2, state["y_store"]], signal=True)
                    else:
                        cc = j - 8
                        d3 = P.op("dve", lambda e: e.tensor_tensor(
                            out=KB[:, l, cc, 128:128 + n], in0=T1[:, r, 0:n], in1=T2[:, r, 0:n], op=ALU.add),
                            [d1, d2, state["kb_ready"][l]], signal=True)
                        if ns:
                            d3 = P.op("dve", lambda e: e.tensor_tensor(
                                out=KS[:, l, cc, 0:ns], in0=T1[:, r, n:ntot], in1=T2[:, r, n:ntot], op=ALU.add),
                                [d1, d2], signal=True)
                    t_free["T1"][r] = [d3]
                    t_free["T2"][r] = [d3]
                    state["rope_last"] = d3

                prev = None
                for j in js:
                    lA = (U_Q + 2 * j if j < 8 else U_K + 2 * (j - 8))
                    iA, bA, fA = BK.get()
                    pa = proj_group(wnext(lA), ntot, hch, 8, h_ready, bA, fA)
                    rq = rot["sq"]
                    rot["sq"] ^= 1
                    ac = P.op("act", lambda e: e.activation(
                        out=SQ[:, rq, 0:ntot], in_=bA[:, 0:ntot], func=AF.Copy), [pa] + sq_free[rq], signal=True)
                    cur = (j, iA, bA, pa, rq, ac)
                    if prev is not None:
                        finish(prev)
                    prev = cur
                    yield
                finish(prev)

            def v_gen():
                sV = wnext(U_V)
                assert sV % 2 == 0
                assert wnext(U_V + 1) == sV + 1
                slotV, wtokV = wunit(sV)
                _, wtokV2 = wunit(sV + 1)
                v_last = None
                pe_last = None
                nblk = n // 128
                for b in range(nblk + (1 if ns else 0)):
                    iV, bV, fV = BK.get()
                    is_s = (b == nblk)
                    m0, m1 = (n, ntot) if is_s else (128 * b, 128 * b + 128)
                    mm = m1 - m0
                    for kc in range(8):
                        pe_last = P.op("pe", lambda e, kc=kc, bV=bV, m0=m0, m1=m1, mm=mm: e.matmul(
                            bV[0:mm, 0:256], H[:, kc, m0:m1], WR[:, slotV:slotV + 2, kc * 128:(kc + 1) * 128],
                            start=(kc == 0), stop=(kc == 7)),
                            ([wtokV, wtokV2] + h_ready + fV) if kc == 0 else [], signal=(kc == 7))
                    if is_s:
                        state["vsf_tok"] = P.op("act", lambda e, bV=bV, mm=mm: e.activation(
                            out=VSF[0:mm, l, :], in_=bV[0:mm, 0:256], func=AF.Copy), [pe_last], signal=True)
                        v_last = P.op("act", lambda e, bV=bV, mm=mm: e.activation(
                            out=VS[0:mm, l, :], in_=bV[0:mm, 0:256], func=AF.Copy), [pe_last, c_init], signal=True)
                    else:
                        if p == npass - 1 and b == nblk - 1:
                            state["vf_tok"] = P.op("act", lambda e, bV=bV: e.activation(
                                out=VF[:, l, :], in_=bV[:, 0:256], func=AF.Copy), [pe_last], signal=True)
                        v_last = P.op("act", lambda e, bV=bV, b=b: e.activation(
                            out=VB[:, l, 1 + b, :], in_=bV[:, 0:256], func=AF.Copy),
                            [pe_last, state["kb_ready"][l]], signal=True)
                    BK.release(iV, [v_last])
                    yield
                wdone(sV, pe_last)
                wdone(sV + 1, pe_last)
                state["v_last"] = v_last

            def attention():
                rot_pt = [0, 0]

                def sample_group(k):
                    hh = k % 2
                    pr = k // 2
                    R0, R1 = 64 * hh, 64 * hh + 64
                    iS, bS, fS = BK.get()
                    r = rot_pt[0]
                    rot_pt[0] ^= 1
                    nq = ns
                    ncol = 4 * nq
                    qr = ARENA[R0:R1, 0:4096].rearrange("p (c t) -> p c t", t=TILE)[:, 4 * pr:4 * pr + 4, n:ntot]
                    P.op("pe", lambda e: e.matmul(
                        bS[:, 0:ncol], KC[R0:R1, l, pr, :], qr, start=True, stop=True), qk_ready + fS + [LDP])
                    ps = P.op("pe", lambda e: e.matmul(
                        bS[0:ns, 256:256 + ncol], KS[R0:R1, l, pr, 0:ns], qr, start=True, stop=True), [], signal=True)
                    P.op("act", lambda e: e.activation(
                        out=PT[:, r, 0:ncol], in_=bS[:, 0:ncol], func=AF.Exp, bias=ZER[:, 0:1], scale=0.125),
                        [ps] + t_free["PT"][r])
                    ap_ = P.op("act", lambda e: e.activation(
                        out=PT[0:ns, r, 256:256 + ncol], in_=bS[0:ns, 256:256 + ncol], func=AF.Exp,
                        bias=ZER[0:ns, 0:1], scale=0.125), [], signal=True)
                    BK.release(iS, [ap_])
                    iO, bO, fO = BK.get()
                    P.op("pe", lambda e: e.matmul(
                        bO[:, 0:ncol], VC[:, l, pr * 128:(pr + 1) * 128], PT[:, r, 0:ncol],
                        start=True, stop=False), [ap_] + fO + v_ready)
                    P.op("pe", lambda e: e.matmul(
                        bO[:, 0:ncol], VS[0:ns, l, pr * 128:(pr + 1) * 128], PT[0:ns, r, 256:256 + ncol],
                        start=False, stop=True))
                    P.op("pe", lambda e: e.matmul(
                        bO[:, 256:256 + ncol], ONESB[:, :], PT[:, r, 0:ncol], start=True, stop=False))
                    P.op("pe", lambda e: e.matmul(
                        bO[:, 256:256 + ncol], ONESB[0:ns, :], PT[0:ns, r, 256:256 + ncol],
                        start=False, stop=False))
                    esr_ap = ESR[:, l, k, :].rearrange("p (g q) -> p g q", q=64)[:, :, 0:nq]
                    po = P.op("pe", lambda e: e.matmul(
                        bO[:, 256:256 + ncol], ONEHOT0[:, :], esr_ap, start=False, stop=True), [c_esr], signal=True)
                    t_free["PT"][r] = [po]
                    rd = rot["dt"]
                    rot["dt"] ^= 1
                    d1 = P.op("act", lambda e: e.activation(
                        out=RD[R0:R1, rd, 0:ncol], in_=bO[R0:R1, 256:256 + ncol], func=AF.Ln),
                        [po] + t_free["DT"][rd], signal=True)
                    d2 = P.op("act", lambda e: e.activation(
                        out=RD[R0:R1, rd, 0:ncol], in_=RD[R0:R1, rd, 0:ncol], func=AF.Exp, scale=-1.0),
                        [d1], signal=True)
                    ao = ARENA[R0:R1, 4096:8192].rearrange("p (c t) -> p c t", t=TILE)[:, 4 * pr:4 * pr + 4, n:ntot]
                    d3 = P.op("dve", lambda e: e.tensor_tensor(
                        out=ao, in0=bO[R0:R1, 0:ncol].rearrange("p (g q) -> p g q", q=nq),
                        in1=RD[R0:R1, rd, 0:ncol].rearrange("p (g q) -> p g q", q=nq), op=ALU.mult),
                        [d2], signal=True)
                    t_free["DT"][rd] = [d3]
                    BK.release(iO, [d3])
                    state["ao_last"] = d3

                def stage_a(c, k):
                    hh = k % 2
                    pr = k // 2
                    R0, R1 = 64 * hh, 64 * hh + 64
                    par = c % 2
                    r = 2 * par + rot_pt[par]
                    rot_pt[par] ^= 1
                    iS, bS, fS = BK.get()
                    q0 = 64 * c
                    qr = ARENA[R0:R1, 0:4096].rearrange("p (c t) -> p c t", t=TILE)[:, 4 * pr:4 * pr + 4, q0:q0 + 64]
                    if par == 0:
                        fcol = 64 * c
                        fblk = c // 2
                        hlo = 64 * c + 128
                        hblk = c // 2 + 1
                        HR0, HR1 = 0, 64
                        hM = 64
                        full_is_halo = first_own and c == 0
                        half_is_halo = False
                    else:
                        fcol = 64 * (c + 1)
                        fblk = (c + 1) // 2
                        hlo = 64 * c - 64
                        hblk = (c - 1) // 2
                        HR0, HR1 = 64, 128
                        hM = 128
                        full_is_halo = False
                        half_is_halo = first_own and c == 1
                    P.op("pe", lambda e: e.matmul(
                        bS[:, 0:256], KB[R0:R1, l, pr, fcol:fcol + 128], qr, start=True, stop=True), qk_ready + fS)
                    ps = P.op("pe", lambda e: e.matmul(
                        bS[0:hM, 256:512], KB[R0:R1, l, pr, hlo:hlo + hM], qr, start=True, stop=True), [], signal=True)
                    bf = FLAGS[:, 0:1] if full_is_halo else ZER[:, 0:1]
                    bh = FLAGS[HR0:HR1, 0:1] if half_is_halo else ZER[HR0:HR1, 0:1]
                    P.op("act", lambda e: e.activation(
                        out=PT[:, r, 0:256], in_=bS[:, 0:256], func=AF.Exp, bias=bf, scale=0.125),
                        [ps, LD] + t_free["PT"][r])
                    ap_ = P.op("act", lambda e: e.activation(
                        out=PT[HR0:HR1, r, 256:512], in_=bS[HR0:HR1, 256:512], func=AF.Exp, bias=bh, scale=0.125),
                        [], signal=True)
                    BK.release(iS, [ap_])
                    return dict(ap=ap_, r=r, fblk=fblk, hblk=hblk, R0=R0, R1=R1, pr=pr, k=k, q0=q0)

                def stage_b(cx):
                    r, fblk, hblk, R0, R1, pr, k, q0 = (cx[x] for x in ("r", "fblk", "hblk", "R0", "R1", "pr", "k", "q0"))
                    iO, bO, fO = BK.get()
                    P.op("pe", lambda e: e.matmul(
                        bO[:, 0:256], VB[:, l, fblk, pr * 128:(pr + 1) * 128], PT[:, r, 0:256],
                        start=True, stop=False), [cx["ap"]] + fO + v_ready)
                    P.op("pe", lambda e: e.matmul(
                        bO[:, 0:256], VB[:, l, hblk, pr * 128:(pr + 1) * 128], PT[:, r, 256:512],
                        start=False, stop=True))
                    P.op("pe", lambda e: e.matmul(
                        bO[:, 256:512], ONESB[:, :], PT[:, r, 0:256], start=True, stop=False))
                    P.op("pe", lambda e: e.matmul(
                        bO[:, 256:512], ONESB[:, :], PT[:, r, 256:512], start=False, stop=False))
                    po = P.op("pe", lambda e: e.matmul(
                        bO[:, 256:512], ONEHOT0[:, :], ESR[:, l, k, :], start=False, stop=True), [c_esr], signal=True)
                    t_free["PT"][r] = [po]
                    rd = rot["dt"]
                    rot["dt"] ^= 1
                    d1 = P.op("act", lambda e: e.activation(
                        out=RD[R0:R1, rd, :], in_=bO[R0:R1, 256:512], func=AF.Ln),
                        [po] + t_free["DT"][rd], signal=True)
                    d2 = P.op("act", lambda e: e.activation(
                        out=RD[R0:R1, rd, :], in_=RD[R0:R1, rd, :], func=AF.Exp, scale=-1.0), [d1], signal=True)
                    ao = ARENA[R0:R1, 4096:8192].rearrange("p (c t) -> p c t", t=TILE)[:, 4 * pr:4 * pr + 4, q0:q0 + 64]
                    d3 = P.op("dve", lambda e: e.tensor_tensor(
                        out=ao, in0=bO[R0:R1, 0:256].rearrange("p (g q) -> p g q", q=64),
                        in1=RD[R0:R1, rd, :].rearrange("p (g q) -> p g q", q=64), op=ALU.mult), [d2], signal=True)
                    t_free["DT"][rd] = [d3]
                    BK.release(iO, [d3])
                    state["ao_last"] = d3

                prev = None
                for c in range(nchunks):
                    for k in range(NKV):
                        cx = stage_a(c, k)
                        if prev is not None:
                            stage_b(prev)
                        prev = cx
                        yield
                stage_b(prev)
                if ns:
                    for k in range(NKV):
                        sample_group(k)
                        yield

            acc_info = {}

            def conv_chain(c):
                r = rot["sg"]
                rot["sg"] ^= 1
                acc = None
                for j in range(TD):
                    w_ap = WDW[:, l, c, j:j + 1]
                    if j == 0:
                        acc = P.op("dve", lambda e: e.tensor_scalar(
                            out=SG[:, r, 0:n], in0=f32(Av(c, 0, n)), scalar1=w_ap, scalar2=vec(V_BDW, c),
                            op0=ALU.mult, op1=ALU.add), a_ready + [LD] + t_free["SG"][r], signal=True)
                        if ns:
                            acc = P.op("dve", lambda e: e.tensor_scalar(
                                out=SG[:, r, n:ntot], in0=f32(AS[:, l, c, 0:ns]), scalar1=w_ap,
                                scalar2=vec(V_BDW, c), op0=ALU.mult, op1=ALU.add), [], signal=True)
                    else:
                        accp = P.op("dve", lambda e: e.scalar_tensor_tensor(
                            out=SG[:, r, 0:n], in0=f32(Av(c, j, j + n)), scalar=w_ap, in1=SG[:, r, 0:n],
                            op0=ALU.mult, op1=ALU.add), [acc], signal=True)
                        if ns:
                            accp = P.op("dve", lambda e: e.scalar_tensor_tensor(
                                out=SG[:, r, n:ntot], in0=f32(AS[:, l, c, j:j + ns]), scalar=w_ap,
                                in1=SG[:, r, n:ntot], op0=ALU.mult, op1=ALU.add), [acc], signal=True)
                        acc = accp
                acc_info[c] = (r, acc)

            def conv():
                for c in range(NCH):
                    if c + 1 < NCH:
                        conv_chain(c + 1)
                    r, acc = acc_info[c]
                    iC, bC, fC = BK.get()
                    pe_t = None
                    for g in range(NCG):
                        s = wnext(U_CONV + 4 * c + g)
                        slot, wtok = wunit(s)
                        ntap = min(8, CONVW - TD - 8 * g)
                        for t in range(ntap):
                            j = TD + 8 * g + t
                            firstu = (t == 0)
                            pe_t = P.op("pe", lambda e, slot=slot, t=t, j=j: e.matmul(
                                bC[:, 0:n], WR[:, slot, t * 128:(t + 1) * 128], Av(c, j, j + n),
                                start=(j == TD), stop=(j == CONVW - 1)),
                                ([wtok] + (a_ready + fC if j == TD else [])) if firstu else [],
                                signal=(t == ntap - 1 and not ns))
                            if ns:
                                pe_t = P.op("pe", lambda e, slot=slot, t=t, j=j: e.matmul(
                                    bC[:, n:ntot], WR[:, slot, t * 128:(t + 1) * 128], AS[:, l, c, j:j + ns],
                                    start=False, stop=(j == CONVW - 1), skip_group_check=True),
                                    [], signal=(t == ntap - 1))
                        wdone(s, pe_t)
                        yield
                    d = P.op("dve", lambda e: e.tensor_tensor(
                        out=ZCv(c, 0, ntot), in0=bC[:, 0:ntot], in1=SG[:, r, 0:ntot], op=ALU.add),
                        [pe_t, acc], signal=True)
                    t_free["SG"][r] = [d]
                    BK.release(iC, [d])
                    state["zc_tok"][c] = d

            conv_chain(0)
            state["zc_tok"] = [None] * NCH
            for _ in conv():
                pass
            conv_pe_done = state["zc_tok"][NCH - 1]

            iM, bM, fM = BK.get()
            iQ, bQ, fQ = BK.get()
            lastm = lastq = None
            for c in range(NCH):
                r = rot["sq"]
                rot["sq"] ^= 1
                zt = state["zc_tok"][c]
                a_t = P.op("act", lambda e, c=c, r=r: e.activation(
                    out=SQ[:, r, 0:ntot], in_=f32(ZCv(c, 0, ntot)), func=AF.Square), [zt] + sq_free[r], signal=True)
                lastm = P.op("pe", lambda e, c=c: e.matmul(
                    bM[:, 0:ntot], ONES[:, :], ZCv(c, 0, ntot), start=(c == 0), stop=(c == NCH - 1)),
                    [zt] + (fM if c == 0 else []), signal=(c == NCH - 1))
                lastq = P.op("pe", lambda e, c=c, r=r: e.matmul(
                    bQ[:, 0:ntot], ONES[:, :], SQ[:, r, 0:ntot], start=(c == 0), stop=(c == NCH - 1)),
                    [a_t] + (fQ if c == 0 else []), signal=True)
                sq_free[r] = [lastq]
            dm = P.op("dve", lambda e: e.tensor_scalar(
                out=MU[:, 0:ntot], in0=bM[:, 0:ntot], scalar1=1.0 / D, scalar2=None, op0=ALU.mult),
                [lastm], signal=True)
            BK.release(iM, [dm])
            dm2 = P.op("dve", lambda e: e.tensor_tensor(
                out=M2[:, 0:ntot], in0=MU[:, 0:ntot], in1=MU[:, 0:ntot], op=ALU.mult), [dm], signal=True)
            dv = P.op("dve", lambda e: e.scalar_tensor_tensor(
                out=M2[:, 0:ntot], in0=bQ[:, 0:ntot], scalar=1.0 / D, in1=M2[:, 0:ntot],
                op0=ALU.mult, op1=ALU.subtract), [lastq, dm2], signal=True)
            BK.release(iQ, [dv])
            asd = P.op("act", lambda e: e.activation(
                out=M2[:, 0:ntot], in_=M2[:, 0:ntot], func=AF.Ln, bias=EPSB[:, 0:1], scale=1.0), [dv], signal=True)
            drs = P.op("act", lambda e: e.activation(out=RS[:, 0:ntot], in_=M2[:, 0:ntot], func=AF.Exp, scale=-0.5),
                       [asd], signal=True)
            ln_toks = []

            def ln_dve():
                for c in range(NCH):
                    r = rot["t"]
                    rot["t"] ^= 1
                    d1 = P.op("dve", lambda e, c=c, r=r: e.tensor_tensor(
                        out=T1[:, r, 0:ntot], in0=f32(ZCv(c, 0, ntot)), in1=MU[:, 0:ntot], op=ALU.subtract),
                        [dm, lastm, lastq] + t_free["T1"][r], signal=True)
                    d2 = P.op("dve", lambda e, c=c, r=r: e.tensor_tensor(
                        out=ZCv(c, 0, ntot), in0=T1[:, r, 0:ntot], in1=RS[:, 0:ntot], op=ALU.mult),
                        [d1, drs], signal=True)
                    t_free["T1"][r] = [d2]
                    ln_toks.append(d2)
                    yield

            def ln_apply():
                z_tok = None
                for c in range(NCH):
                    z_tok = P.op("act", lambda e, c=c: e.activation(
                        out=ZCv(c, 0, ntot), in_=f32(ZCv(c, 0, ntot)), func=AF.Silu, bias=vec(V_LNB, c),
                        scale=vec(V_LNG, c)), [ln_toks[c], LD], signal=True)
                    state["z_tok"] = z_tok
                yield

            alive = [qk_gen(range(8)), ln_dve()]
            while alive:
                for g_ in list(alive):
                    try:
                        next(g_)
                    except StopIteration:
                        alive.remove(g_)
            for _ in qk_gen(range(8, 10)):
                pass
            for _ in ln_apply():
                pass
            z_ready = [state["z_tok"]]
            for _ in v_gen():
                pass
            qk_ready = [state["rope_last"]]
            if l == 1:
                state["rope_free"] = [state["rope_last"]]
            v_ready = [state["v_last"]]
            for _ in attention():
                pass
            ao_ready = [state["ao_last"]]

            kb_tok = P.op("act", lambda e: e.activation(
                out=KB[:, l, :, 0:128], in_=f32(KB[:, l, :, n:n + 128]), func=AF.Copy), ao_ready, signal=True)
            nblk_ = n // 128
            vb_tok = P.op("act", lambda e: e.activation(
                out=VB[:, l, 0, :], in_=VB[:, l, nblk_, :], func=AF.Copy), ao_ready, signal=True)
            state["kb_ready"][l] = vb_tok
            a_all = ARENA[:, 12288:12288 + NCH * ACH].rearrange("p (c t) -> p c t", t=ACH)
            if p == 0:
                car2 = P.op("dve", lambda e: e.tensor_scalar(
                    out=ACAR[:, l, :, :], in0=f32(a_all[:, :, n:n + 30]), scalar1=FLAGS[:, 1:2], scalar2=None,
                    op0=ALU.mult), [conv_pe_done, LD], signal=True)
            else:
                car2 = P.op("dve", lambda e: e.tensor_copy(
                    out=ACAR[:, l, :, :], in_=f32(a_all[:, :, n:n + 30])), [conv_pe_done], signal=True)

            if p == npass - 1:
                state["out_tokens"].append(P.dma("pool", lambda e: e.dma_start(
                    out=kpT[l], in_=f32(KB[:, l, :, 0:128])), fo_sem, [kb_tok]))
                state["out_tokens"].append(P.dma("sp", lambda e: e.dma_start(
                    out=vp[l], in_=VF[:, l, :]), fo_sem, [state["vf_tok"]]))
                state["out_tokens"].append(P.dma("pool", lambda e: e.dma_start(
                    out=cpT[l], in_=f32(ACAR[:, l, :, :])), fo_sem, [car2]))
            if ns:
                state["out_tokens"].append(P.dma("pool", lambda e: e.dma_start(
                    out=ksT[l], in_=f32(KS[:, l, :, :])), fo_sem, qk_ready))
                state["out_tokens"].append(P.dma("sp", lambda e: e.dma_start(
                    out=vs[l], in_=VSF[0:ns, l, :]), fo_sem, [state["vsf_tok"]]))
                state["out_tokens"].append(P.dma("pool", lambda e: e.dma_start(
                    out=csT[l], in_=f32(AS[:, l, :, ns:ns + 30])), fo_sem, a_ready))

            mg_tok = None
            mg_toks = []
            for j in range(NCH):
                lM = U_MERGE + 4 * j
                iA, bA, fA = BK.get()
                iC, bC, fC = BK.get()
                iG, bG, fG = BK.get()
                iH, bH, fH = BK.get()
                pG = proj_group(wnext(lM + 2), ntot, hch, 8, h_ready, bG, fG)
                pH = proj_group(wnext(lM + 3), ntot, hch, 8, h_ready, bH, fH)
                pA = proj_group(wnext(lM), ntot, lambda kc: AOv(kc, 0, ntot), 8, ao_ready, bA, fA)
                pC = proj_group(wnext(lM + 1), ntot, lambda kc: ZCv(kc, 0, ntot), 8, z_ready, bC, fC)
                r = rot["sg"]
                rot["sg"] ^= 1
                r2 = rot["t"]
                rot["t"] ^= 1
                a1 = P.op("act", lambda e, bG=bG, r=r: e.activation(
                    out=SG[:, r, 0:ntot], in_=bG[:, 0:ntot], func=AF.Sigmoid), [pG] + t_free["SG"][r], signal=True)
                a2 = P.op("act", lambda e, bH=bH, r2=r2: e.activation(
                    out=T2[:, r2, 0:ntot], in_=bH[:, 0:ntot], func=AF.Sigmoid), [pH] + t_free["T2"][r2], signal=True)
                BK.release(iG, [a1])
                BK.release(iH, [a2])
                d1 = P.op("dve", lambda e, bA=bA, r=r: e.tensor_tensor(
                    out=SG[:, r, 0:ntot], in0=bA[:, 0:ntot], in1=SG[:, r, 0:ntot], op=ALU.mult), [pA, a1], signal=True)
                d2 = P.op("dve", lambda e, bC=bC, r2=r2: e.tensor_tensor(
                    out=T2[:, r2, 0:ntot], in0=bC[:, 0:ntot], in1=T2[:, r2, 0:ntot], op=ALU.mult), [pC, a2], signal=True)
                BK.release(iA, [d1])
                BK.release(iC, [d2])
                mg_tok = P.op("dve", lambda e, j=j, r=r, r2=r2: e.tensor_tensor(
                    out=Qv(j, 0, ntot), in0=SG[:, r, 0:ntot], in1=T2[:, r2, 0:ntot], op=ALU.add),
                    [d1, d2] + ao_ready, signal=True)
                t_free["SG"][r] = [mg_tok]
                t_free["T2"][r2] = [mg_tok]
                mg_toks.append(mg_tok)
            mg_ready = [mg_tok]

            st2 = Stats(ntot)
            xt = [None] * NCH
            for j in range(NCH):
                s = wnext(U_OUT + j)
                iA, bA, fA = BK.get()
                pA = proj_group(s, ntot, lambda kc: Qv(kc, 0, ntot), 8, (lambda kc: [mg_toks[kc]]) if j == 0 else mg_ready, bA, fA)
                xt[j] = P.op("dve", lambda e, bA=bA, j=j: e.tensor_tensor(
                    out=X[:, j, 0:ntot], in0=bA[:, 0:ntot], in1=X[:, j, 0:ntot], op=ALU.add), [pA, h_tok], signal=True)
                BK.release(iA, [xt[j]])
                if j >= 1:
                    st2.add(X[:, j - 1, 0:ntot], [xt[j - 1]])
            st2.add(X[:, NCH - 1, 0:ntot], [xt[NCH - 1]])
            x2_ready = [xt[NCH - 1]]
            d_rs = st2.finish()

            h2_tok = None
            h2_toks = []
            for c in range(NCH):
                h2_tok = P.op("dve", lambda e, c=c: e.scalar_tensor_tensor(
                    out=H[:, c, 0:ntot], in0=X[:, c, 0:ntot], scalar=vec(V_NMLP, c), in1=RS[:, 0:ntot],
                    op0=ALU.mult, op1=ALU.mult), [d_rs] + x2_ready, signal=True)
                h2_toks.append(h2_tok)
            h2_ready = [h2_tok]

            hid_tok = None
            hid_toks = []
            for i in range(32):
                s = wnext(U_UP + i)
                iA, bA, fA = BK.get()
                pA = proj_group(s, ntot, hch, 8, (lambda kc: [h2_toks[kc]]) if i == 0 else h2_ready, bA, fA)
                r = rot["sg"]
                rot["sg"] ^= 1
                a1 = P.op("act", lambda e, bA=bA, r=r: e.activation(
                    out=SG[:, r, 0:ntot], in_=bA[:, 0:ntot], func=AF.Relu), [pA] + t_free["SG"][r], signal=True)
                BK.release(iA, [a1])
                hid_tok = P.op("dve", lambda e, i=i, r=r: e.tensor_tensor(
                    out=HIDv(i, 0, ntot), in0=SG[:, r, 0:ntot], in1=SG[:, r, 0:ntot], op=ALU.mult),
                    [a1, car2, kb_tok, vb_tok], signal=True)
                t_free["SG"][r] = [hid_tok]
                hid_toks.append(hid_tok)
            hid_ready = [hid_tok]

            st3 = Stats(ntot)
            xt = [None] * NCH
            for j in range(NCH):
                iA, bA, fA = BK.get()
                pA = None
                for g in range(4):
                    s = wnext(U_DOWN + 4 * j + g)
                    pA = proj_group(s, ntot, lambda kc, g=g: HIDv(8 * g + kc, 0, ntot), 8,
                                    (lambda kc, g=g: [hid_toks[8 * g + kc]]) if j == 0 else hid_ready, bA,
                                    fA if g == 0 else [], start=(g == 0), stop=(g == 3))
                xt[j] = P.op("dve", lambda e, bA=bA, j=j: e.tensor_tensor(
                    out=X[:, j, 0:ntot], in0=bA[:, 0:ntot], in1=X[:, j, 0:ntot], op=ALU.add), [pA, h2_tok], signal=True)
                BK.release(iA, [xt[j]])
                if j >= 1:
                    st3.add(X[:, j - 1, 0:ntot], [xt[j - 1]])
            st3.add(X[:, NCH - 1, 0:ntot], [xt[NCH - 1]])
            return [xt[NCH - 1]], st3.finish()

        for p in range(npass):
            if p == 0:
                n, ns = HALO, NSMP
                tok0, rcol0 = 0, 0
            else:
                n, ns = TILE, 0
                tok0 = HALO + TILE * (p - 1)
                rcol0 = HALO + NSMP + TILE * (p - 1)
            ntot = n + ns
            xsrc = xT.rearrange("(c q) t -> q c t", q=128)[:, :, tok0:tok0 + n]
            xt = P.dma("sp", lambda e, xsrc=xsrc, n=n: e.dma_start(out=X[:, :, 0:n], in_=xsrc), x_sem, state["x_free"])
            x_ready = [xt]
            if ns:
                xs_src = xsT.rearrange("(c q) t -> q c t", q=128)
                xt2 = P.dma("sp", lambda e, xs_src=xs_src, n=n, ntot=ntot: e.dma_start(
                    out=X[:, :, n:ntot], in_=xs_src), x_sem, state["x_free"])
                x_ready = [xt2]
            rt1 = P.dma("sp", lambda e, rcol0=rcol0, ntot=ntot: e.dma_start(
                out=RC[:, 0:ntot], in_=ropec_d[:, rcol0:rcol0 + ntot]), r_sem, state["rope_free"])
            rt2 = P.dma("sp", lambda e, rcol0=rcol0, ntot=ntot: e.dma_start(
                out=RSN[:, 0:ntot], in_=ropes_d[:, rcol0:rcol0 + ntot]), r_sem, state["rope_free"])
            rope_ready = [rt2]

            xr = x_ready
            d_rs = None
            for l in range(2):
                xr, d_rs = layer(p, l, n, ns, xr, rope_ready, d_rs)

            y_tok = None
            for c in range(NCH):
                y_tok = P.op("dve", lambda e, c=c, ntot=ntot: e.scalar_tensor_tensor(
                    out=X[:, c, 0:ntot], in0=X[:, c, 0:ntot], scalar=vec(80, c), in1=RS[:, 0:ntot],
                    op0=ALU.mult, op1=ALU.mult), [d_rs] + xr, signal=True)
            if p == 0:
                st = P.dma("sp", lambda e, n=n, ntot=ntot: e.dma_start(
                    out=ysT.rearrange("(c q) t -> q c t", q=128), in_=X[:, :, n:ntot]), o_sem, [y_tok])
            else:
                o0 = TILE * (p - 1)
                st = P.dma("sp", lambda e, o0=o0, n=n: e.dma_start(
                    out=ypT.rearrange("(c q) t -> q c t", q=128)[:, :, o0:o0 + n], in_=X[:, :, 0:n]),
                    o_sem, [y_tok])
            state["x_free"] = [st]

        finals = [(o_sem[0], o_sem[1]), (fo_sem[0], fo_sem[1])]

        with nc.Block() as block:
            @block.tensor
            def _(e):
                P.run("pe", e)

            @block.scalar
            def _(e):
                P.run("act", e)

            @block.vector
            def _(e):
                P.run("dve", e)

            @block.gpsimd
            def _(e):
                P.run("pool", e)

            @block.sync
            def _(e):
                P.run("sp", e, final_waits=finals)
    return nc, order


def _units_for_layer(w_in, w_o, w_dw, w_pw2, w_out, w_up, w_down):
    U = np.zeros((NU, 128, 1024), np.float32)

    def put(u, mat):
        U[u] = mat.reshape(8, 128, 128).transpose(1, 0, 2).reshape(128, 1024)

    m = np.arange(128)
    for cc in range(8):
        heads = np.array([q_head(cc, 0)] * 64 + [q_head(cc, 1)] * 64)
        dd = m % 64
        put(U_Q + 2 * cc, w_in[:, heads * 64 + dd])
        put(U_Q + 2 * cc + 1, w_in[:, heads * 64 + (dd + 32) % 64])
    for cc in range(2):
        base = 1024 + cc * 128
        put(U_K + 2 * cc, w_in[:, base + m])
        put(U_K + 2 * cc + 1, w_in[:, base + (m // 64) * 64 + (m % 64 + 32) % 64])
    for cc in range(2):
        put(U_V + cc, w_in[:, 1280 + cc * 128 + m])
    for c in range(8):
        put(U_U + 2 * c, w_in[:, 1536 + c * 128 + m])
        put(U_U + 2 * c + 1, w_in[:, 2560 + c * 128 + m])
    for c in range(8):
        for g in range(4):
            blk = np.zeros((128, 8, 128), np.float32)
            for t in range(8):
                j = TD + 8 * g + t
                if j < CONVW:
                    blk[m, t, m] = w_dw[j, c * 128 + m]
            U[U_CONV + 4 * c + g] = blk.reshape(128, 1024)
    rows = np.zeros(1024, np.int64)
    for cc in range(8):
        for half in range(2):
            rows[cc * 128 + half * 64: cc * 128 + half * 64 + 64] = q_head(cc, half) * 64 + np.arange(64)
    w_o_p = w_o[rows, :]
    for j in range(8):
        put(U_MERGE + 4 * j, w_o_p[:, j * 128 + m])
        put(U_MERGE + 4 * j + 1, w_pw2[:, j * 128 + m])
        put(U_MERGE + 4 * j + 2, w_in[:, 3584 + j * 128 + m])
        put(U_MERGE + 4 * j + 3, w_in[:, 4608 + j * 128 + m])
        put(U_OUT + j, w_out[:, j * 128 + m])
    for i in range(32):
        put(U_UP + i, w_up[:, i * 128 + m])
    for j in range(8):
        for g in range(4):
            put(U_DOWN + 4 * j + g, w_down[g * 1024:(g + 1) * 1024, j * 128 + m])
    return U


def _vec_cols(v):
    return np.ascontiguousarray(v.reshape(8, 128).T)


def _rope_tables(pos):
    half = HD // 2
    inv_freq = (np.float32(10000.0) ** (-np.arange(half, dtype=np.float32) / np.float32(half))).astype(np.float32)
    ang = pos.astype(np.float32)[None, :] * inv_freq[:, None]
    cos = np.cos(ang).astype(np.float32)
    sin = np.sin(ang).astype(np.float32)
    c64 = np.concatenate([cos, cos], 0)
    s64 = np.concatenate([-sin, sin], 0)
    return np.concatenate([c64, c64], 0), np.concatenate([s64, s64], 0)


_CACHE = {}


def kernel(x_prompt, x_sample, cache_k, cache_v, state_conv, norm_mix, w_in, sinks, w_o_attn,
           w_dw, b_dw, ln_conv_g, ln_conv_b, w_pw2, w_out, norm_mlp, w_up, w_down, norm_final):
    f = lambda a: np.asarray(a, dtype=np.float32)
    x_prompt, x_sample, cache_k, cache_v, state_conv = map(f, (x_prompt, x_sample, cache_k, cache_v, state_conv))
    norm_mix, w_in, sinks, w_o_attn, w_dw, b_dw = map(f, (norm_mix, w_in, sinks, w_o_attn, w_dw, b_dw))
    ln_conv_g, ln_conv_b, w_pw2, w_out, norm_mlp, w_up, w_down, norm_final = map(
        f, (ln_conv_g, ln_conv_b, w_pw2, w_out, norm_mlp, w_up, w_down, norm_final))
    B, SEQ, _ = x_prompt.shape
    own = SEQ // 2
    n_own = own // TILE
    ncores = 8
    assert B * 2 == ncores and x_sample.shape[0] == ncores
    if n_own not in _CACHE:
        _CACHE[n_own] = build_program(n_own)
    nc, order = _CACHE[n_own]

    assert len(order) == NUS and len(set(order)) == NUS
    wst = np.stack([_units_for_layer(w_in[l], w_o_attn[l], w_dw[l], w_pw2[l], w_out[l], w_up[l], w_down[l])[order]
                    for l in range(2)], 0)
    vecs = np.zeros((128, NVEC), np.float32)
    for l in range(2):
        for i, v in enumerate((norm_mix[l], b_dw[l], ln_conv_g[l], ln_conv_b[l], norm_mlp[l])):
            vecs[:, 40 * l + 8 * i: 40 * l + 8 * i + 8] = _vec_cols(v)
    vecs[:, 80:88] = _vec_cols(norm_final)
    wdw = np.ascontiguousarray(w_dw.reshape(2, CONVW, NCH, 128).transpose(3, 0, 2, 1)[:, :, :, :TD])
    sinks_b = np.ascontiguousarray(np.broadcast_to(sinks.reshape(1, 32), (128, 32))).astype(np.float32)

    mm_ = np.arange(128)
    perm = np.zeros((128, 128), np.float32)
    perm[(mm_ // 64) * 64 + (mm_ % 64 + 32) % 64, mm_] = 1.0
    in_maps = []
    for core in range(ncores):
        b, hf = core // 2, core % 2
        start = hf * own
        xT = np.zeros((D, HALO + own), np.float32)
        if hf == 1:
            xT[:, :] = x_prompt[b, start - HALO:start + own, :].T
        else:
            xT[:, HALO:] = x_prompt[b, 0:own, :].T
        pos = np.concatenate([np.arange(start - HALO, start), PAST + np.arange(NSMP), np.arange(start, start + own)])
        rc, rs = _rope_tables(pos.astype(np.float32))
        flags = np.zeros((128, 4), np.float32)
        flags[0, 2] = 1.0
        flags[:, 0] = 0.0 if hf == 1 else -30000.0
        flags[:, 1] = 1.0 if hf == 1 else 0.0
        in_maps.append({
            "xT": xT,
            "xsT": np.ascontiguousarray(x_sample[core].T),
            "wst": wst,
            "vecs": vecs,
            "wdw": wdw,
            "ropec": rc, "ropes": rs,
            "kcT": np.ascontiguousarray(cache_k[:, core].reshape(2, 128, 256).transpose(0, 2, 1)
                                        .reshape(2, 2, 128, 128).transpose(0, 2, 1, 3)),
            "vc": np.ascontiguousarray(cache_v[:, core].reshape(2, 128, 256)),
            "scT": np.ascontiguousarray(state_conv[:, core].transpose(0, 2, 1).reshape(2, 8, 128, 30)
                                        .transpose(0, 2, 1, 3)),
            "sinks": sinks_b,
            "perm": perm,
            "flags": flags,
        })
    res = run_bass_kernel_spmd(nc, in_maps, core_ids=list(range(ncores)))
    R = res.results

    y_prompt = np.zeros((B, SEQ, D), np.float32)
    y_sample = np.zeros((ncores, NSMP, D), np.float32)
    k_p = np.zeros((2, B, 128, NKV, HD), np.float32)
    v_p = np.zeros((2, B, 128, NKV, HD), np.float32)
    c_p = np.zeros((2, B, 30, D), np.float32)
    k_s = np.zeros((2, ncores, NSMP, NKV, HD), np.float32)
    v_s = np.zeros((2, ncores, NSMP, NKV, HD), np.float32)
    c_s = np.zeros((2, ncores, 30, D), np.float32)
    for core in range(ncores):
        b, hf = core // 2, core % 2
        r = R[core]
        y_prompt[b, hf * own:(hf + 1) * own, :] = r["ypT"].T
        y_sample[core] = r["ysT"].T
        k_s[:, core] = r["ksT"].transpose(0, 3, 2, 1).reshape(2, NSMP, NKV, HD)
        v_s[:, core] = r["vs"].reshape(2, NSMP, NKV, HD)
        c_s[:, core] = r["csT"].transpose(0, 3, 2, 1).reshape(2, 30, D)
        if hf == 1:
            k_p[:, b] = r["kpT"].transpose(0, 3, 2, 1).reshape(2, 128, NKV, HD)
            v_p[:, b] = r["vp"].reshape(2, 128, NKV, HD)
            c_p[:, b] = r["cpT"].transpose(0, 3, 2, 1).reshape(2, 30, D)
    return (y_prompt, y_sample, k_p, v_p, c_p, k_s, v_s, c_s)
```

```python
import contextlib
import numpy as np
import concourse.bass as bass
import concourse.mybir as mybir
from concourse.bass_utils import run_bass_kernel_spmd

F32 = mybir.dt.float32
F32R = mybir.dt.float32r
MMDT = mybir.dt.bfloat16
BF16 = mybir.dt.bfloat16
AF = mybir.ActivationFunctionType
ALU = mybir.AluOpType

D = 1024
NCH = 8
HD = 64
NQH = 16
NKV = 4
CONVW = 31
DFF = 4096
NIN = 5632
EPS = 1e-6
HALO = 256
TILE = 512
NSMP = 16
PAST = 2048
NU = 174
TD = 7
NCG = 3
NUS = 156
U_Q, U_K, U_V, U_U, U_CONV, U_MERGE, U_OUT, U_UP, U_DOWN = 0, 16, 20, 22, 38, 70, 102, 110, 142
NS = 16
ACH = 544
NVEC = 88
N_OWN_TILES = 8


def q_head(cc, half):
    return (cc + 4 * half) if cc < 4 else (8 + (cc - 4) + 4 * half)


class _Rec:
    def __init__(self):
        self.call = None

    def __getattr__(self, name):
        def f(*a, **k):
            self.call = (name, a, k)
            return None
        return f


def _record(fn):
    r = _Rec()
    fn(r)
    assert r.call is not None
    return r.call


class Prog:
    ENGS = ("pe", "act", "dve", "pool", "sp")

    def __init__(self, nc, es):
        self.nc = nc
        self.q = {e: [] for e in self.ENGS}
        self.sem = {e: es.enter_context(nc.semaphore("prog_" + e)) for e in self.ENGS}
        self.cnt = {e: 0 for e in self.ENGS}

    def op(self, eng, fn, waits=(), signal=False):
        tok = None
        if signal:
            self.cnt[eng] += 1
            tok = (self.sem[eng], self.cnt[eng])
        ws = []
        for w in waits:
            if w is None:
                continue
            if isinstance(w, list):
                ws.extend([x for x in w if x is not None])
            else:
                ws.append(w)
        self.q[eng].append((_record(fn), tuple(ws), tok, 1))
        return tok

    def dma(self, eng, fn, dsem, waits=()):
        dsem[1] += 16
        tok = (dsem[0], dsem[1])
        ws = []
        for w in waits:
            if w is None:
                continue
            if isinstance(w, list):
                ws.extend([x for x in w if x is not None])
            else:
                ws.append(w)
        self.q[eng].append((_record(fn), tuple(ws), tok, 16))
        return tok

    def run(self, eng, e, final_waits=()):
        seen = {}
        for (mname, margs, mkw), waits, tok, inc in self.q[eng]:
            for (sem, val) in waits:
                k = sem.num
                if seen.get(k, 0) >= val:
                    continue
                e.wait_ge(sem, val)
                seen[k] = val
            ins = getattr(e, mname)(*margs, **mkw)
            if tok is not None:
                ins.then_inc(tok[0], inc)
        for (sem, val) in final_waits:
            e.wait_ge(sem, val)


class Banks:
    def __init__(self, tensors):
        self.t = tensors
        self.free = [[] for _ in tensors]
        self.held = [False] * len(tensors)
        self.i = 0

    def get(self):
        n = len(self.t)
        for _ in range(n):
            i = self.i
            self.i = (i + 1) % n
            if not self.held[i]:
                self.held[i] = True
                fr = self.free[i]
                self.free[i] = []
                return i, self.t[i], fr
        raise RuntimeError("no free psum bank")

    def release(self, i, toks):
        self.held[i] = False
        self.free[i] = [t for t in toks if t is not None]


def build_program(n_own):
    npass = 1 + n_own
    ntok = HALO + TILE * n_own
    nc = bass.Bass("TRN2", target_bir_lowering=False)

    def din(name, shape):
        return nc.dram_tensor(name, list(shape), F32, kind="ExternalInput").ap()

    def dout(name, shape):
        return nc.dram_tensor(name, list(shape), F32, kind="ExternalOutput").ap()

    xT = din("xT", [D, ntok])
    xsT = din("xsT", [D, NSMP])
    wst = din("wst", [2, NUS, 128, 1024])
    perm_d = din("perm", [128, 128])
    vecs_d = din("vecs", [128, NVEC])
    wdw_d = din("wdw", [128, 2, NCH, TD])
    ropec_d = din("ropec", [128, ntok + NSMP])
    ropes_d = din("ropes", [128, ntok + NSMP])
    kcT_d = din("kcT", [2, 128, 2, 128])
    vc_d = din("vc", [2, 128, 256])
    scT_d = din("scT", [2, 128, 8, 30])
    sinks_d = din("sinks", [128, 32])
    flags_d = din("flags", [128, 4])

    ypT = dout("ypT", [D, TILE * n_own])
    ysT = dout("ysT", [D, NSMP])
    kpT = dout("kpT", [2, 128, 2, 128])
    vp = dout("vp", [2, 128, 256])
    cpT = dout("cpT", [2, 128, 8, 30])
    ksT = dout("ksT", [2, 128, 2, NSMP])
    vs = dout("vs", [2, NSMP, 256])
    csT = dout("csT", [2, 128, 8, 30])

    es = contextlib.ExitStack()
    with es:
        def sb(name, shape, dt=F32):
            return es.enter_context(nc.sbuf_tensor(name, list(shape), dt))

        XX = [sb("XA", [128, NCH, TILE]), sb("XB", [128, NCH, TILE])]
        X = XX[0]
        H = sb("H", [128, NCH, TILE], MMDT)
        ARENA = sb("ARENA", [128, 12288 + NCH * ACH], MMDT)
        KB = sb("KB", [128, 2, 2, 128 + TILE], MMDT)
        VB = sb("VB", [128, 2, 5, 256], BF16)
        VF = sb("VF", [128, 2, 256])
        VSF = sb("VSF", [128, 2, 256])
        ACAR = sb("ACAR", [128, 2, NCH, 30], MMDT)
        AS = sb("AS", [128, 2, NCH, 30 + NSMP], MMDT)
        KC = sb("KC", [128, 2, 2, 128], MMDT)
        VC = sb("VC", [128, 2, 256], BF16)
        KS = sb("KS", [128, 2, 2, NSMP], MMDT)
        VS = sb("VS", [128, 2, 256], BF16)
        PT = sb("PT", [128, 4, 512], BF16)
        T1 = sb("T1", [128, 2, TILE])
        T2 = sb("T2", [128, 2, TILE])
        SG = sb("SG", [128, 2, TILE])
        SQ = sb("SQ", [128, 2, TILE], MMDT)
        QS = sb("QS", [128, 2, TILE], MMDT)
        RS = sb("RS", [128, TILE])
        RSF = sb("RSF", [128, TILE])
        MUF = sb("MUF", [128, TILE])
        MU = sb("MU", [128, TILE])
        M2 = sb("M2", [128, TILE])
        RD = sb("RD", [128, 2, 256])
        RC = sb("RC", [128, TILE])
        RSN = sb("RSN", [128, TILE])
        ESK = sb("ESK", [128, 32])
        ESR = sb("ESR", [128, 2, 4, 256], BF16)
        ONES = sb("ONES", [128, 128], MMDT)
        ONEHOT0 = sb("ONEHOT0", [128, 128], BF16)
        ONESB = sb("ONESB", [128, 128], BF16)
        PERM = sb("PERM", [128, 128], MMDT)
        ZER = sb("ZER", [128, 64])
        VECS = sb("VECS", [128, NVEC])
        WDW = sb("WDW", [128, 2, NCH, TD])
        FLAGS = sb("FLAGS", [128, 4])
        EPSB = sb("EPSB", [128, 1])
        WR = sb("WR", [128, NS, 1024], MMDT)

        banks_t = [es.enter_context(nc.psum_tensor("bank%d" % i, [128, 512], F32)) for i in range(8)]
        BK = Banks(banks_t)

        P = Prog(nc, es)

        def newsem(name):
            return [es.enter_context(nc.semaphore(name)), 0]

        wsem = [newsem("wsem%d" % i) for i in range(NS // 2)]
        ld_sem = newsem("ld")
        ldp_sem = newsem("ldp")
        x_sem = newsem("xs")
        r_sem = newsem("rs")
        o_sem = newsem("os")
        fo_sem = newsem("fo")

        def Qv(c, a, b):
            return ARENA[:, c * TILE + a: c * TILE + b]

        def AOv(c, a, b):
            return ARENA[:, 4096 + c * TILE + a: 4096 + c * TILE + b]

        def ZCv(c, a, b):
            return ARENA[:, 8192 + c * TILE + a: 8192 + c * TILE + b]

        def Av(c, a, b):
            return ARENA[:, 12288 + c * ACH + a: 12288 + c * ACH + b]

        def HIDv(i, a, b):
            return ARENA[:, i * TILE + a: i * TILE + b]

        def f32(ap):
            return ap.bitcast(F32) if ap.dtype == F32R else ap

        total_units = npass * 2 * NUS

        def issue_wdma(s0, waits):
            rem = s0 % (2 * NUS)
            l, u = rem // NUS, rem % NUS
            slot = s0 % NS
            pair = slot // 2
            src = wst[l, u:u + 2].rearrange("u p f -> p u f")
            dst = WR[:, slot:slot + 2, :]
            P.dma("pool", lambda e, dst=dst, src=src: e.dma_start(out=dst, in_=src), wsem[pair], waits)

        def wunit(s):
            slot = s % NS
            pair = slot // 2
            fill = s // NS + 1
            return slot, (wsem[pair][0], 16 * fill)

        def wdone(s, tok):
            if s % 2 == 1:
                nxt = s - 1 + NS
                if nxt < total_units:
                    issue_wdma(nxt, [tok])

        order = []
        wcount = [0]

        def wnext(logical):
            s_ = wcount[0]
            wcount[0] += 1
            if s_ < NUS:
                order.append(logical)
            else:
                assert order[s_ % NUS] == logical, (s_, logical)
            return s_

        P.dma("sp", lambda e: e.dma_start(out=VECS[:, :], in_=vecs_d[:, :]), ld_sem)
        P.dma("sp", lambda e: e.dma_start(out=FLAGS[:, :], in_=flags_d[:, :]), ld_sem)
        P.dma("sp", lambda e: e.dma_start(out=ESK[:, :], in_=sinks_d[:, :]), ld_sem)
        P.dma("sp", lambda e: e.dma_start(out=WDW[:, :, :, :], in_=wdw_d[:, :, :, :]), ld_sem)
        LD = (ld_sem[0], ld_sem[1])
        P.dma("pool", lambda e: e.dma_start(out=KC[:, :, :, :], in_=kcT_d.rearrange("l p c t -> p l c t")), ldp_sem)
        P.dma("pool", lambda e: e.dma_start(out=VC[:, :, :], in_=vc_d.rearrange("l p f -> p l f")), ldp_sem)
        for l in range(2):
            P.dma("pool", lambda e, l=l: e.dma_start(out=AS[:, l, :, 0:30], in_=scT_d[l]), ldp_sem)
        P.dma("pool", lambda e: e.dma_start(out=PERM[:, :], in_=perm_d[:, :]), ldp_sem)
        LDP = (ldp_sem[0], ldp_sem[1])
        for s0 in range(0, NS, 2):
            issue_wdma(s0, [])

        P.op("dve", lambda e: e.memset(T1[:, :, :], 0.0))
        P.op("dve", lambda e: e.memset(ZER[:, :], 0.0))
        P.op("dve", lambda e: e.memset(EPSB[:, :], EPS))
        z_src = T1[:, :, :].rearrange("p a b -> p (a b)")
        c_ones = P.op("dve", lambda e: e.tensor_scalar(out=ONES[:, :], in0=z_src[:, 0:128], scalar1=1.0,
                                                        scalar2=None, op0=ALU.add), signal=True)

        P.op("dve", lambda e: e.tensor_scalar(out=ONESB[:, :], in0=z_src[:, 0:128], scalar1=1.0,
                                              scalar2=None, op0=ALU.add))
        P.op("dve", lambda e: e.tensor_scalar(out=ONEHOT0[:, :], in0=z_src[:, 0:128], scalar1=FLAGS[:, 2:3],
                                              scalar2=None, op0=ALU.add), [LD])

        def zero_fill(flat, ncols):
            tok = None
            for c0 in range(0, ncols, 1024):
                c1 = min(ncols, c0 + 1024)
                tok = P.op("dve", lambda e, c0=c0, c1=c1: e.tensor_copy(out=flat[:, c0:c1], in_=z_src[:, 0:c1 - c0]),
                           signal=True)
            return tok

        zero_fill(KB[:, :, :, :].rearrange("p a b c -> p (a b c)"), 2 * 2 * (128 + TILE))
        zero_fill(VB[:, :, :, :].rearrange("p a b c -> p (a b c)"), 2 * 5 * 256)
        zero_fill(ACAR[:, :, :, :].rearrange("p a b c -> p (a b c)"), 2 * NCH * 30)
        zero_fill(PT[:, :, :].rearrange("p a b -> p (a b)"), 2048)
        c_init = zero_fill(VS[:, :, :].rearrange("p a b -> p (a b)"), 512)
        c_esk = P.op("act", lambda e: e.activation(out=ESK[:, :], in_=ESK[:, :], func=AF.Exp), [LD], signal=True)
        c_esr = None
        for l in range(2):
            for k in range(NKV):
                for g in range(4):
                    i = l * 16 + 4 * k + g
                    c_esr = P.op("dve", lambda e, l=l, k=k, g=g, i=i: e.tensor_scalar(
                        out=ESR[:, l, k, g * 64:(g + 1) * 64], in0=ZER[:, :], scalar1=ESK[:, i:i + 1], scalar2=None,
                        op0=ALU.add), [c_esk], signal=True)

        state = {
            "x_free": [],
            "rope_free": [],
            "y_store": None,
            "kb_ready": [None, None],
            "out_tokens": [],
        }
        sq_free = [[], []]
        qs_free = [[], []]
        t_free = {"T1": [[], []], "T2": [[], []], "SG": [[], []], "PT": [[], [], [], []], "DT": [[], []]}
        rot = {"sq": 0, "qs": 0, "t": 0, "sg": 0, "pt": 0, "dt": 0}

        def vec(base, c):
            return VECS[:, base + c: base + c + 1]

        class Stats:
            def __init__(self, ntot, mu=None, rs=None):
                self.mu = MU if mu is None else mu
                self.rs = RS if rs is None else rs
                self.ntot = ntot
                self.bi, self.bank, self.bfree = BK.get()
                self.n = 0
                self.last = None

            def add(self, src_ap, waits):
                ntot, bank, c = self.ntot, self.bank, self.n
                r = rot["sq"]
                rot["sq"] ^= 1
                a_t = P.op("act", lambda e: e.activation(out=SQ[:, r, 0:ntot], in_=src_ap, func=AF.Square),
                           list(waits) + sq_free[r], signal=True)
                self.last = P.op("pe", lambda e: e.matmul(
                    bank[:, 0:ntot], ONES[:, :], SQ[:, r, 0:ntot], start=(c == 0), stop=(c == NCH - 1)),
                    [a_t, c_ones] + (self.bfree if c == 0 else []), signal=True)
                sq_free[r] = [self.last]
                self.n += 1

            def finish(self):
                ntot, bank, mu_, rs_ = self.ntot, self.bank, self.mu, self.rs
                assert self.n == NCH
                a2 = P.op("act", lambda e: e.activation(out=mu_[:, 0:ntot], in_=bank[:, 0:ntot], func=AF.Ln,
                                                         bias=EPSB[:, 0:1], scale=1.0 / D), [self.last], signal=True)
                BK.release(self.bi, [a2])
                return P.op("act", lambda e: e.activation(out=rs_[:, 0:ntot], in_=mu_[:, 0:ntot], func=AF.Exp,
                                                          scale=-0.5), [a2], signal=True)

        def rms_stats(src_chunk, ntot, src_waits):
            st = Stats(ntot)
            for c in range(NCH):
                st.add(src_chunk(c), src_waits)
            return st.finish()

        def proj_group(s, ntot, rhs_chunk, nk, rhs_waits, bank, bfree, start=True, stop=True, col0=0):
            slot, wtok = wunit(s)
            last = None
            for kc in range(nk):
                first = (kc == 0)
                last = P.op("pe", lambda e, kc=kc, slot=slot, first=first: e.matmul(
                    bank[:, col0:col0 + ntot], WR[:, slot, kc * 128:(kc + 1) * 128], rhs_chunk(kc),
                    start=(start and first), stop=(stop and kc == nk - 1)),
                    (rhs_waits(kc) + ([wtok] + bfree if first else [])) if callable(rhs_waits)
                    else (([wtok] + rhs_waits + bfree) if first else []), signal=(kc == nk - 1))
            wdone(s, last)
            return last

        def layer(p, l, n, ns, x_ready, rope_ready, d_rs_in, pre_h=None):
            X = XX[p % 2]
            ntot = n + ns
            vb = 40 * l
            V_NM, V_BDW, V_LNG, V_LNB, V_NMLP, V_NF = vb, vb + 8, vb + 16, vb + 24, vb + 32, 80
            nchunks = n // 64
            first_own = (p == 1)

            if pre_h is not None:
                h_toks = list(pre_h)
                h_tok = h_toks[-1]
            else:
                d_rs = d_rs_in if d_rs_in is not None else rms_stats(lambda c: X[:, c, 0:ntot], ntot, x_ready)
                h_tok = None
                h_toks = []
                for c in range(NCH):
                    h_tok = P.op("dve", lambda e, c=c: e.scalar_tensor_tensor(
                        out=H[:, c, 0:ntot], in0=X[:, c, 0:ntot], scalar=vec(V_NM, c), in1=RS[:, 0:ntot],
                        op0=ALU.mult, op1=ALU.mult), [d_rs, LD] + x_ready, signal=True)
                    h_toks.append(h_tok)
            h_ready = [h_tok]
            h_first = [True]

            def h_waits(kc):
                return [h_toks[kc]]

            def hch(kc):
                return H[:, kc, 0:ntot]

            car_tok = P.op("act", lambda e: e.activation(
                out=ARENA[:, 12288:12288 + NCH * ACH].rearrange("p (c t) -> p c t", t=ACH)[:, :, 0:30],
                in_=f32(ACAR[:, l, :, :]), func=AF.Copy), [c_init] + x_ready, signal=True)
            a_last = None
            for c in range(NCH):
                iA, bA, fA = BK.get()
                iB, bB, fB = BK.get()
                pa = proj_group(wnext(U_U + 2 * c), ntot, hch, 8, h_waits if c == 0 else h_ready, bA, fA)
                pb = proj_group(wnext(U_U + 2 * c + 1), ntot, hch, 8, h_ready, bB, fB)
                r = rot["sg"]
                rot["sg"] ^= 1
                a1 = P.op("act", lambda e, bB=bB, r=r: e.activation(
                    out=SG[:, r, 0:ntot], in_=bB[:, 0:ntot], func=AF.Sigmoid), [pb] + t_free["SG"][r], signal=True)
                BK.release(iB, [a1])
                d1 = P.op("dve", lambda e, bA=bA, r=r, c=c: e.tensor_tensor(
                    out=Av(c, 30, 30 + n), in0=bA[:, 0:n], in1=SG[:, r, 0:n], op=ALU.mult), [pa, a1], signal=True)
                if ns:
                    d1 = P.op("dve", lambda e, bA=bA, r=r, c=c: e.tensor_tensor(
                        out=AS[:, l, c, 30:30 + ns], in0=bA[:, n:ntot], in1=SG[:, r, n:ntot], op=ALU.mult),
                        [pa, a1, LDP], signal=True)
                BK.release(iA, [d1])
                t_free["SG"][r] = [d1]
                a_last = d1
            a_ready = [a_last, car_tok, LDP]

            def qk_gen(js):
                def finish(cx):
                    j, iA, bA, pa, rq, ac = cx
                    isq = j < 8
                    iB, bB, fB = BK.get()
                    pb = P.op("pe", lambda e: e.matmul(
                        bB[:, 0:ntot], PERM[:, :], QS[:, rq, 0:ntot], start=True, stop=True),
                        [ac, LDP] + fB, signal=True)
                    qs_free[rq] = [pb]
                    r = rot["t"]
                    rot["t"] ^= 1
                    d1 = P.op("dve", lambda e: e.tensor_tensor(
                        out=T1[:, r, 0:ntot], in0=bA[:, 0:ntot], in1=RC[:, 0:ntot], op=ALU.mult),
                        [pa, ac] + rope_ready + t_free["T1"][r], signal=True)
                    d2 = P.op("dve", lambda e: e.tensor_tensor(
                        out=T2[:, r, 0:ntot], in0=bB[:, 0:ntot], in1=RSN[:, 0:ntot], op=ALU.mult),
                        [pb] + rope_ready + t_free["T2"][r], signal=True)
                    BK.release(iA, [d1, ac])
                    BK.release(iB, [d2])
                    if isq:
                        d3 = P.op("dve", lambda e: e.tensor_tensor(
                            out=Qv(j, 0, ntot), in0=T1[:, r, 0:ntot], in1=T2[:, r, 0:ntot], op=ALU.add),
                            [d1, d2, state["y_store"]], signal=True)
                    else:
                        cc = j - 8
                        d3 = P.op("dve", lambda e: e.tensor_tensor(
                            out=KB[:, l, cc, 128:128 + n], in0=T1[:, r, 0:n], in1=T2[:, r, 0:n], op=ALU.add),
                            [d1, d2, state["kb_ready"][l]], signal=True)
                        if ns:
                            d3 = P.op("dve", lambda e: e.tensor_tensor(
                                out=KS[:, l, cc, 0:ns], in0=T1[:, r, n:ntot], in1=T2[:, r, n:ntot], op=ALU.add),
                                [d1, d2], signal=True)
                    t_free["T1"][r] = [d3]
                    t_free["T2"][r] = [d3]
                    state["rope_last"] = d3

                prev = None
                for j in js:
                    lA = (U_Q + 2 * j if j < 8 else U_K + 2 * (j - 8))
                    iA, bA, fA = BK.get()
                    pa = proj_group(wnext(lA), ntot, hch, 8, h_ready, bA, fA)
                    rq = rot["qs"]
                    rot["qs"] ^= 1
                    ac = P.op("act", lambda e: e.activation(
                        out=QS[:, rq, 0:ntot], in_=bA[:, 0:ntot], func=AF.Copy), [pa] + qs_free[rq], signal=True)
                    cur = (j, iA, bA, pa, rq, ac)
                    if prev is not None:
                        finish(prev)
                    prev = cur
                    yield
                finish(prev)

            def v_gen():
                sV = wnext(U_V)
                assert sV % 2 == 0
                assert wnext(U_V + 1) == sV + 1
                slotV, wtokV = wunit(sV)
                _, wtokV2 = wunit(sV + 1)
                v_last = None
                pe_last = None
                nblk = n // 128
                for b in range(nblk + (1 if ns else 0)):
                    iV, bV, fV = BK.get()
                    is_s = (b == nblk)
                    m0, m1 = (n, ntot) if is_s else (128 * b, 128 * b + 128)
                    mm = m1 - m0
                    for kc in range(8):
                        pe_last = P.op("pe", lambda e, kc=kc, bV=bV, m0=m0, m1=m1, mm=mm: e.matmul(
                            bV[0:mm, 0:256], H[:, kc, m0:m1], WR[:, slotV:slotV + 2, kc * 128:(kc + 1) * 128],
                            start=(kc == 0), stop=(kc == 7)),
                            ([wtokV, wtokV2] + h_ready + fV) if kc == 0 else [], signal=(kc == 7))
                    if is_s:
                        state["vsf_tok"] = P.op("act", lambda e, bV=bV, mm=mm: e.activation(
                            out=VSF[0:mm, l, :], in_=bV[0:mm, 0:256], func=AF.Copy), [pe_last], signal=True)
                        v_last = P.op("act", lambda e, bV=bV, mm=mm: e.activation(
                            out=VS[0:mm, l, :], in_=bV[0:mm, 0:256], func=AF.Copy), [pe_last, c_init], signal=True)
                    else:
                        if p == npass - 1 and b == nblk - 1:
                            state["vf_tok"] = P.op("act", lambda e, bV=bV: e.activation(
                                out=VF[:, l, :], in_=bV[:, 0:256], func=AF.Copy), [pe_last], signal=True)
                        v_last = P.op("act", lambda e, bV=bV, b=b: e.activation(
                            out=VB[:, l, 1 + b, :], in_=bV[:, 0:256], func=AF.Copy),
                            [pe_last, state["kb_ready"][l]], signal=True)
                    BK.release(iV, [v_last])
                    yield
                wdone(sV, pe_last)
                wdone(sV + 1, pe_last)
                state["v_last"] = v_last

            def attention():
                rot_pt = [0, 0]

                def sample_group(k):
                    hh = k % 2
                    pr = k // 2
                    R0, R1 = 64 * hh, 64 * hh + 64
                    iS, bS, fS = BK.get()
                    r = rot_pt[0]
                    rot_pt[0] ^= 1
                    nq = ns
                    ncol = 4 * nq
                    qr = ARENA[R0:R1, 0:4096].rearrange("p (c t) -> p c t", t=TILE)[:, 4 * pr:4 * pr + 4, n:ntot]
                    P.op("pe", lambda e: e.matmul(
                        bS[:, 0:ncol], KC[R0:R1, l, pr, :], qr, start=True, stop=True), qk_ready + fS + [LDP])
                    ps = P.op("pe", lambda e: e.matmul(
                        bS[0:ns, 256:256 + ncol], KS[R0:R1, l, pr, 0:ns], qr, start=True, stop=True), [], signal=True)
                    P.op("act", lambda e: e.activation(
                        out=PT[:, r, 0:ncol], in_=bS[:, 0:ncol], func=AF.Exp, bias=ZER[:, 0:1], scale=0.125),
                        [ps] + t_free["PT"][r])
                    ap_ = P.op("act", lambda e: e.activation(
                        out=PT[0:ns, r, 256:256 + ncol], in_=bS[0:ns, 256:256 + ncol], func=AF.Exp,
                        bias=ZER[0:ns, 0:1], scale=0.125), [], signal=True)
                    BK.release(iS, [ap_])
                    iO, bO, fO = BK.get()
                    P.op("pe", lambda e: e.matmul(
                        bO[:, 0:ncol], VC[:, l, pr * 128:(pr + 1) * 128], PT[:, r, 0:ncol],
                        start=True, stop=False), [ap_] + fO + v_ready)
                    P.op("pe", lambda e: e.matmul(
                        bO[:, 0:ncol], VS[0:ns, l, pr * 128:(pr + 1) * 128], PT[0:ns, r, 256:256 + ncol],
                        start=False, stop=True))
                    P.op("pe", lambda e: e.matmul(
                        bO[:, 256:256 + ncol], ONESB[:, :], PT[:, r, 0:ncol], start=True, stop=False))
                    P.op("pe", lambda e: e.matmul(
                        bO[:, 256:256 + ncol], ONESB[0:ns, :], PT[0:ns, r, 256:256 + ncol],
                        start=False, stop=False))
                    esr_ap = ESR[:, l, k, :].rearrange("p (g q) -> p g q", q=64)[:, :, 0:nq]
                    po = P.op("pe", lambda e: e.matmul(
                        bO[:, 256:256 + ncol], ONEHOT0[:, :], esr_ap, start=False, stop=True), [c_esr], signal=True)
                    t_free["PT"][r] = [po]
                    rd = rot["dt"]
                    rot["dt"] ^= 1
                    d1 = P.op("act", lambda e: e.activation(
                        out=RD[R0:R1, rd, 0:ncol], in_=bO[R0:R1, 256:256 + ncol], func=AF.Ln),
                        [po] + t_free["DT"][rd], signal=True)
                    d2 = P.op("act", lambda e: e.activation(
                        out=RD[R0:R1, rd, 0:ncol], in_=RD[R0:R1, rd, 0:ncol], func=AF.Exp, scale=-1.0),
                        [d1], signal=True)
                    ao = ARENA[R0:R1, 4096:8192].rearrange("p (c t) -> p c t", t=TILE)[:, 4 * pr:4 * pr + 4, n:ntot]
                    d3 = P.op("dve", lambda e: e.tensor_tensor(
                        out=ao, in0=bO[R0:R1, 0:ncol].rearrange("p (g q) -> p g q", q=nq),
                        in1=RD[R0:R1, rd, 0:ncol].rearrange("p (g q) -> p g q", q=nq), op=ALU.mult),
                        [d2], signal=True)
                    t_free["DT"][rd] = [d3]
                    BK.release(iO, [d3])
                    state["ao_last"] = d3

                def stage_a(c, k):
                    hh = k % 2
                    pr = k // 2
                    R0, R1 = 64 * hh, 64 * hh + 64
                    par = c % 2
                    r = 2 * par + rot_pt[par]
                    rot_pt[par] ^= 1
                    iS, bS, fS = BK.get()
                    q0 = 64 * c
                    qr = ARENA[R0:R1, 0:4096].rearrange("p (c t) -> p c t", t=TILE)[:, 4 * pr:4 * pr + 4, q0:q0 + 64]
                    if par == 0:
                        fcol = 64 * c
                        fblk = c // 2
                        hlo = 64 * c + 128
                        hblk = c // 2 + 1
                        HR0, HR1 = 0, 64
                        hM = 64
                        full_is_halo = first_own and c == 0
                        half_is_halo = False
                    else:
                        fcol = 64 * (c + 1)
                        fblk = (c + 1) // 2
                        hlo = 64 * c - 64
                        hblk = (c - 1) // 2
                        HR0, HR1 = 64, 128
                        hM = 128
                        full_is_halo = False
                        half_is_halo = first_own and c == 1
                    P.op("pe", lambda e: e.matmul(
                        bS[:, 0:256], KB[R0:R1, l, pr, fcol:fcol + 128], qr, start=True, stop=True), qk_ready + fS)
                    ps = P.op("pe", lambda e: e.matmul(
                        bS[0:hM, 256:512], KB[R0:R1, l, pr, hlo:hlo + hM], qr, start=True, stop=True), [], signal=True)
                    bf = FLAGS[:, 0:1] if full_is_halo else 0.0
                    bh = FLAGS[HR0:HR1, 0:1] if half_is_halo else 0.0
                    P.op("act", lambda e: e.activation(
                        out=PT[:, r, 0:256], in_=bS[:, 0:256], func=AF.Exp, bias=bf, scale=0.125),
                        [ps, LD] + t_free["PT"][r])
                    ap_ = P.op("act", lambda e: e.activation(
                        out=PT[HR0:HR1, r, 256:512], in_=bS[HR0:HR1, 256:512], func=AF.Exp, bias=bh, scale=0.125),
                        [], signal=True)
                    BK.release(iS, [ap_])
                    return dict(ap=ap_, r=r, fblk=fblk, hblk=hblk, R0=R0, R1=R1, pr=pr, k=k, q0=q0)

                def stage_b(cx):
                    r, fblk, hblk, R0, R1, pr, k, q0 = (cx[x] for x in ("r", "fblk", "hblk", "R0", "R1", "pr", "k", "q0"))
                    iO, bO, fO = BK.get()
                    P.op("pe", lambda e: e.matmul(
                        bO[:, 0:256], VB[:, l, fblk, pr * 128:(pr + 1) * 128], PT[:, r, 0:256],
                        start=True, stop=False), [cx["ap"]] + fO + v_ready)
                    P.op("pe", lambda e: e.matmul(
                        bO[:, 0:256], VB[:, l, hblk, pr * 128:(pr + 1) * 128], PT[:, r, 256:512],
                        start=False, stop=True))
                    P.op("pe", lambda e: e.matmul(
                        bO[:, 256:512], ONESB[:, :], PT[:, r, 0:256], start=True, stop=False))
                    P.op("pe", lambda e: e.matmul(
                        bO[:, 256:512], ONESB[:, :], PT[:, r, 256:512], start=False, stop=False))
                    po = P.op("pe", lambda e: e.matmul(
                        bO[:, 256:512], ONEHOT0[:, :], ESR[:, l, k, :], start=False, stop=True), [c_esr], signal=True)
                    t_free["PT"][r] = [po]
                    rd = rot["dt"]
                    rot["dt"] ^= 1
                    d1 = P.op("act", lambda e: e.activation(
                        out=RD[R0:R1, rd, :], in_=bO[R0:R1, 256:512], func=AF.Ln),
                        [po] + t_free["DT"][rd], signal=True)
                    d2 = P.op("act", lambda e: e.activation(
                        out=RD[R0:R1, rd, :], in_=RD[R0:R1, rd, :], func=AF.Exp, scale=-1.0), [d1], signal=True)
                    ao = ARENA[R0:R1, 4096:8192].rearrange("p (c t) -> p c t", t=TILE)[:, 4 * pr:4 * pr + 4, q0:q0 + 64]
                    d3 = P.op("dve", lambda e: e.tensor_tensor(
                        out=ao, in0=bO[R0:R1, 0:256].rearrange("p (g q) -> p g q", q=64),
                        in1=RD[R0:R1, rd, :].rearrange("p (g q) -> p g q", q=64), op=ALU.mult), [d2], signal=True)
                    t_free["DT"][rd] = [d3]
                    BK.release(iO, [d3])
                    state["ao_last"] = d3

                prev = None
                for c in range(nchunks):
                    for k in range(NKV):
                        cx = stage_a(c, k)
                        if prev is not None:
                            stage_b(prev)
                        prev = cx
                        yield
                stage_b(prev)
                if ns:
                    for k in range(NKV):
                        sample_group(k)
                        yield

            acc_info = {}

            def conv_chain(c):
                r = rot["sg"]
                rot["sg"] ^= 1
                acc = None
                for j in range(TD):
                    w_ap = WDW[:, l, c, j:j + 1]
                    if j == 0:
                        acc = P.op("dve", lambda e: e.tensor_scalar(
                            out=SG[:, r, 0:n], in0=f32(Av(c, 0, n)), scalar1=w_ap, scalar2=vec(V_BDW, c),
                            op0=ALU.mult, op1=ALU.add), a_ready + [LD] + t_free["SG"][r], signal=True)
                        if ns:
                            acc = P.op("dve", lambda e: e.tensor_scalar(
                                out=SG[:, r, n:ntot], in0=f32(AS[:, l, c, 0:ns]), scalar1=w_ap,
                                scalar2=vec(V_BDW, c), op0=ALU.mult, op1=ALU.add), [], signal=True)
                    else:
                        accp = P.op("dve", lambda e: e.scalar_tensor_tensor(
                            out=SG[:, r, 0:n], in0=f32(Av(c, j, j + n)), scalar=w_ap, in1=SG[:, r, 0:n],
                            op0=ALU.mult, op1=ALU.add), [acc], signal=True)
                        if ns:
                            accp = P.op("dve", lambda e: e.scalar_tensor_tensor(
                                out=SG[:, r, n:ntot], in0=f32(AS[:, l, c, j:j + ns]), scalar=w_ap,
                                in1=SG[:, r, n:ntot], op0=ALU.mult, op1=ALU.add), [acc], signal=True)
                        acc = accp
                acc_info[c] = (r, acc)

            def conv():
                for c in range(NCH):
                    if c + 1 < NCH:
                        conv_chain(c + 1)
                    r, acc = acc_info[c]
                    iC, bC, fC = BK.get()
                    pe_t = None
                    for g in range(NCG):
                        s = wnext(U_CONV + 4 * c + g)
                        slot, wtok = wunit(s)
                        ntap = min(8, CONVW - TD - 8 * g)
                        for t in range(ntap):
                            j = TD + 8 * g + t
                            firstu = (t == 0)
                            pe_t = P.op("pe", lambda e, slot=slot, t=t, j=j: e.matmul(
                                bC[:, 0:n], WR[:, slot, t * 128:(t + 1) * 128], Av(c, j, j + n),
                                start=(j == TD), stop=(j == CONVW - 1)),
                                ([wtok] + (a_ready + fC if j == TD else [])) if firstu else [],
                                signal=(t == ntap - 1 and not ns))
                            if ns:
                                pe_t = P.op("pe", lambda e, slot=slot, t=t, j=j: e.matmul(
                                    bC[:, n:ntot], WR[:, slot, t * 128:(t + 1) * 128], AS[:, l, c, j:j + ns],
                                    start=False, stop=(j == CONVW - 1), skip_group_check=True),
                                    [], signal=(t == ntap - 1))
                        wdone(s, pe_t)
                        yield
                    d = P.op("dve", lambda e: e.tensor_tensor(
                        out=ZCv(c, 0, ntot), in0=bC[:, 0:ntot], in1=SG[:, r, 0:ntot], op=ALU.add),
                        [pe_t, acc], signal=True)
                    t_free["SG"][r] = [d]
                    BK.release(iC, [d])
                    state["zc_tok"][c] = d

            conv_chain(0)
            state["zc_tok"] = [None] * NCH
            for _ in conv():
                pass
            conv_pe_done = state["zc_tok"][NCH - 1]

            qg = qk_gen(range(8))
            next(qg)
            iM, bM, fM = BK.get()
            iQ, bQ, fQ = BK.get()
            lastm = lastq = None
            for c in range(NCH):
                r = rot["sq"]
                rot["sq"] ^= 1
                zt = state["zc_tok"][c]
                a_t = P.op("act", lambda e, c=c, r=r: e.activation(
                    out=SQ[:, r, 0:ntot], in_=f32(ZCv(c, 0, ntot)), func=AF.Square), [zt] + sq_free[r], signal=True)
                lastm = P.op("pe", lambda e, c=c: e.matmul(
                    bM[:, 0:ntot], ONES[:, :], ZCv(c, 0, ntot), start=(c == 0), stop=(c == NCH - 1)),
                    [zt] + (fM if c == 0 else []), signal=(c == NCH - 1))
                lastq = P.op("pe", lambda e, c=c, r=r: e.matmul(
                    bQ[:, 0:ntot], ONES[:, :], SQ[:, r, 0:ntot], start=(c == 0), stop=(c == NCH - 1)),
                    [a_t] + (fQ if c == 0 else []), signal=True)
                sq_free[r] = [lastq]
            dm = P.op("dve", lambda e: e.tensor_scalar(
                out=MU[:, 0:ntot], in0=bM[:, 0:ntot], scalar1=1.0 / D, scalar2=None, op0=ALU.mult),
                [lastm], signal=True)
            BK.release(iM, [dm])
            dm2 = P.op("dve", lambda e: e.tensor_tensor(
                out=M2[:, 0:ntot], in0=MU[:, 0:ntot], in1=MU[:, 0:ntot], op=ALU.mult), [dm], signal=True)
            dv = P.op("dve", lambda e: e.scalar_tensor_tensor(
                out=M2[:, 0:ntot], in0=bQ[:, 0:ntot], scalar=1.0 / D, in1=M2[:, 0:ntot],
                op0=ALU.mult, op1=ALU.subtract), [lastq, dm2], signal=True)
            BK.release(iQ, [dv])
            asd = P.op("act", lambda e: e.activation(
                out=M2[:, 0:ntot], in_=M2[:, 0:ntot], func=AF.Ln, bias=EPSB[:, 0:1], scale=1.0), [dv], signal=True)
            drs = P.op("act", lambda e: e.activation(out=RS[:, 0:ntot], in_=M2[:, 0:ntot], func=AF.Exp, scale=-0.5),
                       [asd], signal=True)
            ln_toks = []

            def ln_dve():
                for c in range(NCH):
                    r = rot["t"]
                    rot["t"] ^= 1
                    d1 = P.op("dve", lambda e, c=c, r=r: e.tensor_tensor(
                        out=T1[:, r, 0:ntot], in0=f32(ZCv(c, 0, ntot)), in1=MU[:, 0:ntot], op=ALU.subtract),
                        [dm, lastm, lastq] + t_free["T1"][r], signal=True)
                    d2 = P.op("dve", lambda e, c=c, r=r: e.tensor_tensor(
                        out=ZCv(c, 0, ntot), in0=T1[:, r, 0:ntot], in1=RS[:, 0:ntot], op=ALU.mult),
                        [d1, drs], signal=True)
                    t_free["T1"][r] = [d2]
                    ln_toks.append(d2)
                    yield

            def ln_apply():
                z_tok = None
                for c in range(NCH):
                    z_tok = P.op("act", lambda e, c=c: e.activation(
                        out=ZCv(c, 0, ntot), in_=f32(ZCv(c, 0, ntot)), func=AF.Silu, bias=vec(V_LNB, c),
                        scale=vec(V_LNG, c)), [ln_toks[c], LD], signal=True)
                    state["z_tok"] = z_tok
                yield

            alive = [qg, ln_dve()]
            while alive:
                for g_ in list(alive):
                    try:
                        next(g_)
                    except StopIteration:
                        alive.remove(g_)
            for _ in qk_gen(range(8, 10)):
                pass
            for _ in ln_apply():
                pass
            z_ready = [state["z_tok"]]
            for _ in v_gen():
                pass
            qk_ready = [state["rope_last"]]
            if l == 1:
                state["rope_free"] = [state["rope_last"]]
            v_ready = [state["v_last"]]
            for _ in attention():
                pass
            ao_ready = [state["ao_last"]]

            kb_tok = P.op("act", lambda e: e.activation(
                out=KB[:, l, :, 0:128], in_=f32(KB[:, l, :, n:n + 128]), func=AF.Copy), ao_ready, signal=True)
            nblk_ = n // 128
            vb_tok = P.op("act", lambda e: e.activation(
                out=VB[:, l, 0, :], in_=VB[:, l, nblk_, :], func=AF.Copy), ao_ready, signal=True)
            state["kb_ready"][l] = vb_tok
            a_all = ARENA[:, 12288:12288 + NCH * ACH].rearrange("p (c t) -> p c t", t=ACH)
            if p == 0:
                car2 = P.op("dve", lambda e: e.tensor_scalar(
                    out=ACAR[:, l, :, :], in0=f32(a_all[:, :, n:n + 30]), scalar1=FLAGS[:, 1:2], scalar2=None,
                    op0=ALU.mult), [conv_pe_done, LD], signal=True)
            else:
                car2 = P.op("dve", lambda e: e.tensor_copy(
                    out=ACAR[:, l, :, :], in_=f32(a_all[:, :, n:n + 30])), [conv_pe_done], signal=True)

            if p == npass - 1:
                state["out_tokens"].append(P.dma("pool", lambda e: e.dma_start(
                    out=kpT[l], in_=f32(KB[:, l, :, 0:128])), fo_sem, [kb_tok]))
                state["out_tokens"].append(P.dma("sp", lambda e: e.dma_start(
                    out=vp[l], in_=VF[:, l, :]), fo_sem, [state["vf_tok"]]))
                state["out_tokens"].append(P.dma("pool", lambda e: e.dma_start(
                    out=cpT[l], in_=f32(ACAR[:, l, :, :])), fo_sem, [car2]))
            if ns:
                state["out_tokens"].append(P.dma("pool", lambda e: e.dma_start(
                    out=ksT[l], in_=f32(KS[:, l, :, :])), fo_sem, qk_ready))
                state["out_tokens"].append(P.dma("sp", lambda e: e.dma_start(
                    out=vs[l], in_=VSF[0:ns, l, :]), fo_sem, [state["vsf_tok"]]))
                state["out_tokens"].append(P.dma("pool", lambda e: e.dma_start(
                    out=csT[l], in_=f32(AS[:, l, :, ns:ns + 30])), fo_sem, a_ready))

            mg_tok = None
            mg_toks = []
            for j in range(NCH):
                lM = U_MERGE + 4 * j
                iA, bA, fA = BK.get()
                iC, bC, fC = BK.get()
                iG, bG, fG = BK.get()
                iH, bH, fH = BK.get()
                pG = proj_group(wnext(lM + 2), ntot, hch, 8, h_ready, bG, fG)
                pH = proj_group(wnext(lM + 3), ntot, hch, 8, h_ready, bH, fH)
                pA = proj_group(wnext(lM), ntot, lambda kc: AOv(kc, 0, ntot), 8, ao_ready, bA, fA)
                pC = proj_group(wnext(lM + 1), ntot, lambda kc: ZCv(kc, 0, ntot), 8, z_ready, bC, fC)
                r = rot["sg"]
                rot["sg"] ^= 1
                r2 = rot["t"]
                rot["t"] ^= 1
                a1 = P.op("act", lambda e, bG=bG, r=r: e.activation(
                    out=SG[:, r, 0:ntot], in_=bG[:, 0:ntot], func=AF.Sigmoid), [pG] + t_free["SG"][r], signal=True)
                a2 = P.op("act", lambda e, bH=bH, r2=r2: e.activation(
                    out=T2[:, r2, 0:ntot], in_=bH[:, 0:ntot], func=AF.Sigmoid), [pH] + t_free["T2"][r2], signal=True)
                BK.release(iG, [a1])
                BK.release(iH, [a2])
                d1 = P.op("dve", lambda e, bA=bA, r=r: e.tensor_tensor(
                    out=SG[:, r, 0:ntot], in0=bA[:, 0:ntot], in1=SG[:, r, 0:ntot], op=ALU.mult), [pA, a1], signal=True)
                d2 = P.op("dve", lambda e, bC=bC, r2=r2: e.tensor_tensor(
                    out=T2[:, r2, 0:ntot], in0=bC[:, 0:ntot], in1=T2[:, r2, 0:ntot], op=ALU.mult), [pC, a2], signal=True)
                BK.release(iA, [d1])
                BK.release(iC, [d2])
                mg_tok = P.op("dve", lambda e, j=j, r=r, r2=r2: e.tensor_tensor(
                    out=Qv(j, 0, ntot), in0=SG[:, r, 0:ntot], in1=T2[:, r2, 0:ntot], op=ALU.add),
                    [d1, d2] + ao_ready, signal=True)
                t_free["SG"][r] = [mg_tok]
                t_free["T2"][r2] = [mg_tok]
                mg_toks.append(mg_tok)
            mg_ready = [mg_tok]

            st2 = Stats(ntot)
            xt = [None] * NCH
            for j in range(NCH):
                s = wnext(U_OUT + j)
                iA, bA, fA = BK.get()
                pA = proj_group(s, ntot, lambda kc: Qv(kc, 0, ntot), 8, (lambda kc: [mg_toks[kc]]) if j == 0 else mg_ready, bA, fA)
                xt[j] = P.op("dve", lambda e, bA=bA, j=j: e.tensor_tensor(
                    out=X[:, j, 0:ntot], in0=bA[:, 0:ntot], in1=X[:, j, 0:ntot], op=ALU.add), [pA, h_tok], signal=True)
                BK.release(iA, [xt[j]])
                if j >= 1:
                    st2.add(X[:, j - 1, 0:ntot], [xt[j - 1]])
            st2.add(X[:, NCH - 1, 0:ntot], [xt[NCH - 1]])
            x2_ready = [xt[NCH - 1]]
            d_rs = st2.finish()

            h2_tok = None
            h2_toks = []
            for c in range(NCH):
                h2_tok = P.op("dve", lambda e, c=c: e.scalar_tensor_tensor(
                    out=H[:, c, 0:ntot], in0=X[:, c, 0:ntot], scalar=vec(V_NMLP, c), in1=RS[:, 0:ntot],
                    op0=ALU.mult, op1=ALU.mult), [d_rs] + x2_ready, signal=True)
                h2_toks.append(h2_tok)
            h2_ready = [h2_tok]

            hid_tok = None
            hid_toks = []
            for i in range(32):
                s = wnext(U_UP + i)
                iA, bA, fA = BK.get()
                pA = proj_group(s, ntot, hch, 8, (lambda kc: [h2_toks[kc]]) if i == 0 else h2_ready, bA, fA)
                r = rot["sg"]
                rot["sg"] ^= 1
                a1 = P.op("act", lambda e, bA=bA, r=r: e.activation(
                    out=SG[:, r, 0:ntot], in_=bA[:, 0:ntot], func=AF.Relu), [pA] + t_free["SG"][r], signal=True)
                BK.release(iA, [a1])
                hid_tok = P.op("dve", lambda e, i=i, r=r: e.tensor_tensor(
                    out=HIDv(i, 0, ntot), in0=SG[:, r, 0:ntot], in1=SG[:, r, 0:ntot], op=ALU.mult),
                    [a1, car2, kb_tok, vb_tok], signal=True)
                t_free["SG"][r] = [hid_tok]
                hid_toks.append(hid_tok)
            hid_ready = [hid_tok]

            if l == 1 and p + 1 < npass:
                Xn = XX[(p + 1) % 2]
                xw = state["x_ready"][p + 1]
                d_n = rms_stats(lambda c: Xn[:, c, 0:TILE], TILE, xw)
                nh = []
                for c in range(NCH):
                    nh.append(P.op("dve", lambda e, c=c: e.scalar_tensor_tensor(
                        out=H[:, c, 0:TILE], in0=Xn[:, c, 0:TILE], scalar=vec(0, c), in1=RS[:, 0:TILE],
                        op0=ALU.mult, op1=ALU.mult), [d_n, LD, hid_tok] + xw, signal=True))
                state["pre_h"] = nh

            st3 = Stats(ntot, MUF, RSF) if l == 1 else Stats(ntot)
            xt = [None] * NCH
            for j in range(NCH):
                iA, bA, fA = BK.get()
                pA = None
                for g in range(4):
                    s = wnext(U_DOWN + 4 * j + g)
                    pA = proj_group(s, ntot, lambda kc, g=g: HIDv(8 * g + kc, 0, ntot), 8,
                                    (lambda kc, g=g: [hid_toks[8 * g + kc]]) if j == 0 else hid_ready, bA,
                                    fA if g == 0 else [], start=(g == 0), stop=(g == 3))
                xt[j] = P.op("dve", lambda e, bA=bA, j=j: e.tensor_tensor(
                    out=X[:, j, 0:ntot], in0=bA[:, 0:ntot], in1=X[:, j, 0:ntot], op=ALU.add), [pA, h2_tok], signal=True)
                BK.release(iA, [xt[j]])
                if j >= 1:
                    st3.add(X[:, j - 1, 0:ntot], [xt[j - 1]])
            st3.add(X[:, NCH - 1, 0:ntot], [xt[NCH - 1]])
            return [xt[NCH - 1]], st3.finish()

        def pass_geom(p):
            if p == 0:
                return HALO, NSMP, 0, 0
            return TILE, 0, HALO + TILE * (p - 1), HALO + NSMP + TILE * (p - 1)

        x_free = [[], []]
        state["x_ready"] = {}

        def load_x(p):
            n, ns, tok0, _ = pass_geom(p)
            Xp = XX[p % 2]
            xsrc = xT.rearrange("(c q) t -> q c t", q=128)[:, :, tok0:tok0 + n]
            xt_ = P.dma("sp", lambda e: e.dma_start(out=Xp[:, :, 0:n], in_=xsrc), x_sem, x_free[p % 2])
            if ns:
                xs_src = xsT.rearrange("(c q) t -> q c t", q=128)
                xt_ = P.dma("sp", lambda e: e.dma_start(out=Xp[:, :, n:n + ns], in_=xs_src), x_sem, x_free[p % 2])
            state["x_ready"][p] = [xt_]

        load_x(0)
        state["pre_h"] = None
        for p in range(npass):
            n, ns, tok0, rcol0 = pass_geom(p)
            ntot = n + ns
            X = XX[p % 2]
            if p + 1 < npass:
                load_x(p + 1)
            x_ready = state["x_ready"][p]
            rt1 = P.dma("sp", lambda e, rcol0=rcol0, ntot=ntot: e.dma_start(
                out=RC[:, 0:ntot], in_=ropec_d[:, rcol0:rcol0 + ntot]), r_sem, state["rope_free"])
            rt2 = P.dma("sp", lambda e, rcol0=rcol0, ntot=ntot: e.dma_start(
                out=RSN[:, 0:ntot], in_=ropes_d[:, rcol0:rcol0 + ntot]), r_sem, state["rope_free"])
            rope_ready = [rt2]

            xr = x_ready
            d_rs = None
            pre_h = state["pre_h"]
            state["pre_h"] = None
            for l in range(2):
                xr, d_rs = layer(p, l, n, ns, xr, rope_ready, d_rs, pre_h if l == 0 else None)

            y_tok = None
            for c in range(NCH):
                y_tok = P.op("dve", lambda e, c=c, ntot=ntot, X=X: e.scalar_tensor_tensor(
                    out=X[:, c, 0:ntot], in0=X[:, c, 0:ntot], scalar=vec(80, c), in1=RSF[:, 0:ntot],
                    op0=ALU.mult, op1=ALU.mult), [d_rs] + xr, signal=True)
            if p == 0:
                st = P.dma("sp", lambda e, n=n, ntot=ntot, X=X: e.dma_start(
                    out=ysT.rearrange("(c q) t -> q c t", q=128), in_=X[:, :, n:ntot]), o_sem, [y_tok])
            else:
                o0 = TILE * (p - 1)
                st = P.dma("sp", lambda e, o0=o0, n=n, X=X: e.dma_start(
                    out=ypT.rearrange("(c q) t -> q c t", q=128)[:, :, o0:o0 + n], in_=X[:, :, 0:n]),
                    o_sem, [y_tok])
            x_free[p % 2] = [st]

        finals = [(o_sem[0], o_sem[1]), (fo_sem[0], fo_sem[1])]

        with nc.Block() as block:
            @block.tensor
            def _(e):
                P.run("pe", e)

            @block.scalar
            def _(e):
                P.run("act", e)

            @block.vector
            def _(e):
                P.run("dve", e)

            @block.gpsimd
            def _(e):
                P.run("pool", e)

            @block.sync
            def _(e):
                P.run("sp", e, final_waits=finals)
    return nc, order


def _units_for_layer(w_in, w_o, w_dw, w_pw2, w_out, w_up, w_down):
    U = np.zeros((NU, 128, 1024), np.float32)

    def put(u, mat):
        U[u] = mat.reshape(8, 128, 128).transpose(1, 0, 2).reshape(128, 1024)

    m = np.arange(128)
    for cc in range(8):
        heads = np.array([q_head(cc, 0)] * 64 + [q_head(cc, 1)] * 64)
        dd = m % 64
        put(U_Q + 2 * cc, w_in[:, heads * 64 + dd])
        put(U_Q + 2 * cc + 1, w_in[:, heads * 64 + (dd + 32) % 64])
    for cc in range(2):
        base = 1024 + cc * 128
        put(U_K + 2 * cc, w_in[:, base + m])
        put(U_K + 2 * cc + 1, w_in[:, base + (m // 64) * 64 + (m % 64 + 32) % 64])
    for cc in range(2):
        put(U_V + cc, w_in[:, 1280 + cc * 128 + m])
    for c in range(8):
        put(U_U + 2 * c, w_in[:, 1536 + c * 128 + m])
        put(U_U + 2 * c + 1, w_in[:, 2560 + c * 128 + m])
    for c in range(8):
        for g in range(4):
            blk = np.zeros((128, 8, 128), np.float32)
            for t in range(8):
                j = TD + 8 * g + t
                if j < CONVW:
                    blk[m, t, m] = w_dw[j, c * 128 + m]
            U[U_CONV + 4 * c + g] = blk.reshape(128, 1024)
    rows = np.zeros(1024, np.int64)
    for cc in range(8):
        for half in range(2):
            rows[cc * 128 + half * 64: cc * 128 + half * 64 + 64] = q_head(cc, half) * 64 + np.arange(64)
    w_o_p = w_o[rows, :]
    for j in range(8):
        put(U_MERGE + 4 * j, w_o_p[:, j * 128 + m])
        put(U_MERGE + 4 * j + 1, w_pw2[:, j * 128 + m])
        put(U_MERGE + 4 * j + 2, w_in[:, 3584 + j * 128 + m])
        put(U_MERGE + 4 * j + 3, w_in[:, 4608 + j * 128 + m])
        put(U_OUT + j, w_out[:, j * 128 + m])
    for i in range(32):
        put(U_UP + i, w_up[:, i * 128 + m])
    for j in range(8):
        for g in range(4):
            put(U_DOWN + 4 * j + g, w_down[g * 1024:(g + 1) * 1024, j * 128 + m])
    return U


def _vec_cols(v):
    return np.ascontiguousarray(v.reshape(8, 128).T)


def _rope_tables(pos):
    half = HD // 2
    inv_freq = (np.float32(10000.0) ** (-np.arange(half, dtype=np.float32) / np.float32(half))).astype(np.float32)
    ang = pos.astype(np.float32)[None, :] * inv_freq[:, None]
    cos = np.cos(ang).astype(np.float32)
    sin = np.sin(ang).astype(np.float32)
    c64 = np.concatenate([cos, cos], 0)
    s64 = np.concatenate([-sin, sin], 0)
    return np.concatenate([c64, c64], 0), np.concatenate([s64, s64], 0)


_CACHE = {}


def kernel(x_prompt, x_sample, cache_k, cache_v, state_conv, norm_mix, w_in, sinks, w_o_attn,
           w_dw, b_dw, ln_conv_g, ln_conv_b, w_pw2, w_out, norm_mlp, w_up, w_down, norm_final):
    f = lambda a: np.asarray(a, dtype=np.float32)
    x_prompt, x_sample, cache_k, cache_v, state_conv = map(f, (x_prompt, x_sample, cache_k, cache_v, state_conv))
    norm_mix, w_in, sinks, w_o_attn, w_dw, b_dw = map(f, (norm_mix, w_in, sinks, w_o_attn, w_dw, b_dw))
    ln_conv_g, ln_conv_b, w_pw2, w_out, norm_mlp, w_up, w_down, norm_final = map(
        f, (ln_conv_g, ln_conv_b, w_pw2, w_out, norm_mlp, w_up, w_down, norm_final))
    B, SEQ, _ = x_prompt.shape
    own = SEQ // 2
    n_own = own // TILE
    ncores = 8
    assert B * 2 == ncores and x_sample.shape[0] == ncores
    if n_own not in _CACHE:
        _CACHE[n_own] = build_program(n_own)
    nc, order = _CACHE[n_own]

    assert len(order) == NUS and len(set(order)) == NUS
    wst = np.stack([_units_for_layer(w_in[l], w_o_attn[l], w_dw[l], w_pw2[l], w_out[l], w_up[l], w_down[l])[order]
                    for l in range(2)], 0)
    vecs = np.zeros((128, NVEC), np.float32)
    for l in range(2):
        for i, v in enumerate((norm_mix[l], b_dw[l], ln_conv_g[l], ln_conv_b[l], norm_mlp[l])):
            vecs[:, 40 * l + 8 * i: 40 * l + 8 * i + 8] = _vec_cols(v)
    vecs[:, 80:88] = _vec_cols(norm_final)
    wdw = np.ascontiguousarray(w_dw.reshape(2, CONVW, NCH, 128).transpose(3, 0, 2, 1)[:, :, :, :TD])
    sinks_b = np.ascontiguousarray(np.broadcast_to(sinks.reshape(1, 32), (128, 32))).astype(np.float32)

    mm_ = np.arange(128)
    perm = np.zeros((128, 128), np.float32)
    perm[(mm_ // 64) * 64 + (mm_ % 64 + 32) % 64, mm_] = 1.0
    in_maps = []
    for core in range(ncores):
        b, hf = core // 2, core % 2
        start = hf * own
        xT = np.zeros((D, HALO + own), np.float32)
        if hf == 1:
            xT[:, :] = x_prompt[b, start - HALO:start + own, :].T
        else:
            xT[:, HALO:] = x_prompt[b, 0:own, :].T
        pos = np.concatenate([np.arange(start - HALO, start), PAST + np.arange(NSMP), np.arange(start, start + own)])
        rc, rs = _rope_tables(pos.astype(np.float32))
        flags = np.zeros((128, 4), np.float32)
        flags[0, 2] = 1.0
        flags[:, 0] = 0.0 if hf == 1 else -30000.0
        flags[:, 1] = 1.0 if hf == 1 else 0.0
        in_maps.append({
            "xT": xT,
            "xsT": np.ascontiguousarray(x_sample[core].T),
            "wst": wst,
            "vecs": vecs,
            "wdw": wdw,
            "ropec": rc, "ropes": rs,
            "kcT": np.ascontiguousarray(cache_k[:, core].reshape(2, 128, 256).transpose(0, 2, 1)
                                        .reshape(2, 2, 128, 128).transpose(0, 2, 1, 3)),
            "vc": np.ascontiguousarray(cache_v[:, core].reshape(2, 128, 256)),
            "scT": np.ascontiguousarray(state_conv[:, core].transpose(0, 2, 1).reshape(2, 8, 128, 30)
                                        .transpose(0, 2, 1, 3)),
            "sinks": sinks_b,
            "perm": perm,
            "flags": flags,
        })
    res = run_bass_kernel_spmd(nc, in_maps, core_ids=list(range(ncores)))
    R = res.results

    y_prompt = np.zeros((B, SEQ, D), np.float32)
    y_sample = np.zeros((ncores, NSMP, D), np.float32)
    k_p = np.zeros((2, B, 128, NKV, HD), np.float32)
    v_p = np.zeros((2, B, 128, NKV, HD), np.float32)
    c_p = np.zeros((2, B, 30, D), np.float32)
    k_s = np.zeros((2, ncores, NSMP, NKV, HD), np.float32)
    v_s = np.zeros((2, ncores, NSMP, NKV, HD), np.float32)
    c_s = np.zeros((2, ncores, 30, D), np.float32)
    for core in range(ncores):
        b, hf = core // 2, core % 2
        r = R[core]
        y_prompt[b, hf * own:(hf + 1) * own, :] = r["ypT"].T
        y_sample[core] = r["ysT"].T
        k_s[:, core] = r["ksT"].transpose(0, 3, 2, 1).reshape(2, NSMP, NKV, HD)
        v_s[:, core] = r["vs"].reshape(2, NSMP, NKV, HD)
        c_s[:, core] = r["csT"].transpose(0, 3, 2, 1).reshape(2, 30, D)
        if hf == 1:
            k_p[:, b] = r["kpT"].transpose(0, 3, 2, 1).reshape(2, 128, NKV, HD)
            v_p[:, b] = r["vp"].reshape(2, 128, NKV, HD)
            c_p[:, b] = r["cpT"].transpose(0, 3, 2, 1).reshape(2, 30, D)
    return (y_prompt, y_sample, k_p, v_p, c_p, k_s, v_s, c_s)
```
